# Optimizing a Trainium2 kernel written in Bass

```python
import math
import jax, jax.numpy as jnp
from jax import lax
import numpy as np

D_MODEL = 1024
BATCH = 2
SEQ = 8192
DEPTH = 4

N_EVEN = (DEPTH + 1) // 2
N_ODD = DEPTH // 2

A_CHUNK = 128
A_GROUPS = 4
A_WIDTH = D_MODEL // 2
A_GROUP_DIM = A_WIDTH // A_GROUPS

B_HEADS = 8
B_NOPE = 64
B_ROPE = 32
B_VDIM = 64
B_Q_RANK = 384
B_KV_RANK = 256
B_BLOCK = 128
B_WIDTH = B_HEADS * B_VDIM
ROPE_THETA = 10000.0

EVEN_IN = 2 * A_WIDTH + B_Q_RANK + B_KV_RANK + B_ROPE

C_HEADS = 4
C_DK = D_MODEL // 2 // C_HEADS
C_DV = D_MODEL // C_HEADS
C_GATE_RANK = 16
C_GATE_TAU = 16.0
C_CHUNK = 64
ODD_IN = 2 * C_HEADS * C_DK + 2 * C_HEADS * C_DV + C_GATE_RANK

D_FF = 4 * D_MODEL

ALPHA = (2.0 * DEPTH) ** 0.25
BETA = (8.0 * DEPTH) ** -0.25
LN_EPS = 1e-5

kernel_name = "hybrid_gmlp_mla_gla_deepnorm"


def layer_norm(x, g, b):
    xf = x.astype(jnp.float32)
    mu = xf.mean(-1, keepdims=True)
    var = jnp.square(xf - mu).mean(-1, keepdims=True)
    return ((xf - mu) * lax.rsqrt(var + LN_EPS) * g + b).astype(x.dtype)


def rms_norm(x, g):
    xf = x.astype(jnp.float32)
    ms = jnp.square(xf).mean(-1, keepdims=True)
    return (xf * lax.rsqrt(ms + LN_EPS) * g).astype(x.dtype)


def rope(x, positions):
    half = x.shape[-1] // 2
    inv = ROPE_THETA ** (-jnp.arange(half, dtype=jnp.float32) / half)
    ang = positions.astype(jnp.float32)[..., None] * inv
    ang = ang.reshape(ang.shape[:2] + (1,) * (x.ndim - 3) + (half,))
    cos, sin = jnp.cos(ang), jnp.sin(ang)
    xf = x.astype(jnp.float32)
    x1, x2 = xf[..., :half], xf[..., half:]
    return jnp.concatenate([x1 * cos - x2 * sin, x2 * cos + x1 * sin], -1).astype(x.dtype)


def chunk_gmlp(u, v, w_s, b_s, g_v, bias_v):
    bn, s, _ = u.shape
    nc = s // A_CHUNK
    v = v.reshape(bn, nc, A_CHUNK, A_GROUPS, A_GROUP_DIM)
    v = layer_norm(v, g_v, bias_v)
    causal = jnp.tril(jnp.ones((A_CHUNK, A_CHUNK), dtype=bool))
    w = jnp.where(causal[None], w_s, 0.0)
    mixed = jnp.einsum('gts,bcsgd->bctgd', w, v) + b_s.T[None, None, :, :, None]
    return (u.reshape(mixed.shape) * mixed).reshape(bn, s, A_WIDTH)


def mla(c_q, c_kv, k_r, positions, g_q, g_kv, w_uq, w_ukv):
    bn, s, _ = c_q.shape
    q = (rms_norm(c_q, g_q) @ w_uq).reshape(bn, s, B_HEADS, B_NOPE + B_ROPE)
    kv = (rms_norm(c_kv, g_kv) @ w_ukv).reshape(bn, s, B_HEADS, B_NOPE + B_VDIM)
    q = jnp.concatenate([q[..., :B_NOPE], rope(q[..., B_NOPE:], positions)], -1)
    k_r = rope(k_r, positions)
    k = jnp.concatenate([kv[..., :B_NOPE],
                         jnp.broadcast_to(k_r[:, :, None, :], (bn, s, B_HEADS, B_ROPE))], -1)
    v = kv[..., B_NOPE:]
    scale = (B_NOPE + B_ROPE) ** -0.5
    nq = s // B_BLOCK
    qb = q.reshape(bn, nq, B_BLOCK, B_HEADS, B_NOPE + B_ROPE).transpose(1, 0, 2, 3, 4)
    kpos = jnp.arange(s)

    def block(args):
        qi, i = args
        sc = jnp.einsum('bqhd,bkhd->bhqk', qi, k, preferred_element_type=jnp.float32) * scale
        qpos = i * B_BLOCK + jnp.arange(B_BLOCK)
        sc = jnp.where(kpos[None, :] <= qpos[:, None], sc, -jnp.inf)
        p = jax.nn.softmax(sc, axis=-1)
        return jnp.einsum('bhqk,bkhd->bqhd', p.astype(v.dtype), v)

    o = lax.map(block, (qb, jnp.arange(nq)))
    return o.transpose(1, 0, 2, 3, 4).reshape(bn, s, B_WIDTH)


def gla(q, k, v, log_a):
    bn, s = q.shape[:2]
    nc = s // C_CHUNK

    def to_chunks(t):
        return t.reshape(bn, nc, C_CHUNK, C_HEADS, -1).transpose(1, 0, 3, 2, 4).astype(jnp.float32)

    qc, kc, vc, gc = to_chunks(q * C_DK ** -0.5), to_chunks(k), to_chunks(v), to_chunks(log_a)
    causal = jnp.tril(jnp.ones((C_CHUNK, C_CHUNK), dtype=bool))

    def step(state, xs):
        qi, ki, vi, gi = xs
        b = jnp.cumsum(gi, axis=2)
        o_inter = jnp.einsum('bhld,bhde->bhle', qi * jnp.exp(b), state)
        diff = b[:, :, :, None, :] - b[:, :, None, :, :]
        decay = jnp.exp(jnp.where(causal[:, :, None], diff, -jnp.inf))
        attn = jnp.einsum('bhtsd,bhsd->bhts', qi[:, :, :, None, :] * decay, ki)
        o = o_inter + jnp.einsum('bhts,bhse->bhte', attn, vi)
        b_last = b[:, :, -1:, :]
        state = (jnp.exp(b_last[:, :, 0, :])[..., None] * state
                 + jnp.einsum('bhsd,bhse->bhde', ki * jnp.exp(b_last - b), vi))
        return state, o

    state0 = jnp.zeros((bn, C_HEADS, C_DK, C_DV), jnp.float32)
    _, o = lax.scan(step, state0, (qc, kc, vc, gc))
    return o.transpose(1, 0, 3, 2, 4).reshape(bn, s, C_HEADS, C_DV)


def sqrelu_mlp(x, w1, w2):
    return jnp.square(jax.nn.relu(x @ w1)) @ w2


def setup_inputs(seed: int = 0) -> dict:
    key = jax.random.key(seed)
    ks = jax.random.split(key, 26)

    def nrm(k, shape, scale):
        return jax.random.normal(k, shape, jnp.float32) * scale

    def gain(k, shape):
        return 1.0 + nrm(k, shape, 0.02)

    d = D_MODEL
    return {
        "x": nrm(ks[0], (BATCH, SEQ, d), 1.0),
        "positions": jnp.broadcast_to(jnp.arange(SEQ, dtype=jnp.int32), (BATCH, SEQ)),
        "ln1_g": gain(ks[1], (DEPTH, d)),
        "ln1_b": nrm(ks[2], (DEPTH, d), 0.02),
        "ln2_g": gain(ks[3], (DEPTH, d)),
        "ln2_b": nrm(ks[4], (DEPTH, d), 0.02),
        "w_in_even": nrm(ks[5], (N_EVEN, d, EVEN_IN), d ** -0.5),
        "a_w_s": nrm(ks[6], (N_EVEN, A_GROUPS, A_CHUNK, A_CHUNK), A_CHUNK ** -0.5),
        "a_b_s": 1.0 + nrm(ks[7], (N_EVEN, A_GROUPS, A_CHUNK), 0.1),
        "a_ln_g": gain(ks[8], (N_EVEN, A_GROUPS, A_GROUP_DIM)),
        "a_ln_b": nrm(ks[9], (N_EVEN, A_GROUPS, A_GROUP_DIM), 0.02),
        "b_q_norm": gain(ks[10], (N_EVEN, B_Q_RANK)),
        "b_kv_norm": gain(ks[11], (N_EVEN, B_KV_RANK)),
        "b_w_uq": nrm(ks[12], (N_EVEN, B_Q_RANK, B_HEADS * (B_NOPE + B_ROPE)), B_Q_RANK ** -0.5),
        "b_w_ukv": nrm(ks[13], (N_EVEN, B_KV_RANK, B_HEADS * (B_NOPE + B_VDIM)), B_KV_RANK ** -0.5),
        "w_out_even": nrm(ks[14], (N_EVEN, A_WIDTH + B_WIDTH, d), BETA * (A_WIDTH + B_WIDTH) ** -0.5),
        "w_in_odd": nrm(ks[15], (N_ODD, d, ODD_IN), d ** -0.5),
        "c_w_gate": nrm(ks[16], (N_ODD, C_GATE_RANK, C_HEADS * C_DK), C_GATE_RANK ** -0.5),
        "c_b_gate": nrm(ks[17], (N_ODD, C_HEADS * C_DK), 0.1),
        "c_ln_g": gain(ks[18], (N_ODD, C_DV)),
        "c_ln_b": nrm(ks[19], (N_ODD, C_DV), 0.02),
        "w_out_odd": nrm(ks[20], (N_ODD, C_HEADS * C_DV, d), BETA * (C_HEADS * C_DV) ** -0.5),
        "w_ff1": nrm(ks[21], (DEPTH, d, D_FF), d ** -0.5),
        "w_ff2": nrm(ks[22], (DEPTH, D_FF, d), BETA * D_FF ** -0.5),
    }


def reference(x, positions, ln1_g, ln1_b, ln2_g, ln2_b,
              w_in_even, a_w_s, a_b_s, a_ln_g, a_ln_b,
              b_q_norm, b_kv_norm, b_w_uq, b_w_ukv, w_out_even,
              w_in_odd, c_w_gate, c_b_gate, c_ln_g, c_ln_b, w_out_odd,
              w_ff1, w_ff2):
    bn, s, _ = x.shape
    for layer in range(DEPTH):
        j = layer // 2
        if layer % 2 == 0:
            z = x @ w_in_even[j]
            o1 = A_WIDTH
            o2 = o1 + A_WIDTH
            o3 = o2 + B_Q_RANK
            o4 = o3 + B_KV_RANK
            u = jax.nn.gelu(z[..., :o1])
            vv = jax.nn.gelu(z[..., o1:o2])
            y_a = chunk_gmlp(u, vv, a_w_s[j], a_b_s[j], a_ln_g[j], a_ln_b[j])
            y_b = mla(z[..., o2:o3], z[..., o3:o4], z[..., o4:], positions,
                      b_q_norm[j], b_kv_norm[j], b_w_uq[j], b_w_ukv[j])
            y = jnp.concatenate([y_a, y_b], -1) @ w_out_even[j]
        else:
            z = x @ w_in_odd[j]
            hk, hv = C_HEADS * C_DK, C_HEADS * C_DV
            q = z[..., :hk].reshape(bn, s, C_HEADS, C_DK)
            k = z[..., hk:2 * hk].reshape(bn, s, C_HEADS, C_DK)
            v = z[..., 2 * hk:2 * hk + hv].reshape(bn, s, C_HEADS, C_DV)
            g = z[..., 2 * hk + hv:2 * hk + 2 * hv]
            zg = z[..., 2 * hk + 2 * hv:]
            logits = (zg @ c_w_gate[j] + c_b_gate[j]).astype(jnp.float32)
            log_a = (jax.nn.log_sigmoid(logits) / C_GATE_TAU).reshape(bn, s, C_HEADS, C_DK)
            o = gla(q, k, v, log_a)
            o = layer_norm(o, c_ln_g[j], c_ln_b[j]).astype(x.dtype).reshape(bn, s, hv)
            y = (o * jax.nn.silu(g)) @ w_out_odd[j]
        x = layer_norm(ALPHA * x + y, ln1_g[layer], ln1_b[layer])
        x = layer_norm(ALPHA * x + sqrelu_mlp(x, w_ff1[layer], w_ff2[layer]), ln2_g[layer], ln2_b[layer])
    return x
```

```python
import contextlib
import math
import numpy as np
import ml_dtypes
import concourse.bass as bass
import concourse.mybir as mybir
from concourse.bass_utils import run_bass_kernel_spmd

F32 = mybir.dt.float32
BF16 = mybir.dt.bfloat16
I32 = mybir.dt.int32
AF = mybir.ActivationFunctionType
ALU = mybir.AluOpType

NCORES = 8
TOK = 2048
NTL = 16
D = 1024
DFF = 4096
ALPHA = 8.0 ** 0.25
EPS = 1e-5
PI = math.pi
ENGS = ("pe", "act", "dve", "pool", "sp")
NDS = 8
NO_BARRIER = False
FUSED = True
OVERLAP_XO = True
OVERLAP_QK = True
OVERLAP_XE2 = True
OVERLAP_XE1 = True
DIRECT = True
CC_INC = 16


class Buf:
    __slots__ = ("w", "r")

    def __init__(self):
        self.w = None
        self.r = []


class Op:
    __slots__ = ("eng", "fn", "deps", "dma", "needed", "done", "cc")

    def __init__(self, eng, fn, dma):
        self.eng = eng
        self.fn = fn
        self.deps = []
        self.dma = dma
        self.needed = False
        self.done = None
        self.cc = False


class Phase:
    def __init__(self, C, name):
        self.C = C
        self.nc = C.nc
        self.name = name
        self.ops = {e: [] for e in ENGS}
        self.st = contextlib.ExitStack()
        self.n = 0

    def sb(self, name, shape, dt):
        self.n += 1
        t = self.st.enter_context(self.nc.sbuf_tensor("%s_%s" % (self.name, name), shape, dt))
        return t, Buf()

    def add(self, eng, fn, reads=(), writes=(), dma=False, cc=False):
        op = Op(eng, fn, dma)
        op.cc = cc
        deps = {}
        for b in reads:
            if b.w is not None:
                deps[id(b.w)] = b.w
        for b in writes:
            if b.w is not None:
                deps[id(b.w)] = b.w
            for r in b.r:
                deps[id(r)] = r
        for d in deps.values():
            need = True
            if d.eng == eng and not d.dma and not dma and eng == "pe":
                need = False
            if need:
                d.needed = True
                op.deps.append(d)
        for b in reads:
            b.r.append(op)
        for b in writes:
            b.w = op
            b.r = []
        self.ops[eng].append(op)
        return op

    def finish(self):
        nc = self.nc
        with contextlib.ExitStack() as st:
            esem = self.C.esem
            dsem = self.C.dsem
            last_dma = {}
            for e in ENGS:
                c = self.C.ecount[e]
                nd = self.C.ndma[e]
                last = None
                for op in self.ops[e]:
                    if not op.dma:
                        last = op
                if last is not None:
                    last.needed = True
                for op in self.ops[e]:
                    if op.cc:
                        self.C.ccn += 1
                        op.done = (self.C.ccsem, self.C.ccn, ("cc",))
                        last_dma[(e, "cc")] = op
                    elif op.dma:
                        k = nd % NDS
                        op.done = (dsem[e][k], 16 * (nd // NDS + 1), ("d", e, k))
                        last_dma[(e, k)] = op
                        nd += 1
                    elif op.needed:
                        c += 1
                        op.done = (esem[e], c, ("e", e))
                self.C.ecount[e] = c
                self.C.ndma[e] = nd
            block = st.enter_context(nc.Block())

            def gen(e):
                def body(eng):
                    waited = {}

                    def wait(d):
                        sem, val, key = d.done
                        if waited.get(key, 0) >= val:
                            return
                        eng.wait_ge(sem, val)
                        waited[key] = val

                    lastc = None
                    for op in self.ops[e]:
                        for d in op.deps:
                            wait(d)
                        if op.cc:
                            sem, val, key = op.done
                            op.fn(eng).then_inc(sem, 1)
                        elif op.dma:
                            sem, val, key = op.done
                            if val > 16 and waited.get(key, 0) < val - 16:
                                eng.wait_ge(sem, val - 16)
                                waited[key] = val - 16
                            op.fn(eng).then_inc(sem, 16)
                        else:
                            ins = op.fn(eng)
                            if op.needed:
                                ins.then_inc(op.done[0], 1)
                            lastc = op
                    for (ee, k), d in last_dma.items():
                        if ee == e:
                            wait(d)
                    if lastc is not None:
                        wait(lastc)
                return body

            block.tensor(gen("pe"))
            block.scalar(gen("act"))
            block.vector(gen("dve"))
            block.gpsimd(gen("pool"))
            block.sync(gen("sp"))
        self.st.close()
        self.C.dbuf.clear()
        for (_, b) in self.C.ps:
            b.w = None
            b.r = []
        if not NO_BARRIER:
            nc.all_engine_barrier()


class Ctx:
    def __init__(self, nc, ext_in, ext_out, shapes):
        self.nc = nc
        self.ext_in = ext_in
        self.ext_out = ext_out
        self.shapes = shapes
        self.dram = {}
        self.dbuf = {}
        self.st = contextlib.ExitStack()
        self.ps = []
        for i in range(8):
            t = self.st.enter_context(nc.psum_tensor("ps%d" % i, [128, 512], F32))
            self.ps.append((t, Buf()))
        self.nx = 0
        self.ccsem = self.st.enter_context(nc.semaphore("ccsem"))
        self.ccn = 0
        self.esem = {e: self.st.enter_context(nc.semaphore("s_%s" % e)) for e in ENGS}
        self.dsem = {e: [self.st.enter_context(nc.semaphore("d_%s%d" % (e, i))) for i in range(NDS)] for e in ("sp", "pool")}
        self.ecount = {e: 0 for e in ENGS}
        self.ndma = {e: 0 for e in ENGS}
        ph = Phase(self, "Z")
        for (t, b) in self.ps:
            ph.add("dve", lambda e, t=t: e.memset(t[:], 0.0), writes=[b])
        ph.finish()

    def d(self, name):
        if name not in self.dram:
            shape, dt = self.shapes[name]
            kind = "Internal"
            if name in self.ext_in:
                kind = "ExternalInput"
            elif name in self.ext_out:
                kind = "ExternalOutput"
            self.dram[name] = self.nc.dram_tensor(name, list(shape), dt, kind=kind).ap()
        return self.dram[name]

    def db(self, name, key=0):
        k = (name, key)
        if k not in self.dbuf:
            self.dbuf[k] = Buf()
        return self.dbuf[k]


def my_rank(ph, e):
    C = ph.C
    if getattr(C, "_rank", None) is None:
        C._rank = e.snap(e.partition_id() % 4, min_val=0, max_val=3)
    return C._rank


def make_ident(ph, ident, identb):
    ph.add("pool", lambda e: e.memset(ident[:], 1.0), writes=[identb])
    ph.add("pool", lambda e: e.affine_select(out=ident[:], in_=ident[:], pattern=[[-1, 128]],
                                            compare_op=ALU.is_equal, fill=0.0, base=0, channel_multiplier=1),
           reads=[identb], writes=[identb])


def load_xT(ph, C, xname, tile0, ntiles, xT, xTb, xst, ident, identb, psA, psB, col0=0):
    x = C.d(xname)
    for i in range(ntiles):
        t = tile0 + i
        xs, xsb = xst[i % len(xst)]
        ph.add("sp", lambda e, xs=xs, t=t: e.dma_start(out=xs[:], in_=x[t * 128:(t + 1) * 128, :]),
               reads=[C.db(xname, t)], writes=[xsb], dma=True)
        for half in range(2):
            pp, pb = (psA, psB)[half]
            for j in range(4):
                k = half * 4 + j
                ph.add("pe", lambda e, pp=pp, j=j, k=k, xs=xs: e.transpose(
                    out=pp[:, j * 128:(j + 1) * 128], in_=xs[:, k * 128:(k + 1) * 128], identity=ident[:]),
                    reads=[xsb, identb], writes=[pb])
            c0 = col0 + i * 128
            ph.add("act" if half == 0 else "dve",
                   (lambda e, pp=pp, half=half, c0=c0: e.copy(
                       out=xT[:, half * 4:(half + 1) * 4, c0:c0 + 128],
                       in_=pp[:].rearrange("p (k t) -> p k t", k=4))) if half == 0 else
                   (lambda e, pp=pp, half=half, c0=c0: e.tensor_copy(
                       out=xT[:, half * 4:(half + 1) * 4, c0:c0 + 128],
                       in_=pp[:].rearrange("p (k t) -> p k t", k=4))),
                   reads=[pb], writes=[xTb])


def load_w_bf16(ph, C, name, dst, dstb, rows, c0, c1):
    w = C.d(name)
    if rows >= 128:
        src = w.rearrange("(k p) n -> p k n", p=128)[:, :, c0:c1]
    else:
        src = w[:, c0:c1]
    ph.add("pool", lambda e: e.dma_start(out=dst, in_=src), writes=[dstb], dma=True)


def phase_E1(C, L, part="all", gathers=None):
    MLA = part in ("all", "mla")
    GM = part in ("all", "gmlp")
    ph = Phase(C, "E1%s_%d" % (part[0], L))
    nc = C.nc
    sb = ph.sb
    xname = "x%d" % L
    PS = C.ps
    ident, identb = sb("ident", [128, 128], F32)
    make_ident(ph, ident, identb)
    Wu, Wub = sb("Wu", [128, 8, 512], BF16)
    Wv, Wvb = sb("Wv", [128, 8, 512], BF16)
    Wcq, Wcqb = sb("Wcq", [128, 8, 384], BF16)
    Wckv, Wckvb = sb("Wckv", [128, 8, 256], BF16)
    Wkr, Wkrb = sb("Wkr", [128, 8, 192], BF16)
    if MLA:
        stg, stgb = sb("stg", [128, 3, 768], F32)
        gq, gqb = sb("gq", [128, 3], F32)
        gkv, gkvb = sb("gkv", [128, 2], F32)
        Wuq, Wuqb = sb("Wuq", [128, 3, 768], BF16)
        Wuqr, Wuqrb = sb("Wuqr", [128, 3, 768], BF16)
        Wuk, Wukb = sb("Wuk", [128, 2, 512], BF16)
        Wuv, Wuvb = sb("Wuv", [128, 2, 512], BF16)
        ph.add("sp", lambda e: e.dma_start(out=gq[:], in_=C.d("gq%d" % L)), writes=[gqb], dma=True)
        ph.add("sp", lambda e: e.dma_start(out=gkv[:], in_=C.d("gkv%d" % L)), writes=[gkvb], dma=True)
        qscale = 96.0 ** -0.5
        for (nm, dst, dstb, nk, ncol, gg, ggb, sc) in (
                ("wuq%d" % L, Wuq, Wuqb, 3, 768, gq, gqb, qscale),
                ("wuqr%d" % L, Wuqr, Wuqrb, 3, 768, gq, gqb, qscale),
                ("wuk%d" % L, Wuk, Wukb, 2, 512, gkv, gkvb, 1.0),
                ("wuv%d" % L, Wuv, Wuvb, 2, 512, gkv, gkvb, 1.0)):
            ph.add("sp", lambda e, nm=nm, nk=nk, ncol=ncol: e.dma_start(
                out=stg[:, 0:nk, 0:ncol], in_=C.d(nm).rearrange("(k p) n -> p k n", p=128)),
                writes=[stgb], dma=True)
            for k in range(nk):
                ph.add("dve", lambda e, dst=dst, k=k, ncol=ncol, gg=gg, sc=sc: e.tensor_scalar(
                    out=dst[:, k, :], in0=stg[:, k, 0:ncol], scalar1=gg[:, k:k + 1], scalar2=sc,
                    op0=ALU.mult, op1=ALU.mult), reads=[stgb, ggb], writes=[dstb])
    if GM:
        wsf, wsfb = sb("wsf", [128, 4, 128], F32)
        wsT, wsTb = sb("wsT", [128, 4, 128], BF16)
        ph.add("sp", lambda e: e.dma_start(out=wsf[:], in_=C.d("wsT%d" % L)), writes=[wsfb], dma=True)
        for g in range(4):
            ph.add("pool", lambda e, g=g: e.affine_select(out=wsf[:, g, :], in_=wsf[:, g, :], pattern=[[1, 128]],
                                                         compare_op=ALU.is_ge, fill=0.0, base=0, channel_multiplier=-1),
                   reads=[wsfb], writes=[wsfb])
        ph.add("pool", lambda e: e.tensor_copy(out=wsT[:], in_=wsf[:]), reads=[wsfb], writes=[wsTb])
        bs, bsb = sb("bs", [1, 512], F32)
        ones1, ones1b = sb("ones1", [1, 128], F32)
        ph.add("sp", lambda e: e.dma_start(out=bs[:], in_=C.d("bs%d" % L)), writes=[bsb], dma=True)
        ph.add("pool", lambda e: e.memset(ones1[:], 1.0), writes=[ones1b])
        lng, lngb = sb("lng", [128, 512], F32)
        lnb, lnbb = sb("lnb", [128, 512], F32)
        ph.add("sp", lambda e: e.dma_start(out=lng[:], in_=C.d("alng%d" % L).partition_broadcast(128)), writes=[lngb], dma=True)
        ph.add("sp", lambda e: e.dma_start(out=lnb[:], in_=C.d("alnb%d" % L).partition_broadcast(128)), writes=[lnbb], dma=True)
    if MLA:
        onesq, onesqb = sb("onesq", [128, 128], F32)
        oneskv, oneskvb = sb("oneskv", [128, 128], F32)
        ph.add("pool", lambda e: e.memset(onesq[:], 1.0 / 384.0), writes=[onesqb])
        ph.add("pool", lambda e: e.memset(oneskv[:], 1.0 / 256.0), writes=[oneskvb])
    if GM:
        load_w_bf16(ph, C, "win%d" % L, Wu[:], Wub, 1024, 0, 512)
        load_w_bf16(ph, C, "win%d" % L, Wv[:], Wvb, 1024, 512, 1024)
    if MLA:
        load_w_bf16(ph, C, "win%d" % L, Wcq[:], Wcqb, 1024, 1024, 1408)
        load_w_bf16(ph, C, "win%d" % L, Wckv[:], Wckvb, 1024, 1408, 1664)
        load_w_bf16(ph, C, "wkr%d" % L, Wkr[:], Wkrb, 1024, 0, 192)
    if gathers:
        for (s_, src_, d_, dst_) in gathers:
            ph.add("pool", lambda e, src_=src_, dst_=dst_: e.collective_compute(
                "AllGather", ALU.bypass, replica_groups=GROUPS, ins=[src_], outs=[dst_]),
                reads=[C.db(s_)], writes=[C.db(d_)], dma=True, cc=True)
    if MLA:
        cosT, cosTb = sb("cosT", [128, TOK], F32)
        sinT, sinTb = sb("sinT", [128, TOK], F32)
        posi, posib = sb("posi", [128, TOK], I32)
        ang, angb = sb("ang", [128, TOK], F32)
        tA, tAb = sb("tA", [128, TOK], F32)
        tB, tBb = sb("tB", [128, TOK], F32)
        ki, kib = sb("ki", [128, TOK], I32)
        invf, invfb = sb("invf", [128, 2], F32)
        ph.add("sp", lambda e: e.dma_start(out=posi[:], in_=C.d("pos").partition_broadcast(128)), writes=[posib], dma=True)
        ph.add("sp", lambda e: e.dma_start(out=invf[:], in_=C.d("invf")), writes=[invfb], dma=True)
        ph.add("dve", lambda e: e.tensor_copy(out=ang[:], in_=posi[:]), reads=[posib], writes=[angb])
        ph.add("dve", lambda e: e.tensor_scalar(out=ang[:], in0=ang[:], scalar1=invf[:, 0:1], scalar2=None, op0=ALU.mult),
               reads=[angb, invfb], writes=[angb])
        C1 = 6.28125
        C2 = 2.0 * PI - C1
        for (dstT, dstTb, shift) in ((sinT, sinTb, 0.0), (cosT, cosTb, PI / 2)):
            ph.add("dve", lambda e, shift=shift: e.tensor_scalar(out=tA[:], in0=ang[:], scalar1=shift, scalar2=None, op0=ALU.add),
                   reads=[angb], writes=[tAb])
            ph.add("dve", lambda e: e.tensor_scalar(out=tB[:], in0=tA[:], scalar1=1.0 / (2 * PI), scalar2=None, op0=ALU.mult),
                   reads=[tAb], writes=[tBb])
            ph.add("dve", lambda e: e.tensor_copy(out=ki[:], in_=tB[:]), reads=[tBb], writes=[kib])
            ph.add("dve", lambda e: e.tensor_copy(out=tB[:], in_=ki[:]), reads=[kib], writes=[tBb])
            ph.add("dve", lambda e: e.scalar_tensor_tensor(out=tA[:], in0=tB[:], scalar=-C1, in1=tA[:], op0=ALU.mult, op1=ALU.add),
                   reads=[tAb, tBb], writes=[tAb])
            ph.add("dve", lambda e: e.scalar_tensor_tensor(out=tA[:], in0=tB[:], scalar=-C2, in1=tA[:], op0=ALU.mult, op1=ALU.add),
                   reads=[tAb, tBb], writes=[tAb])
            ph.add("dve", lambda e: e.tensor_single_scalar(out=tB[:], in_=tA[:], scalar=PI, op=ALU.is_gt), reads=[tAb], writes=[tBb])
            ph.add("dve", lambda e: e.scalar_tensor_tensor(out=tA[:], in0=tB[:], scalar=-2 * PI, in1=tA[:], op0=ALU.mult, op1=ALU.add),
                   reads=[tAb, tBb], writes=[tAb])
            ph.add("dve", lambda e: e.tensor_single_scalar(out=tB[:], in_=tA[:], scalar=-PI, op=ALU.is_lt), reads=[tAb], writes=[tBb])
            ph.add("dve", lambda e: e.scalar_tensor_tensor(out=tA[:], in0=tB[:], scalar=2 * PI, in1=tA[:], op0=ALU.mult, op1=ALU.add),
                   reads=[tAb, tBb], writes=[tAb])
            ph.add("dve", lambda e: e.tensor_scalar(out=tA[:], in0=tA[:], scalar1=-PI, scalar2=PI, op0=ALU.max, op1=ALU.min),
                   reads=[tAb], writes=[tAb])
            ph.add("act", lambda e, dstT=dstT: e.activation(out=dstT[:], in_=tA[:], func=AF.Sin), reads=[tAb], writes=[dstTb])
        ph.add("dve", lambda e: e.tensor_scalar(out=sinT[:], in0=sinT[:], scalar1=invf[:, 1:2], scalar2=None, op0=ALU.mult),
               reads=[sinTb, invfb], writes=[sinTb])

    xst = [sb("xst%d" % i, [128, D], F32) for i in range(2)]
    xT, xTb = sb("xT", [128, 8, 512], BF16)
    uT, uTb = sb("uT", [128, 4, 512], BF16)
    vgs = [sb("vg%d" % i, [128, 512], F32) for i in range(4)]
    vlns = [sb("vln%d" % i, [128, 512], BF16) for i in range(4)]
    vscr = []
    for i in range(4):
        a_, ab_ = sb("st4_%d" % i, [128, 4, 6], F32)
        b_, bb_ = sb("mv4_%d" % i, [128, 4, 2], F32)
        c_, cb_ = sb("rs4_%d" % i, [128, 4], F32)
        vscr.append((a_, ab_, b_, bb_, c_, cb_))
    yaT, yaTb = sb("yaT", [128, 4, 512], BF16)
    cqT, cqTb = sb("cqT", [128, 3, 512], BF16)
    sqq, sqqb = sb("sqq", [128, 3, 512], F32)
    ckvT, ckvTb = sb("ckvT", [128, 2, 512], BF16)
    sqkv, sqkvb = sb("sqkv", [128, 2, 512], F32)
    rq, rqb = sb("rq", [128, 512], F32)
    rkv, rkvb = sb("rkv", [128, 512], F32)
    rkvt, rkvtb = sb("rkvt", [128, 4], F32)
    Cq, Cqb = sb("Cq", [128, 512], F32)
    Sq, Sqb = sb("Sq", [128, 512], F32)
    t1, t1b = sb("t1", [128, 512], F32)
    t2, t2b = sb("t2", [128, 512], F32)
    krp, krpb = sb("krp", [128, 512], BF16)
    QT, QTb = sb("QT", [96, 8, 512], BF16)
    KT, KTb = sb("KT", [96, 8, 512], BF16)
    Vt, Vtb = sb("Vt", [128, 4, 512], BF16)

    QKT = OVERLAP_QK and part == "mla"
    if QKT:
        qst = C.d("qst%d" % L)
        kst = C.d("kst%d" % L)
    else:
        qs = C.d("qs%d" % L).rearrange("h p t -> p h t")
        ks = C.d("ks%d" % L).rearrange("h p t -> p h t")
    vs = C.d("vs%d" % L)
    yad = C.d("yaT%d" % L).rearrange("c p t -> p c t")

    for tg in range(4):
        load_xT(ph, C, xname, tg * 4, 4, xT, xTb, xst, ident, identb, PS[0], PS[1])
        if GM:
            for t in range(4):
                pp, pb = PS[4 + t % 2]
                vg, vgb = vgs[t]
                for k in range(8):
                    ph.add("pe", lambda e, pp=pp, t=t, k=k: e.matmul(out=pp[:, :], lhsT=xT[:, k, t * 128:(t + 1) * 128],
                                                                     rhs=Wv[:, k, :], start=(k == 0), stop=(k == 7)),
                           reads=[Wvb, xTb], writes=[pb])
                ph.add("act", lambda e, pp=pp, vg=vg: e.activation(out=vg[:], in_=pp[:, :], func=AF.Gelu_apprx_tanh),
                       reads=[pb], writes=[vgb])
            for c in range(4):
                pp, pb = PS[2 + c % 2]
                for k in range(8):
                    ph.add("pe", lambda e, pp=pp, c=c, k=k: e.matmul(out=pp[:, :], lhsT=Wu[:, k, c * 128:(c + 1) * 128],
                                                                     rhs=xT[:, k, :], start=(k == 0), stop=(k == 7)),
                           reads=[Wub, xTb], writes=[pb])
                ph.add("act", lambda e, pp=pp, c=c: e.activation(out=uT[:, c, :], in_=pp[:, :], func=AF.Gelu_apprx_tanh),
                       reads=[pb], writes=[uTb])
        if MLA:
            for c in range(3):
                pp, pb = PS[2 + c % 2]
                for k in range(8):
                    ph.add("pe", lambda e, pp=pp, c=c, k=k: e.matmul(out=pp[:, :], lhsT=Wcq[:, k, c * 128:(c + 1) * 128],
                                                                     rhs=xT[:, k, :], start=(k == 0), stop=(k == 7)),
                           reads=[Wcqb, xTb], writes=[pb])
                ph.add("act", lambda e, pp=pp, c=c: e.copy(out=cqT[:, c, :], in_=pp[:, :]), reads=[pb], writes=[cqTb])
                ph.add("act", lambda e, pp=pp, c=c: e.activation(out=sqq[:, c, :], in_=pp[:, :], func=AF.Square), reads=[pb], writes=[sqqb])
            for c in range(2):
                pp, pb = PS[4 + c % 2]
                for k in range(8):
                    ph.add("pe", lambda e, pp=pp, c=c, k=k: e.matmul(out=pp[:, :], lhsT=Wckv[:, k, c * 128:(c + 1) * 128],
                                                                     rhs=xT[:, k, :], start=(k == 0), stop=(k == 7)),
                           reads=[Wckvb, xTb], writes=[pb])
                ph.add("act", lambda e, pp=pp, c=c: e.copy(out=ckvT[:, c, :], in_=pp[:, :]), reads=[pb], writes=[ckvTb])
                ph.add("act", lambda e, pp=pp, c=c: e.activation(out=sqkv[:, c, :], in_=pp[:, :], func=AF.Square), reads=[pb], writes=[sqkvb])
        if GM:
            for t in range(4):
                vg, vgb = vgs[t]
                st4, st4b, mv4, mv4b, rs4, rs4b = vscr[t]
                for g in range(4):
                    ph.add("dve", lambda e, g=g, vg=vg, st4=st4: e.bn_stats(out=st4[:, g, :], in_=vg[:, g * 128:(g + 1) * 128]),
                           reads=[vgb], writes=[st4b])
                for g in range(4):
                    ph.add("dve", lambda e, g=g, st4=st4, mv4=mv4: e.bn_aggr(out=mv4[:, g, :], in_=st4[:, g, :]), reads=[st4b], writes=[mv4b])
                ph.add("act", lambda e, mv4=mv4, rs4=rs4: e.activation(out=rs4[:], in_=mv4[:, :, 1], func=AF.Sqrt, bias=EPS, scale=1.0),
                       reads=[mv4b], writes=[rs4b])
            for t in range(4):
                vg, vgb = vgs[t]
                vln, vlnb = vlns[t]
                st4, st4b, mv4, mv4b, rs4, rs4b = vscr[t]
                ph.add("dve", lambda e, rs4=rs4: e.reciprocal(out=rs4[:], in_=rs4[:]), reads=[rs4b], writes=[rs4b])
                for g in range(4):
                    ph.add("dve", lambda e, g=g, vg=vg, mv4=mv4, rs4=rs4: e.tensor_scalar(
                        out=vg[:, g * 128:(g + 1) * 128], in0=vg[:, g * 128:(g + 1) * 128],
                        scalar1=mv4[:, g, 0:1], scalar2=rs4[:, g:g + 1], op0=ALU.subtract, op1=ALU.mult),
                        reads=[vgb, mv4b, rs4b], writes=[vgb])
                ph.add("dve", lambda e, vg=vg: e.tensor_tensor(out=vg[:], in0=vg[:], in1=lng[:], op=ALU.mult), reads=[vgb, lngb], writes=[vgb])
                ph.add("dve", lambda e, vg=vg, vln=vln: e.tensor_tensor(out=vln[:], in0=vg[:], in1=lnb[:], op=ALU.add), reads=[vgb, lnbb], writes=[vlnb])
        if MLA:
            pp, pb = PS[6]
            for c in range(3):
                ph.add("pe", lambda e, pp=pp, c=c: e.matmul(out=pp[:, :], lhsT=onesq[:], rhs=sqq[:, c, :], start=(c == 0), stop=(c == 2)),
                       reads=[onesqb, sqqb], writes=[pb])
            ph.add("act", lambda e, pp=pp: e.activation(out=rq[:], in_=pp[:, :], func=AF.Sqrt, bias=EPS, scale=1.0), reads=[pb], writes=[rqb])
            ph.add("dve", lambda e: e.reciprocal(out=rq[:], in_=rq[:]), reads=[rqb], writes=[rqb])
            pp, pb = PS[7]
            for c in range(2):
                ph.add("pe", lambda e, pp=pp, c=c: e.matmul(out=pp[:, :], lhsT=oneskv[:], rhs=sqkv[:, c, :], start=(c == 0), stop=(c == 1)),
                       reads=[oneskvb, sqkvb], writes=[pb])
            ph.add("act", lambda e, pp=pp: e.activation(out=rkv[:], in_=pp[:, :], func=AF.Sqrt, bias=EPS, scale=1.0), reads=[pb], writes=[rkvb])
            ph.add("dve", lambda e: e.reciprocal(out=rkv[:], in_=rkv[:]), reads=[rkvb], writes=[rkvb])
            pp, pb = PS[6]
            for t in range(4):
                for c in range(2):
                    ph.add("pe", lambda e, pp=pp, c=c, t=t: e.matmul(out=pp[:, t:t + 1], lhsT=sqkv[:, c, t * 128:(t + 1) * 128],
                                                                     rhs=oneskv[:, 0:1], start=(c == 0), stop=(c == 1)),
                           reads=[oneskvb, sqkvb], writes=[pb])
            ph.add("act", lambda e, pp=pp: e.activation(out=rkvt[:], in_=pp[:, 0:4], func=AF.Sqrt, bias=EPS, scale=1.0), reads=[pb], writes=[rkvtb])
            ph.add("dve", lambda e: e.reciprocal(out=rkvt[:], in_=rkvt[:]), reads=[rkvtb], writes=[rkvtb])
            cs = slice(tg * 512, (tg + 1) * 512)
            ph.add("dve", lambda e, cs=cs: e.tensor_tensor(out=Cq[64:96, :], in0=rq[64:96, :], in1=cosT[64:96, cs], op=ALU.mult),
                   reads=[rqb, cosTb], writes=[Cqb])
            ph.add("dve", lambda e, cs=cs: e.tensor_tensor(out=Sq[64:96, :], in0=rq[64:96, :], in1=sinT[64:96, cs], op=ALU.mult),
                   reads=[rqb, sinTb], writes=[Sqb])
            for h in range(8):
                pa, pab = PS[2 + (h % 2) * 2]
                pr, prb = PS[3 + (h % 2) * 2]
                for k in range(3):
                    ph.add("pe", lambda e, pa=pa, h=h, k=k: e.matmul(out=pa[0:96, :], lhsT=Wuq[:, k, h * 96:(h + 1) * 96],
                                                                     rhs=cqT[:, k, :], start=(k == 0), stop=(k == 2)),
                           reads=[Wuqb, cqTb], writes=[pab])
                for k in range(3):
                    ph.add("pe", lambda e, pr=pr, h=h, k=k: e.matmul(out=pr[0:96, :], lhsT=Wuqr[:, k, h * 96:(h + 1) * 96],
                                                                     rhs=cqT[:, k, :], start=(k == 0), stop=(k == 2)),
                           reads=[Wuqrb, cqTb], writes=[prb])
                ph.add("dve", lambda e, pa=pa, h=h: e.tensor_tensor(out=QT[0:64, h, :], in0=pa[0:64, :], in1=rq[0:64, :], op=ALU.mult),
                       reads=[pab, rqb], writes=[QTb])
                ph.add("dve", lambda e, pa=pa: e.tensor_tensor(out=t1[64:96, :], in0=pa[64:96, :], in1=Cq[64:96, :], op=ALU.mult),
                       reads=[pab, Cqb], writes=[t1b])
                ph.add("dve", lambda e, pr=pr: e.tensor_tensor(out=t2[64:96, :], in0=pr[64:96, :], in1=Sq[64:96, :], op=ALU.mult),
                       reads=[prb, Sqb], writes=[t2b])
                ph.add("dve", lambda e, h=h: e.tensor_tensor(out=QT[64:96, h, :], in0=t1[64:96, :], in1=t2[64:96, :], op=ALU.add),
                       reads=[t1b, t2b], writes=[QTb])
            if QKT:
                ph.add("sp", lambda e, tg=tg: e.dma_start(out=qst[tg].rearrange("h p t -> p h t"), in_=QT[:]),
                       reads=[QTb], writes=[C.db("qst%d" % L, tg)], dma=True)
                ph.add("pool", lambda e, tg=tg: e.collective_compute(
                    "AllGather", ALU.bypass, replica_groups=GROUPS, ins=[qst[tg].rearrange("h p t -> h (p t)")],
                    outs=[C.d("qallt%d_%d" % (tg, L)).rearrange("r h p t -> (r h) (p t)")]),
                    reads=[C.db("qst%d" % L, tg)], writes=[C.db("qallt%d_%d" % (tg, L))], dma=True, cc=True)
            else:
                ph.add("sp", lambda e, cs=cs: e.dma_start(out=qs[:, :, cs], in_=QT[:]), reads=[QTb], writes=[C.db("qs%d" % L, tg)], dma=True)
            pa, pab = PS[2]
            pr, prb = PS[3]
            for k in range(8):
                ph.add("pe", lambda e, pa=pa, k=k: e.matmul(out=pa[0:96, :], lhsT=Wkr[:, k, 0:96], rhs=xT[:, k, :], start=(k == 0), stop=(k == 7)),
                       reads=[Wkrb, xTb], writes=[pab])
            for k in range(8):
                ph.add("pe", lambda e, pr=pr, k=k: e.matmul(out=pr[0:96, :], lhsT=Wkr[:, k, 96:192], rhs=xT[:, k, :], start=(k == 0), stop=(k == 7)),
                       reads=[Wkrb, xTb], writes=[prb])
            ph.add("dve", lambda e, pa=pa, cs=cs: e.tensor_tensor(out=t1[64:96, :], in0=pa[64:96, :], in1=cosT[64:96, cs], op=ALU.mult),
                   reads=[pab, cosTb], writes=[t1b])
            ph.add("dve", lambda e, pr=pr, cs=cs: e.tensor_tensor(out=t2[64:96, :], in0=pr[64:96, :], in1=sinT[64:96, cs], op=ALU.mult),
                   reads=[prb, sinTb], writes=[t2b])
            ph.add("dve", lambda e: e.tensor_tensor(out=krp[64:96, :], in0=t1[64:96, :], in1=t2[64:96, :], op=ALU.add),
                   reads=[t1b, t2b], writes=[krpb])
            for h in range(8):
                pp, pb = PS[4 + h % 2]
                for k in range(2):
                    ph.add("pe", lambda e, pp=pp, h=h, k=k: e.matmul(out=pp[0:64, :], lhsT=Wuk[:, k, h * 64:(h + 1) * 64],
                                                                     rhs=ckvT[:, k, :], start=(k == 0), stop=(k == 1)),
                           reads=[Wukb, ckvTb], writes=[pb])
                ph.add("dve", lambda e, pp=pp, h=h: e.tensor_tensor(out=KT[0:64, h, :], in0=pp[0:64, :], in1=rkv[0:64, :], op=ALU.mult),
                       reads=[pb, rkvb], writes=[KTb])
                ph.add("act", lambda e, h=h: e.copy(out=KT[64:96, h, :], in_=krp[64:96, :]), reads=[krpb], writes=[KTb])
            if QKT:
                ph.add("sp", lambda e, tg=tg: e.dma_start(out=kst[tg].rearrange("h p t -> p h t"), in_=KT[:]),
                       reads=[KTb], writes=[C.db("kst%d" % L, tg)], dma=True)
                ph.add("pool", lambda e, tg=tg: e.collective_compute(
                    "AllGather", ALU.bypass, replica_groups=GROUPS, ins=[kst[tg].rearrange("h p t -> h (p t)")],
                    outs=[C.d("kallt%d_%d" % (tg, L)).rearrange("r h p t -> (r h) (p t)")]),
                    reads=[C.db("kst%d" % L, tg)], writes=[C.db("kallt%d_%d" % (tg, L))], dma=True, cc=True)
            else:
                ph.add("sp", lambda e, cs=cs: e.dma_start(out=ks[:, :, cs], in_=KT[:]), reads=[KTb], writes=[C.db("ks%d" % L, tg)], dma=True)
            for t in range(4):
                pp, pb = PS[6 + t % 2]
                for k in range(2):
                    ph.add("pe", lambda e, pp=pp, t=t, k=k: e.matmul(out=pp[:, :], lhsT=ckvT[:, k, t * 128:(t + 1) * 128],
                                                                     rhs=Wuv[:, k, :], start=(k == 0), stop=(k == 1)),
                           reads=[Wuvb, ckvTb], writes=[pb])
                ph.add("dve", lambda e, pp=pp, t=t: e.tensor_scalar(out=Vt[:, t, :], in0=pp[:, :], scalar1=rkvt[:, t:t + 1], scalar2=None, op0=ALU.mult),
                       reads=[pb, rkvtb], writes=[Vtb])
            for dst in range(4):
                ph.add("sp", lambda e, dst=dst, tg=tg: e.dma_start(
                    out=vs[dst, tg * 512:(tg + 1) * 512, :].rearrange("(t p) c -> p t c", p=128),
                    in_=Vt[:, :, dst * 128:(dst + 1) * 128]),
                    reads=[Vtb], writes=[C.db("vs%d" % L, tg * 4 + dst)], dma=True)
        if GM:
            for t in range(4):
                vln, vlnb = vlns[t]
                mp, mpb = PS[6 + t % 2]
                for g in range(4):
                    ph.add("pe", lambda e, mp=mp, g=g, vln=vln: e.matmul(out=mp[:, g * 128:(g + 1) * 128], lhsT=vln[:, g * 128:(g + 1) * 128],
                                                                rhs=wsT[:, g, :], start=True, stop=False),
                           reads=[vlnb, wsTb], writes=[mpb])
                    ph.add("pe", lambda e, mp=mp, g=g: e.matmul(out=mp[:, g * 128:(g + 1) * 128], lhsT=ones1[0:1, :],
                                                                rhs=bs[0:1, g * 128:(g + 1) * 128], start=False, stop=True),
                           reads=[ones1b, bsb], writes=[mpb])
                ph.add("dve", lambda e, mp=mp, t=t: e.tensor_tensor(out=yaT[:, :, t * 128:(t + 1) * 128],
                                                                    in0=mp[:].rearrange("p (g t) -> p g t", g=4),
                                                                    in1=uT[:, :, t * 128:(t + 1) * 128], op=ALU.mult),
                       reads=[mpb, uTb], writes=[yaTb])
            ph.add("sp", lambda e, tg=tg: e.dma_start(out=yad[:, :, tg * 512:(tg + 1) * 512], in_=yaT[:]),
                   reads=[yaTb], writes=[C.db("yaT%d" % L, tg)], dma=True)
    ph.finish()


def phase_E2A(C, L):
    ph = Phase(C, "E2A_%d" % L)
    sb = ph.sb
    PS = C.ps
    S = 8192
    if DIRECT and OVERLAP_QK:
        qrn, krn, vrn = [("%s%d" % (n, L)) for n in ("vall", "vall", "vall")]
    elif DIRECT:
        qrn, krn, vrn = [("%s%d" % (n, L)) for n in ("qall", "kall", "vall")]
    else:
        qrn, krn, vrn = [("%s%d" % (n, L)) for n in ("qr", "kr", "vr")]
    qr = C.d(qrn)
    kr = C.d(krn)
    vr = C.d(vrn)
    ybs = C.d("ybs%d" % L)
    tri, trib = sb("tri", [128, 128], BF16)
    trf, trfb = sb("trf", [128, 128], F32)
    ph.add("pool", lambda e: e.memset(trf[:], 1.0), writes=[trfb])
    ph.add("pool", lambda e: e.affine_select(out=trf[:], in_=trf[:], pattern=[[1, 128]], compare_op=ALU.is_ge,
                                            fill=0.0, base=0, channel_multiplier=-1), reads=[trfb], writes=[trfb])
    ph.add("pool", lambda e: e.tensor_copy(out=tri[:], in_=trf[:]), reads=[trfb], writes=[trib])
    sel, selb = sb("sel", [65, 64], F32)
    ph.add("pool", lambda e: e.memset(sel[:], 0.0), writes=[selb])
    ph.add("pool", lambda e: e.memset(sel[64:65, :], 1.0), reads=[selb], writes=[selb])
    QTh, KTh, Vh, yb = [], [], [], []
    Vrb = []
    HQ = 96 * TOK
    for hh in range(2):
        q, qb = sb("Q%d" % hh, [96, 4, 2048], BF16)
        k, kb = sb("K%d" % hh, [96, 4, 2048], BF16)
        v, vb = sb("V%d" % hh, [128, 64, 65], BF16)
        y, yb_ = sb("Y%d" % hh, [64, S], BF16)
        QTh.append((q, qb)); KTh.append((k, kb)); Vh.append((v, vb)); yb.append((y, yb_))
        vrb = [Buf() for _ in range(4)]
        Vrb.append(vrb)
        ph.add("pool", lambda e, v=v: e.memset(v[:, :, 64:65], 1.0), writes=[vb])
        if DIRECT:
            def ldq(e, dst=q, hh=hh, src=qr):
                rank = my_rank(ph, e)
                return e.dma_start(out=dst[:], in_=bass.AP(src.tensor, rank * (8 * HQ) + hh * HQ, [[TOK, 96], [2 * HQ, 4], [1, TOK]]))

            def ldk(e, dst=k, hh=hh, src=kr):
                rank = my_rank(ph, e)
                return e.dma_start(out=dst[:], in_=bass.AP(src.tensor, rank * (8 * HQ) + hh * HQ, [[TOK, 96], [2 * HQ, 4], [1, TOK]]))

            def ldv(e, dst=v, hh=hh, src=vr):
                rank = my_rank(ph, e)
                return e.dma_start(out=dst[:, :, 0:64], in_=bass.AP(src.tensor, rank * (4 * TOK * 128) + hh * 64,
                                                                    [[128, 128], [128 * 128, 64], [1, 64]]))
            if OVERLAP_QK:
                QB = 96 * 512
                for (dst_, dstb_, pre_) in ((q, qb, "qallt"), (k, kb, "kallt")):
                    for tg in range(4):
                        nm_ = "%s%d_%d" % (pre_, tg, L)

                        def ldt(e, dst_=dst_, nm_=nm_, tg=tg, hh=hh):
                            rank = my_rank(ph, e)
                            return e.dma_start(out=dst_[:, :, tg * 512:(tg + 1) * 512],
                                               in_=bass.AP(C.d(nm_).tensor, rank * (2 * QB) + hh * QB,
                                                           [[512, 96], [8 * QB, 4], [1, 512]]))
                        ph.add("sp", ldt, reads=[C.db(nm_)], writes=[dstb_], dma=True)
            else:
                ph.add("sp", ldq, reads=[C.db(qrn)], writes=[qb], dma=True)
                ph.add("sp", ldk, reads=[C.db(krn)], writes=[kb], dma=True)
            ph.add("sp", ldv, reads=[C.db(vrn), vb], writes=vrb, dma=True)
        else:
            ph.add("sp", lambda e, q=q, hh=hh: e.dma_start(out=q[:], in_=qr[:, hh, :, :].rearrange("r p t -> p r t")),
                   reads=[C.db(qrn)], writes=[qb], dma=True)
            ph.add("sp", lambda e, k=k, hh=hh: e.dma_start(out=k[:], in_=kr[:, hh, :, :].rearrange("r p t -> p r t")),
                   reads=[C.db(krn)], writes=[kb], dma=True)
            for r in range(4):
                ph.add("sp", lambda e, v=v, hh=hh, r=r: e.dma_start(
                    out=v[:, r * 16:(r + 1) * 16, 0:64],
                    in_=vr[r, :, hh * 64:(hh + 1) * 64].rearrange("(t p) c -> p t c", p=128)),
                    reads=[C.db(vrn), vb], writes=[vrb[r]], dma=True)
    NSB = 4
    LA = NSB - 1
    PT = [sb("PT%d" % i, [128, 512], BF16) for i in range(NSB)]
    acs = [sb("acs%d" % i, [65, 512], F32) for i in range(2)]
    rden = [sb("rden%d" % i, [64, 512], F32) for i in range(2)]
    steps = []
    for hh in range(2):
        for G in range(16):
            for kt in range(4 * G + 4):
                steps.append((hh, G, kt))
    n = len(steps)

    def geo(i):
        hh, G, kt = steps[i]
        j = kt - 4 * G
        c0 = 0 if j < 0 else j * 128
        return hh, G, kt, j, c0, 512 - c0

    for i in range(n + LA):
        if i < n:
            hh, G, kt, j, c0, ncol = geo(i)
            q, qb = QTh[hh]
            k, kb = KTh[hh]
            qf = q[:].rearrange("p r t -> p (r t)")
            kf = k[:].rearrange("p r t -> p (r t)")
            sp_, spb = PS[i % NSB]
            pt, ptb = PT[i % NSB]
            ph.add("pe", lambda e, sp_=sp_, kt=kt, G=G, c0=c0, ncol=ncol, kf=kf, qf=qf: e.matmul(
                out=sp_[:, 0:ncol], lhsT=kf[0:96, kt * 128:(kt + 1) * 128],
                rhs=qf[0:96, G * 512 + c0:(G + 1) * 512], start=True, stop=True),
                reads=[kb, qb], writes=[spb])
            ph.add("act", lambda e, sp_=sp_, pt=pt, ncol=ncol: e.activation(out=pt[:, 0:ncol], in_=sp_[:, 0:ncol], func=AF.Exp),
                   reads=[spb], writes=[ptb])
            if j >= 0:
                ph.add("dve" if OVERLAP_XE2 else "pool", lambda e, pt=pt: e.tensor_tensor(out=pt[:, 0:128], in0=pt[:, 0:128], in1=tri[:], op=ALU.mult),
                       reads=[ptb, trib], writes=[ptb])
        m = i - LA
        if m >= 0:
            hh, G, kt, j, c0, ncol = geo(m)
            ng = hh * 16 + G
            nkt = 4 * G + 4
            v, vb = Vh[hh]
            y, ybf = yb[hh]
            acc, accb = PS[4 + ng % 2]
            pt, ptb = PT[m % NSB]
            ph.add("pe", lambda e, acc=acc, pt=pt, kt=kt, c0=c0, ncol=ncol, nkt=nkt, v=v: e.matmul(
                out=acc[0:65, c0:512], lhsT=v[:, kt, :], rhs=pt[:, 0:ncol], start=(kt == 0), stop=(kt == nkt - 1)),
                reads=[vb, ptb, Vrb[hh][kt // 16]], writes=[accb])
            if kt == nkt - 1:
                a, ab = acs[ng % 2]
                rd, rdb = rden[ng % 2]
                dn, dnb = PS[6 + ng % 2]
                ph.add("act", lambda e, a=a, acc=acc: e.copy(out=a[:], in_=acc[0:65, :]), reads=[accb], writes=[ab])
                ph.add("pe", lambda e, dn=dn, a=a: e.matmul(out=dn[0:64, :], lhsT=sel[:], rhs=a[:], start=True, stop=True),
                       reads=[selb, ab], writes=[dnb])
                ph.add("dve", lambda e, rd=rd, dn=dn: e.reciprocal(out=rd[:], in_=dn[0:64, :]), reads=[dnb], writes=[rdb])
                ph.add("dve", lambda e, rd=rd, a=a, G=G, y=y: e.tensor_tensor(out=y[:, G * 512:(G + 1) * 512], in0=a[0:64, :], in1=rd[:], op=ALU.mult),
                       reads=[ab, rdb], writes=[ybf])
                if G % 4 == 3:
                    r = G // 4
                    ph.add("sp", lambda e, y=y, hh=hh, r=r: e.dma_start(out=ybs[r, hh, :, :], in_=y[:, r * 2048:(r + 1) * 2048]),
                           reads=[ybf], writes=[C.db("ybs%d" % L, hh * 4 + r)], dma=True)
                    if OVERLAP_XE2 and hh == 1:
                        src_ = ybs[r].rearrange("h p t -> (h p) t")
                        dst_ = C.d("yball%d" % L)[r].rearrange("r p t -> (r p) t")
                        ph.add("pool", lambda e, src_=src_, dst_=dst_: e.collective_compute(
                            "AllGather", ALU.bypass, replica_groups=GROUPS, ins=[src_], outs=[dst_]),
                            reads=[C.db("ybs%d" % L, r), C.db("ybs%d" % L, 4 + r)], writes=[C.db("yball%d" % L, r)], dma=True, cc=True)
    ph.finish()


def ln_stats(ph, src, srcb, dst, dstb, scr):
    st2, st2b, mv, mvb, rs, rsb = scr
    for c in range(2):
        ph.add("dve", lambda e, c=c: e.bn_stats(out=st2[:, c, :], in_=src[:, c * 512:(c + 1) * 512]), reads=[srcb], writes=[st2b])
    ph.add("dve", lambda e: e.bn_aggr(out=mv[:], in_=st2[:]), reads=[st2b], writes=[mvb])
    ph.add("act", lambda e: e.activation(out=rs[:, 0:1], in_=mv[:, 1:2], func=AF.Sqrt, bias=EPS, scale=1.0), reads=[mvb], writes=[rsb])
    ph.add("dve", lambda e: e.reciprocal(out=rs[:, 0:1], in_=rs[:, 0:1]), reads=[rsb], writes=[rsb])
    ph.add("dve", lambda e: e.scalar_tensor_tensor(out=rs[:, 1:2], in0=mv[:, 0:1], scalar=-1.0, in1=rs[:, 0:1], op0=ALU.mult, op1=ALU.mult),
           reads=[mvb, rsb], writes=[rsb])
    ph.add("act", lambda e: e.activation(out=dst, in_=src[:], func=AF.Identity, bias=rs[:, 1:2], scale=rs[:, 0:1]),
           reads=[srcb, rsb], writes=[dstb])


def ln_affine(ph, dst, dstb, gt, gtb, bt, btb):
    ph.add("dve", lambda e: e.tensor_tensor(out=dst, in0=dst, in1=gt[:], op=ALU.mult), reads=[dstb, gtb], writes=[dstb])
    ph.add("dve", lambda e: e.tensor_tensor(out=dst, in0=dst, in1=bt[:], op=ALU.add), reads=[dstb, btb], writes=[dstb])


def phase_POST(C, L, ysrc):
    ph = Phase(C, "PO_%d" % L)
    sb = ph.sb
    PS = C.ps
    xin = C.d("x%d" % L)
    xout = C.d("x%d" % (L + 1))
    ident, identb = sb("ident", [128, 128], F32)
    make_ident(ph, ident, identb)
    Wo, Wob = sb("Wo", [128, 8, 1024], BF16)
    Wob2 = [Buf(), Buf()]
    for n in range(2):
        load_w_bf16(ph, C, "wout%d" % L, Wo[:, :, n * 512:(n + 1) * 512], Wob2[n], 1024, n * 512, (n + 1) * 512)
    lnp = {}
    for nm in ("ln1g", "ln1b", "ln2g", "ln2b"):
        t, b = sb(nm, [128, D], F32)
        lnp[nm] = (t, b)

    def load_lnp(names):
        for nm in names:
            t, b = lnp[nm]
            ph.add("sp", lambda e, t=t, nm=nm: e.dma_start(out=t[:], in_=C.d("%s%d" % (nm, L)).partition_broadcast(128)), writes=[b], dma=True)
    NSG = 2
    TSG = TOK // NSG
    NTS = TSG // 128
    FG = 1024
    NFG = DFF // FG
    R = [sb("R%d" % i, [128, D], F32) for i in range(NTS)]
    xT1, xT1b = sb("xT1", [128, 8, TSG], BF16)
    import os as _os
    if _os.environ.get("PADYT"):
        sb("padyt", [128, 8, 512], BF16)
    yT = [sb("yT%d" % i, [128, 8, 512], BF16) for i in range(2)]
    W1 = [sb("W1_%d" % i, [128, 8, FG], BF16) for i in range(2)]
    W2 = [sb("W2_%d" % i, [128, FG // 128, D], BF16) for i in range(2)]
    hT = [sb("hT%d" % i, [128, FG // 128, 512], BF16) for i in range(2)]
    hr = [sb("hr%d" % i, [128, 512], F32) for i in range(2)]
    xs = [sb("xs%d" % i, [128, D], F32) for i in range(2)]
    scr = []
    for i in range(4):
        a_, ab_ = sb("st2_%d" % i, [128, 2, 6], F32)
        b_, bb_ = sb("mv_%d" % i, [128, 2], F32)
        c_, cb_ = sb("rs_%d" % i, [128, 2], F32)
        scr.append((a_, ab_, b_, bb_, c_, cb_))
    xo = [sb("xo%d" % i, [128, D], F32) for i in range(2)]
    w1d = C.d("wff1%d" % L).rearrange("(k p) n -> p k n", p=128)
    w2d = C.d("wff2%d" % L).rearrange("(k p) n -> p k n", p=128)
    nw = 0
    nh = 0

    def emit_A_load(sg, ti):
        tg, t = ti // 4, ti % 4
        gt = sg * NTS + ti
        if t == 0:
            yt, ytb = yT[tg % 2]
            c0 = 0
            tok0 = sg * TSG + tg * 512
            for (nm, nch) in ysrc:
                if nm.startswith("yball"):
                    def ldy(e, yt=yt, c0=c0, nch=nch, tok0=tok0, src=C.d(nm)):
                        rank = my_rank(ph, e)
                        return e.dma_start(out=yt[:, c0:c0 + nch, :],
                                           in_=bass.AP(src.tensor, rank * (4 * 128 * TOK) + tok0,
                                                       [[TOK, 128], [128 * TOK, 4], [1, 512]]))
                    ph.add("sp", ldy, reads=[C.db(nm)], writes=[ytb], dma=True)
                else:
                    src = C.d(nm).rearrange("c p t -> p c t")[:, :, tok0:tok0 + 512]
                    ph.add("sp", lambda e, yt=yt, c0=c0, nch=nch, src=src: e.dma_start(out=yt[:, c0:c0 + nch, :], in_=src),
                           reads=[C.db(nm)], writes=[ytb], dma=True)
                c0 += nch
        xx, xxb = xs[ti % 2]
        ph.add("sp", lambda e, xx=xx, gt=gt: e.dma_start(out=xx[:], in_=xin[gt * 128:(gt + 1) * 128, :]),
               reads=[C.db("x%d" % L, gt)], writes=[xxb], dma=True)

    def emit_A_mm(sg, ti):
        tg, t = ti // 4, ti % 4
        yt, ytb = yT[tg % 2]
        xx, xxb = xs[ti % 2]
        for n in range(2):
            pp, pb = PS[n + 2 * (t % 2)]
            for k in range(8):
                ph.add("pe", lambda e, pp=pp, yt=yt, k=k, n=n, t=t: e.matmul(
                    out=pp[:, :], lhsT=yt[:, k, t * 128:(t + 1) * 128], rhs=Wo[:, k, n * 512:(n + 1) * 512],
                    start=(k == 0), stop=(k == 7)), reads=[ytb, Wob2[n]], writes=[pb])
            ph.add("dve", lambda e, pp=pp, xx=xx, n=n: e.scalar_tensor_tensor(
                out=xx[:, n * 512:(n + 1) * 512], in0=xx[:, n * 512:(n + 1) * 512], scalar=ALPHA, in1=pp[:, :],
                op0=ALU.mult, op1=ALU.add), reads=[pb, xxb], writes=[xxb])

    def emit_A_stats(sg, ti):
        r, rb = R[ti]
        xx, xxb = xs[ti % 2]
        ln_stats(ph, xx, xxb, r[:], rb, scr[ti % 2])

    def emit_A_aff(sg, ti):
        r, rb = R[ti]
        ln_affine(ph, r[:], rb, lnp["ln1g"][0], lnp["ln1g"][1], lnp["ln1b"][0], lnp["ln1b"][1])
        for half in range(2):
            pp, pb = PS[4 + half]
            for j in range(4):
                k = half * 4 + j
                ph.add("pe", lambda e, pp=pp, r=r, j=j, k=k: e.transpose(
                    out=pp[:, j * 128:(j + 1) * 128], in_=r[:, k * 128:(k + 1) * 128], identity=ident[:]),
                    reads=[rb, identb], writes=[pb])
            ph.add("act", lambda e, pp=pp, half=half, ti=ti: e.copy(
                out=xT1[:, half * 4:(half + 1) * 4, ti * 128:(ti + 1) * 128],
                in_=pp[:].rearrange("p (k t) -> p k t", k=4)), reads=[pb], writes=[xT1b])

    def emit_C_stats(sg, ti):
        r, rb = R[ti]
        o, ob = xo[ti % 2]
        ln_stats(ph, r, rb, o[:], ob, scr[2 + ti % 2])

    def emit_C_aff(sg, ti):
        gt = sg * NTS + ti
        o, ob = xo[ti % 2]
        ln_affine(ph, o[:], ob, lnp["ln2g"][0], lnp["ln2g"][1], lnp["ln2b"][0], lnp["ln2b"][1])
        ph.add("sp", lambda e, o=o, gt=gt: e.dma_start(out=xout[gt * 128:(gt + 1) * 128, :], in_=o[:]),
               reads=[ob], writes=[C.db("x%d" % (L + 1), gt)], dma=True)

    for sg in range(NSG):
        if sg > 0:
            emit_C_stats(sg - 1, 0)
        emit_A_load(sg, 0)
        if sg == 0:
            load_lnp(("ln1g", "ln1b", "ln2g", "ln2b"))
        emit_A_mm(sg, 0)
        emit_A_stats(sg, 0)
        for ti in range(NTS):
            if sg > 0:
                emit_C_aff(sg - 1, ti)
                if ti + 1 < NTS:
                    emit_C_stats(sg - 1, ti + 1)
            if ti + 1 < NTS:
                emit_A_load(sg, ti + 1)
                emit_A_mm(sg, ti + 1)
                emit_A_stats(sg, ti + 1)
            emit_A_aff(sg, ti)
        msteps = [(fg, tg) for fg in range(NFG) for tg in range(TSG // 512)]
        wcur = {}

        def emit_mm1(si):
            nonlocal nw
            fg, tg = msteps[si]
            if tg == 0:
                w1, w1b = W1[nw % 2]
                w2, w2b = W2[nw % 2]
                nw += 1
                wcur[fg] = (w1, w1b, w2, w2b)
                ph.add("pool", lambda e, w1=w1, fg=fg: e.dma_start(out=w1[:], in_=w1d[:, :, fg * FG:(fg + 1) * FG]), writes=[w1b], dma=True)
                ph.add("pool", lambda e, w2=w2, fg=fg: e.dma_start(out=w2[:], in_=w2d[:, fg * (FG // 128):(fg + 1) * (FG // 128), :]), writes=[w2b], dma=True)
            w1, w1b, w2, w2b = wcur[fg]
            h, hb = hT[si % 2]
            for f in range(FG // 128):
                pp, pb = PS[f % 2]
                hrr, hrb = hr[f % 2]
                for k in range(8):
                    ph.add("pe", lambda e, pp=pp, w1=w1, f=f, k=k, tg=tg: e.matmul(
                        out=pp[:, :], lhsT=w1[:, k, f * 128:(f + 1) * 128], rhs=xT1[:, k, tg * 512:(tg + 1) * 512],
                        start=(k == 0), stop=(k == 7)), reads=[w1b, xT1b], writes=[pb])
                ph.add("act", lambda e, pp=pp, hrr=hrr: e.activation(out=hrr[:], in_=pp[:, :], func=AF.Relu), reads=[pb], writes=[hrb])
                ph.add("act", lambda e, hrr=hrr, h=h, f=f: e.activation(out=h[:, f, :], in_=hrr[:], func=AF.Square),
                       reads=[hrb], writes=[hb])

        def emit_mm2(si):
            fg, tg = msteps[si]
            w1, w1b, w2, w2b = wcur[fg]
            h, hb = hT[si % 2]
            for t in range(4):
                ti = tg * 4 + t
                r, rb = R[ti]
                for n in range(2):
                    pp, pb = PS[2 + n + 2 * (t % 2)]
                    for f in range(FG // 128):
                        ph.add("pe", lambda e, pp=pp, h=h, w2=w2, f=f, n=n, t=t: e.matmul(
                            out=pp[:, :], lhsT=h[:, f, t * 128:(t + 1) * 128], rhs=w2[:, f, n * 512:(n + 1) * 512],
                            start=(f == 0), stop=(f == FG // 128 - 1)), reads=[hb, w2b], writes=[pb])
                    if fg == 0:
                        ph.add("dve", lambda e, pp=pp, r=r, n=n: e.scalar_tensor_tensor(
                            out=r[:, n * 512:(n + 1) * 512], in0=r[:, n * 512:(n + 1) * 512], scalar=ALPHA, in1=pp[:, :],
                            op0=ALU.mult, op1=ALU.add), reads=[pb, rb], writes=[rb])
                    else:
                        ph.add("dve", lambda e, pp=pp, r=r, n=n: e.tensor_tensor(
                            out=r[:, n * 512:(n + 1) * 512], in0=r[:, n * 512:(n + 1) * 512], in1=pp[:, :], op=ALU.add),
                            reads=[pb, rb], writes=[rb])

        for si in range(len(msteps) + 1):
            if si < len(msteps):
                emit_mm1(si)
            if si >= 1:
                emit_mm2(si - 1)
    emit_C_stats(NSG - 1, 0)
    for ti in range(NTS):
        if ti + 1 < NTS:
            emit_C_stats(NSG - 1, ti + 1)
        emit_C_aff(NSG - 1, ti)
    ph.finish()


def phase_O1(C, L):
    ph = Phase(C, "O1_%d" % L)
    sb = ph.sb
    PS = C.ps
    xname = "x%d" % L
    ident, identb = sb("ident", [128, 128], F32)
    make_ident(ph, ident, identb)
    wn = "win%d" % L
    Wq, Wqb = sb("Wq", [128, 8, 512], BF16)
    Wk, Wkb = sb("Wk", [128, 8, 512], BF16)
    Wv, Wvb = sb("Wv", [128, 8, 1024], BF16)
    Wg, Wgb = sb("Wg", [128, 8, 1024], BF16)
    Wz, Wzb = sb("Wz", [128, 8, 16], BF16)
    wga, wgab = sb("wga", [16, 512], BF16)
    bga, bgab = sb("bga", [1, 512], F32)
    ones1, ones1b = sb("ones1", [1, 128], F32)
    ph.add("sp", lambda e: e.dma_start(out=bga[:], in_=C.d("bgate%d" % L)), writes=[bgab], dma=True)
    ph.add("pool", lambda e: e.memset(ones1[:], 1.0), writes=[ones1b])
    triU, triUb = sb("triU", [128, 128], F32)
    triL, triLb = sb("triL", [128, 128], F32)
    ph.add("pool", lambda e: e.memset(triU[:], -1.0 / 16.0), writes=[triUb])
    ph.add("pool", lambda e: e.affine_select(out=triU[:], in_=triU[:], pattern=[[1, 128]], compare_op=ALU.is_ge,
                                            fill=0.0, base=0, channel_multiplier=-1), reads=[triUb], writes=[triUb])
    ph.add("pool", lambda e: e.memset(triL[:], -1.0 / 16.0), writes=[triLb])
    ph.add("pool", lambda e: e.affine_select(out=triL[:], in_=triL[:], pattern=[[-1, 128]], compare_op=ALU.is_gt,
                                            fill=0.0, base=0, channel_multiplier=1), reads=[triLb], writes=[triLb])
    load_w_bf16(ph, C, wn, Wq[:], Wqb, 1024, 0, 512)
    load_w_bf16(ph, C, wn, Wk[:], Wkb, 1024, 512, 1024)
    load_w_bf16(ph, C, wn, Wv[:], Wvb, 1024, 1024, 2048)
    load_w_bf16(ph, C, wn, Wg[:], Wgb, 1024, 2048, 3072)
    load_w_bf16(ph, C, wn, Wz[:], Wzb, 1024, 3072, 3088)
    load_w_bf16(ph, C, "wgate%d" % L, wga[:], wgab, 16, 0, 512)
    xst = [sb("xst%d" % i, [128, D], F32) for i in range(2)]
    xT, xTb = sb("xT", [128, 8, 512], BF16)
    zgT, zgTb = sb("zgT", [16, 512], BF16)
    ex, exb = sb("ex", [128, 512], F32)
    Gp, Gpb = sb("Gp", [128, 512], F32)
    eb4, eb4b = sb("eb4", [128, 4, 4, 128], F32)
    enb4, enb4b = sb("enb4", [128, 4, 4, 128], F32)
    erev, erevb = sb("erev", [128, 512], F32)
    ebl, eblb = sb("ebl", [128, 4, 16], F32)
    dt, dtb = sb("dt", [128, 4], F32)
    qeT, qeTb = sb("qeT", [128, 4, 512], BF16)
    keT, keTb = sb("keT", [128, 4, 512], BF16)
    kdg, kdgb = sb("kdg", [128, 4, 512], BF16)
    vvg, vvgb = sb("vvg", [128, 4, 1024], BF16)
    sgg, sggb = sb("sgg", [128, 4, 1024], BF16)
    Sl, Slb = sb("Sl", [128, 1028], F32)
    ph.add("dve", lambda e: e.memset(Sl[:], 0.0), writes=[Slb])
    ph.add("dve", lambda e: e.memset(dt[:], 1.0), writes=[dtb])
    qed = C.d("qeT%d" % L).rearrange("h p t -> p h t")
    ked = C.d("keT%d" % L).rearrange("h p t -> p h t")
    kdd = C.d("kd%d" % L).rearrange("(t p) n -> p t n", p=128)
    vvd = C.d("vv%d" % L).rearrange("(t p) n -> p t n", p=128)
    sgd = C.d("sg%d" % L).rearrange("(t p) n -> p t n", p=128)
    qsc = 128.0 ** -0.5
    for tg in range(4):
        load_xT(ph, C, xname, tg * 4, 4, xT, xTb, xst, ident, identb, PS[0], PS[1])
        cs = slice(tg * 512, (tg + 1) * 512)
        pp, pb = PS[2]
        for k in range(8):
            ph.add("pe", lambda e, pp=pp, k=k: e.matmul(out=pp[0:16, :], lhsT=Wz[:, k, :], rhs=xT[:, k, :], start=(k == 0), stop=(k == 7)),
                   reads=[Wzb, xTb], writes=[pb])
        ph.add("act", lambda e, pp=pp: e.copy(out=zgT[:], in_=pp[0:16, :]), reads=[pb], writes=[zgTb])
        for t in range(4):
            ch = tg * 4 + t
            pp, pb = PS[3]
            ph.add("pe", lambda e, pp=pp, t=t: e.matmul(out=pp[:, :], lhsT=zgT[0:16, t * 128:(t + 1) * 128], rhs=wga[0:16, :], start=True, stop=False),
                   reads=[zgTb, wgab], writes=[pb])
            ph.add("pe", lambda e, pp=pp: e.matmul(out=pp[:, :], lhsT=ones1[0:1, :], rhs=bga[0:1, :], start=False, stop=True),
                   reads=[ones1b, bgab], writes=[pb])
            ph.add("act", lambda e, pp=pp: e.activation(out=ex[:], in_=pp[:, :], func=AF.Exp, scale=-1.0), reads=[pb], writes=[exb])
            ph.add("act", lambda e: e.activation(out=Gp[:], in_=ex[:], func=AF.Ln, bias=1.0, scale=1.0), reads=[exb], writes=[Gpb])
            pc, pcb = PS[4]
            for h in range(4):
                ph.add("pe", lambda e, pc=pc, h=h: e.matmul(out=pc[:, h * 128:(h + 1) * 128], lhsT=Gp[:, h * 128:(h + 1) * 128], rhs=triU[:],
                                                            start=True, stop=True), reads=[Gpb, triUb], writes=[pcb])
            ph.add("act", lambda e, pc=pc, t=t: e.activation(out=eb4[:, t, :, :], in_=pc[:].rearrange("p (h t) -> p h t", h=4), func=AF.Exp),
                   reads=[pcb], writes=[eb4b])
            ph.add("act", lambda e, pc=pc, t=t: e.activation(out=enb4[:, t, :, :], in_=pc[:].rearrange("p (h t) -> p h t", h=4), func=AF.Exp, scale=-1.0),
                   reads=[pcb], writes=[enb4b])
            ph.add("dve", lambda e, t=t, ch=ch: e.tensor_copy(out=ebl[:, :, ch], in_=eb4[:, t, :, 127]), reads=[eb4b], writes=[eblb])
            ph.add("dve", lambda e, ch=ch: e.tensor_tensor(out=dt[:], in0=dt[:], in1=ebl[:, :, ch], op=ALU.mult), reads=[eblb, dtb], writes=[dtb])
            pr, prb = PS[5]
            ph.add("pe", lambda e, pr=pr: e.matmul(out=pr[:, :], lhsT=triL[:], rhs=Gp[:], start=True, stop=True), reads=[Gpb, triLb], writes=[prb])
            ph.add("act", lambda e, pr=pr: e.activation(out=erev[:], in_=pr[:, :], func=AF.Exp), reads=[prb], writes=[erevb])
            pk, pkb = PS[6]
            for k in range(8):
                ph.add("pe", lambda e, pk=pk, k=k, t=t: e.matmul(out=pk[:, :], lhsT=xT[:, k, t * 128:(t + 1) * 128], rhs=Wk[:, k, :],
                                                                 start=(k == 0), stop=(k == 7)), reads=[xTb, Wkb], writes=[pkb])
            ph.add("dve", lambda e, pk=pk, t=t: e.tensor_tensor(out=kdg[:, t, :], in0=pk[:, :], in1=erev[:], op=ALU.mult),
                   reads=[pkb, erevb], writes=[kdgb])
            for n in range(2):
                pv, pvb = PS[2 + n * 5]
                for k in range(8):
                    ph.add("pe", lambda e, pv=pv, k=k, t=t, n=n: e.matmul(out=pv[:, :], lhsT=xT[:, k, t * 128:(t + 1) * 128],
                                                                          rhs=Wv[:, k, n * 512:(n + 1) * 512], start=(k == 0), stop=(k == 7)),
                           reads=[xTb, Wvb], writes=[pvb])
                ph.add("act" if n == 0 else "dve",
                       (lambda e, pv=pv, t=t, n=n: e.copy(out=vvg[:, t, n * 512:(n + 1) * 512], in_=pv[:, :])) if n == 0 else
                       (lambda e, pv=pv, t=t, n=n: e.tensor_copy(out=vvg[:, t, n * 512:(n + 1) * 512], in_=pv[:, :])),
                       reads=[pvb], writes=[vvgb])
            for n in range(2):
                pv, pvb = PS[2 + n * 5]
                for k in range(8):
                    ph.add("pe", lambda e, pv=pv, k=k, t=t, n=n: e.matmul(out=pv[:, :], lhsT=xT[:, k, t * 128:(t + 1) * 128],
                                                                          rhs=Wg[:, k, n * 512:(n + 1) * 512], start=(k == 0), stop=(k == 7)),
                           reads=[xTb, Wgb], writes=[pvb])
                ph.add("act", lambda e, pv=pv, t=t, n=n: e.activation(out=sgg[:, t, n * 512:(n + 1) * 512], in_=pv[:, :], func=AF.Silu),
                       reads=[pvb], writes=[sggb])
            for h in range(4):
                pss, pssb = PS[3 + (h % 2)]
                ph.add("pe", lambda e, pss=pss, h=h, t=t: e.matmul(out=pss[:, 0:256], lhsT=kdg[:, t, h * 128:(h + 1) * 128],
                                                                   rhs=vvg[:, t, h * 256:(h + 1) * 256], start=True, stop=True),
                       reads=[kdgb, vvgb], writes=[pssb])
                ph.add("dve", lambda e, pss=pss, h=h, ch=ch: e.scalar_tensor_tensor(
                    out=Sl[:, h * 256:(h + 1) * 256], in0=Sl[:, h * 256:(h + 1) * 256], scalar=ebl[:, h, ch:ch + 1],
                    in1=pss[:, 0:256], op0=ALU.mult, op1=ALU.add), reads=[pssb, eblb, Slb], writes=[Slb])
        for h in range(4):
            pq, pqb = PS[5 + (h % 2)]
            for k in range(8):
                ph.add("pe", lambda e, pq=pq, h=h, k=k: e.matmul(out=pq[:, :], lhsT=Wq[:, k, h * 128:(h + 1) * 128], rhs=xT[:, k, :],
                                                                 start=(k == 0), stop=(k == 7)), reads=[Wqb, xTb], writes=[pqb])
            ph.add("dve", lambda e, pq=pq, h=h: e.scalar_tensor_tensor(
                out=qeT[:, h, :].rearrange("p (c t) -> p c t", c=4), in0=pq[:].rearrange("p (c t) -> p c t", c=4), scalar=qsc,
                in1=eb4[:, :, h, :], op0=ALU.mult, op1=ALU.mult), reads=[pqb, eb4b], writes=[qeTb])
            pq, pqb = PS[2 + 5 * (h % 2)]
            for k in range(8):
                ph.add("pe", lambda e, pq=pq, h=h, k=k: e.matmul(out=pq[:, :], lhsT=Wk[:, k, h * 128:(h + 1) * 128], rhs=xT[:, k, :],
                                                                 start=(k == 0), stop=(k == 7)), reads=[Wkb, xTb], writes=[pqb])
            ph.add("dve", lambda e, pq=pq, h=h: e.tensor_tensor(
                out=keT[:, h, :].rearrange("p (c t) -> p c t", c=4), in0=pq[:].rearrange("p (c t) -> p c t", c=4),
                in1=enb4[:, :, h, :], op=ALU.mult), reads=[pqb, enb4b], writes=[keTb])
        ph.add("sp", lambda e, cs=cs: e.dma_start(out=qed[:, :, cs], in_=qeT[:]), reads=[qeTb], writes=[C.db("qeT%d" % L, tg)], dma=True)
        ph.add("sp", lambda e, cs=cs: e.dma_start(out=ked[:, :, cs], in_=keT[:]), reads=[keTb], writes=[C.db("keT%d" % L, tg)], dma=True)
        ph.add("sp", lambda e, tg=tg: e.dma_start(out=kdd[:, tg * 4:(tg + 1) * 4, :], in_=kdg[:]), reads=[kdgb], writes=[C.db("kd%d" % L, tg)], dma=True)
        ph.add("sp", lambda e, tg=tg: e.dma_start(out=vvd[:, tg * 4:(tg + 1) * 4, :], in_=vvg[:]), reads=[vvgb], writes=[C.db("vv%d" % L, tg)], dma=True)
        ph.add("sp", lambda e, tg=tg: e.dma_start(out=sgd[:, tg * 4:(tg + 1) * 4, :], in_=sgg[:]), reads=[sggb], writes=[C.db("sg%d" % L, tg)], dma=True)
    ph.add("dve", lambda e: e.tensor_copy(out=Sl[:, 1024:1028], in_=dt[:]), reads=[dtb, Slb], writes=[Slb])
    ph.add("sp", lambda e: e.dma_start(out=C.d("sloc%d" % L), in_=Sl[:]), reads=[Slb], writes=[C.db("sloc%d" % L)], dma=True)
    if OVERLAP_XO:
        ph.add("pool", lambda e: e.collective_compute(
            "AllGather", ALU.bypass, replica_groups=GROUPS, ins=[C.d("sloc%d" % L)],
            outs=[C.d("sall%d" % L).rearrange("r p n -> (r p) n")]),
            reads=[C.db("sloc%d" % L)], writes=[C.db("sall%d" % L)], dma=True, cc=True)
    ph.add("sp", lambda e: e.dma_start(out=C.d("ebl%d" % L), in_=ebl[:].rearrange("p h c -> p (h c)")), reads=[eblb], writes=[C.db("ebl%d" % L)], dma=True)
    ph.finish()


def phase_O2(C, L):
    ph = Phase(C, "O2_%d" % L)
    sb = ph.sb
    PS = C.ps
    ident, identb = sb("ident", [128, 128], F32)
    make_ident(ph, ident, identb)
    tri4, tri4b = sb("tri4", [128, 4, 128], F32)
    ph.add("pool", lambda e: e.memset(tri4[:], 1.0), writes=[tri4b])
    for h in range(4):
        ph.add("pool", lambda e, h=h: e.affine_select(out=tri4[:, h, :], in_=tri4[:, h, :], pattern=[[1, 128]], compare_op=ALU.is_ge,
                                                     fill=0.0, base=0, channel_multiplier=-1), reads=[tri4b], writes=[tri4b])
    SA, SAb = sb("SA", [128, 4, 1028], F32)
    oh, ohb = sb("oh", [128, 4], F32)
    ebl, eblb = sb("ebl", [128, 4, 16], F32)
    cg, cgb = sb("cg", [128, 4, 256], F32)
    cb, cbb = sb("cb", [128, 4, 256], F32)
    ph.add("sp", lambda e: e.dma_start(out=SA[:], in_=C.d("sall%d" % L).rearrange("r p n -> p r n")), reads=[C.db("sall%d" % L)], writes=[SAb], dma=True)
    ph.add("sp", lambda e: e.dma_start(out=oh[:], in_=C.d("oh")), writes=[ohb], dma=True)
    ph.add("sp", lambda e: e.dma_start(out=ebl[:].rearrange("p h c -> p (h c)"), in_=C.d("ebl%d" % L)), reads=[C.db("ebl%d" % L)], writes=[eblb], dma=True)
    for h in range(4):
        ph.add("sp", lambda e, h=h: e.dma_start(out=cg[:, h, :], in_=C.d("clng%d" % L).partition_broadcast(128)), writes=[cgb], dma=True)
        ph.add("sp", lambda e, h=h: e.dma_start(out=cb[:, h, :], in_=C.d("clnb%d" % L).partition_broadcast(128)), writes=[cbb], dma=True)
    S, Sb_ = sb("S", [128, 1024], F32)
    Sbf, Sbfb = sb("Sbf", [128, 1024], BF16)
    cur, curb = sb("cur", [128, 1024], F32)
    ph.add("dve", lambda e: e.tensor_copy(out=cur[:], in_=SA[:, 0, 0:1024]), reads=[SAb], writes=[curb])
    ph.add("dve", lambda e: e.tensor_scalar(out=S[:], in0=cur[:], scalar1=oh[:, 1:2], scalar2=None, op0=ALU.mult), reads=[curb, ohb], writes=[Sb_])
    for r in (1, 2):
        for h in range(4):
            ph.add("dve", lambda e, r=r, h=h: e.scalar_tensor_tensor(
                out=cur[:, h * 256:(h + 1) * 256], in0=cur[:, h * 256:(h + 1) * 256], scalar=SA[:, r, 1024 + h:1025 + h],
                in1=SA[:, r, h * 256:(h + 1) * 256], op0=ALU.mult, op1=ALU.add), reads=[curb, SAb], writes=[curb])
        ph.add("dve", lambda e, r=r: e.scalar_tensor_tensor(out=S[:], in0=cur[:], scalar=oh[:, r + 1:r + 2], in1=S[:], op0=ALU.mult, op1=ALU.add),
               reads=[curb, ohb, Sb_], writes=[Sb_])
    ph.add("act", lambda e: e.copy(out=Sbf[:], in_=S[:]), reads=[Sb_], writes=[Sbfb])
    NB = 4
    qe = [sb("qe%d" % i, [128, 4, 128], BF16) for i in range(NB)]
    ke = [sb("ke%d" % i, [128, 4, 128], BF16) for i in range(NB)]
    kd = [sb("kd%d" % i, [128, 512], BF16) for i in range(NB)]
    vv = [sb("vv%d" % i, [128, 1024], BF16) for i in range(NB)]
    sg = [sb("sg%d" % i, [128, 1024], BF16) for i in range(NB)]
    am = [sb("am%d" % i, [128, 4, 128], BF16) for i in range(NB)]
    ob = [sb("ob%d" % i, [128, 1024], F32) for i in range(NB)]
    sgf = [sb("sgf%d" % i, [128, 1024], F32) for i in range(NB)]
    yo = [sb("yo%d" % i, [128, 8, 128], BF16) for i in range(NB)]
    st4, st4b = sb("st4", [128, 4, 6], F32)
    mv4, mv4b = sb("mv4", [128, 4, 2], F32)
    rs4, rs4b = sb("rs4", [128, 4], F32)
    qed = C.d("qeT%d" % L).rearrange("h p t -> p h t")
    ked = C.d("keT%d" % L).rearrange("h p t -> p h t")
    kdd = C.d("kd%d" % L)
    vvd = C.d("vv%d" % L)
    sgd = C.d("sg%d" % L)
    yod = C.d("yoT%d" % L).rearrange("c p t -> p c t")
    def o2_scan(ch):
            i = ch % NB
            cs = slice(ch * 128, (ch + 1) * 128)
            q_, qb = qe[i]; k_, kb = ke[i]; d_, db_ = kd[i]; v_, vb = vv[i]; s_, sb_ = sg[i]
            a_, ab = am[i]; o_, obb = ob[i]; sf, sfb = sgf[i]; y_, yb_ = yo[i]
            tgk = ch // 4
            ph.add("sp", lambda e, q_=q_, cs=cs: e.dma_start(out=q_[:], in_=qed[:, :, cs]), reads=[C.db("qeT%d" % L, tgk)], writes=[qb], dma=True)
            ph.add("sp", lambda e, k_=k_, cs=cs: e.dma_start(out=k_[:], in_=ked[:, :, cs]), reads=[C.db("keT%d" % L, tgk)], writes=[kb], dma=True)
            ph.add("sp", lambda e, d_=d_, cs=cs: e.dma_start(out=d_[:], in_=kdd[cs, :]), reads=[C.db("kd%d" % L, tgk)], writes=[db_], dma=True)
            ph.add("sp", lambda e, v_=v_, cs=cs: e.dma_start(out=v_[:], in_=vvd[cs, :]), reads=[C.db("vv%d" % L, tgk)], writes=[vb], dma=True)
            ph.add("sp", lambda e, s_=s_, cs=cs: e.dma_start(out=s_[:], in_=sgd[cs, :]), reads=[C.db("sg%d" % L, tgk)], writes=[sb_], dma=True)
            pa, pab = PS[0 + ch % 2]
            for h in range(4):
                ph.add("pe", lambda e, pa=pa, h=h, k_=k_, q_=q_: e.matmul(out=pa[:, h * 128:(h + 1) * 128], lhsT=k_[:, h, :], rhs=q_[:, h, :], start=True, stop=True),
                       reads=[kb, qb], writes=[pab])
            ph.add("dve", lambda e, pa=pa, a_=a_: e.tensor_tensor(out=a_[:], in0=pa[:].rearrange("p (h t) -> p h t", h=4), in1=tri4[:], op=ALU.mult),
                   reads=[pab, tri4b], writes=[ab])
            po = [PS[2 + 2 * (ch % 2)], PS[3 + 2 * (ch % 2)]]
            for h in range(4):
                pp, pb = po[h // 2]
                oc = (h % 2) * 256
                ph.add("pe", lambda e, pp=pp, h=h, oc=oc, q_=q_: e.matmul(out=pp[:, oc:oc + 256], lhsT=q_[:, h, :], rhs=Sbf[:, h * 256:(h + 1) * 256], start=True, stop=False),
                       reads=[qb, Sbfb], writes=[pb])
                ph.add("pe", lambda e, pp=pp, h=h, oc=oc, a_=a_, v_=v_: e.matmul(out=pp[:, oc:oc + 256], lhsT=a_[:, h, :], rhs=v_[:, h * 256:(h + 1) * 256], start=False, stop=True),
                       reads=[ab, vb], writes=[pb])
            for h in range(4):
                pss, pssb = PS[6 + (h % 2)]
                ph.add("pe", lambda e, pss=pss, h=h, d_=d_, v_=v_: e.matmul(out=pss[:, 0:256], lhsT=d_[:, h * 128:(h + 1) * 128], rhs=v_[:, h * 256:(h + 1) * 256], start=True, stop=True),
                       reads=[db_, vb], writes=[pssb])
                ph.add("dve", lambda e, pss=pss, h=h, ch=ch: e.scalar_tensor_tensor(
                    out=S[:, h * 256:(h + 1) * 256], in0=S[:, h * 256:(h + 1) * 256], scalar=ebl[:, h, ch:ch + 1], in1=pss[:, 0:256],
                    op0=ALU.mult, op1=ALU.add), reads=[pssb, eblb, Sb_], writes=[Sb_])
            ph.add("act", lambda e: e.copy(out=Sbf[:], in_=S[:]), reads=[Sb_], writes=[Sbfb])

    def o2_epi_a(ch):
            i = ch % NB
            cs = slice(ch * 128, (ch + 1) * 128)
            q_, qb = qe[i]; k_, kb = ke[i]; d_, db_ = kd[i]; v_, vb = vv[i]; s_, sb_ = sg[i]
            a_, ab = am[i]; o_, obb = ob[i]; sf, sfb = sgf[i]; y_, yb_ = yo[i]
            tgk = ch // 4
            po = [PS[2 + 2 * (ch % 2)], PS[3 + 2 * (ch % 2)]]
            for n in range(2):
                pp, pb = po[n]
                ph.add("act", lambda e, pp=pp, n=n, o_=o_: e.copy(out=o_[:, n * 512:(n + 1) * 512], in_=pp[:, :]), reads=[pb], writes=[obb])
            for h in range(4):
                ph.add("dve", lambda e, h=h, o_=o_: e.bn_stats(out=st4[:, h, :], in_=o_[:, h * 256:(h + 1) * 256]), reads=[obb], writes=[st4b])
            for h in range(4):
                ph.add("dve", lambda e, h=h: e.bn_aggr(out=mv4[:, h, :], in_=st4[:, h, :]), reads=[st4b], writes=[mv4b])
            ph.add("act", lambda e: e.activation(out=rs4[:], in_=mv4[:, :, 1], func=AF.Sqrt, bias=EPS, scale=1.0), reads=[mv4b], writes=[rs4b])
            ph.add("dve", lambda e: e.reciprocal(out=rs4[:], in_=rs4[:]), reads=[rs4b], writes=[rs4b])
            for h in range(4):
                ph.add("dve", lambda e, h=h, o_=o_: e.tensor_scalar(out=o_[:, h * 256:(h + 1) * 256], in0=o_[:, h * 256:(h + 1) * 256],
                                                                 scalar1=mv4[:, h, 0:1], scalar2=rs4[:, h:h + 1], op0=ALU.subtract, op1=ALU.mult),
                       reads=[obb, mv4b, rs4b], writes=[obb])
            ph.add("dve", lambda e, o_=o_: e.tensor_tensor(out=o_[:], in0=o_[:], in1=cg[:].rearrange("p h d -> p (h d)"), op=ALU.mult), reads=[obb, cgb], writes=[obb])
            ph.add("dve", lambda e, o_=o_: e.tensor_tensor(out=o_[:], in0=o_[:], in1=cb[:].rearrange("p h d -> p (h d)"), op=ALU.add), reads=[obb, cbb], writes=[obb])
            ph.add("act", lambda e, sf=sf, s_=s_: e.copy(out=sf[:], in_=s_[:]), reads=[sb_], writes=[sfb])
            ph.add("dve", lambda e, o_=o_, sf=sf: e.tensor_tensor(out=o_[:], in0=o_[:], in1=sf[:], op=ALU.mult), reads=[obb, sfb], writes=[obb])

    def o2_epi_b(ch):
            i = ch % NB
            cs = slice(ch * 128, (ch + 1) * 128)
            q_, qb = qe[i]; k_, kb = ke[i]; d_, db_ = kd[i]; v_, vb = vv[i]; s_, sb_ = sg[i]
            a_, ab = am[i]; o_, obb = ob[i]; sf, sfb = sgf[i]; y_, yb_ = yo[i]
            tgk = ch // 4
            po = [PS[2 + 2 * (ch % 2)], PS[3 + 2 * (ch % 2)]]
            for half in range(2):
                pp, pb = PS[6 + half]
                for j in range(4):
                    k = half * 4 + j
                    ph.add("pe", lambda e, pp=pp, j=j, k=k, o_=o_: e.transpose(out=pp[:, j * 128:(j + 1) * 128], in_=o_[:, k * 128:(k + 1) * 128], identity=ident[:]),
                           reads=[obb, identb], writes=[pb])
                ph.add("act", lambda e, pp=pp, half=half, y_=y_: e.copy(out=y_[:, half * 4:(half + 1) * 4, :], in_=pp[:].rearrange("p (k t) -> p k t", k=4)),
                       reads=[pb], writes=[yb_])
            ph.add("sp", lambda e, y_=y_, cs=cs: e.dma_start(out=yod[:, :, cs], in_=y_[:]), reads=[yb_], writes=[C.db("yoT%d" % L, ch)], dma=True)

    for ch in range(18):
        if ch < 16:
            o2_scan(ch)
        if 1 <= ch <= 16:
            o2_epi_a(ch - 1)
        if ch >= 2:
            o2_epi_b(ch - 2)
    ph.finish()


GROUPS = [[0, 1, 2, 3], [4, 5, 6, 7]]


def phase_X(C, gathers, selects=()):
    ph = Phase(C, "X%d" % C.nx)
    C.nx += 1
    for (s, src, d, dst) in gathers:
        ph.add("pool", lambda e, src=src, dst=dst: e.collective_compute(
            "AllGather", ALU.bypass, replica_groups=GROUPS, ins=[src], outs=[dst]),
            reads=[C.db(s)], writes=[C.db(d)], dma=True, cc=True)
    for (gn, dn, n) in selects:
        def sel(e, gn=gn, dn=dn, n=n):
            rank = my_rank(ph, e)
            q = n // 16
            return e.dma_start(out=bass.AP(C.d(dn).tensor, 0, [[q, 16], [1, q]]),
                               in_=bass.AP(C.d(gn).tensor, rank * n, [[q, 16], [1, q]]))
        ph.add("sp", sel, reads=[C.db(g[2]) for g in gathers], writes=[C.db(dn)], dma=True)
    ph.finish()


def flat2(ap):
    nd = len(ap.shape)
    if nd == 2:
        return ap
    names = " ".join("a%d" % i for i in range(nd))
    rest = " ".join("a%d" % i for i in range(1, nd))
    return ap.rearrange("%s -> a0 (%s)" % (names, rest))


def bf(*s):
    return (tuple(s), BF16)


def f32(*s):
    return (tuple(s), F32)


def dram_shapes():
    sh = {"dbgr0": f32(128, 1024), "dbgr4": f32(128, 1024), "dbgyt": bf(128, 8, 512), "pos": ((1, TOK), I32), "invf": f32(128, 2), "oh": f32(128, 4)}
    for L in range(5):
        sh["x%d" % L] = f32(TOK, D)
    for L in range(4):
        for nm in ("ln1g", "ln1b", "ln2g", "ln2b"):
            sh["%s%d" % (nm, L)] = f32(1, D)
        sh["wff1%d" % L] = f32(D, DFF)
        sh["wff2%d" % L] = f32(DFF, D)
        sh["wout%d" % L] = f32(D, D)
        if L % 2 == 0:
            sh["win%d" % L] = f32(D, 1696)
            sh["wkr%d" % L] = f32(D, 192)
            sh["wuq%d" % L] = f32(384, 768)
            sh["wuqr%d" % L] = f32(384, 768)
            sh["wuk%d" % L] = f32(256, 512)
            sh["wuv%d" % L] = f32(256, 512)
            sh["gq%d" % L] = f32(128, 3)
            sh["gkv%d" % L] = f32(128, 2)
            sh["wsT%d" % L] = f32(128, 4, 128)
            sh["bs%d" % L] = f32(1, 512)
            sh["alng%d" % L] = f32(1, 512)
            sh["alnb%d" % L] = f32(1, 512)
            sh["qs%d" % L] = bf(8, 96, TOK)
            sh["ks%d" % L] = bf(8, 96, TOK)
            sh["vs%d" % L] = bf(4, TOK, 128)
            sh["qr%d" % L] = bf(4, 2, 96, TOK)
            sh["kr%d" % L] = bf(4, 2, 96, TOK)
            sh["vr%d" % L] = bf(4, TOK, 128)
            sh["yaT%d" % L] = bf(4, 128, TOK)
            sh["ybs%d" % L] = bf(4, 2, 64, TOK)
            sh["ybr%d" % L] = bf(4, 128, TOK)
            sh["qall%d" % L] = bf(4, 4, 2, 96, TOK)
            sh["kall%d" % L] = bf(4, 4, 2, 96, TOK)
            sh["vall%d" % L] = bf(4, 4, TOK, 128)
            sh["yball%d" % L] = bf(4, 4, 128, TOK)
            sh["qst%d" % L] = bf(4, 8, 96, 512)
            sh["kst%d" % L] = bf(4, 8, 96, 512)
            for tg in range(4):
                sh["qallt%d_%d" % (tg, L)] = bf(4, 8, 96, 512)
                sh["kallt%d_%d" % (tg, L)] = bf(4, 8, 96, 512)
        else:
            sh["win%d" % L] = f32(D, 3088)
            sh["wgate%d" % L] = f32(16, 512)
            sh["bgate%d" % L] = f32(1, 512)
            sh["clng%d" % L] = f32(1, 256)
            sh["clnb%d" % L] = f32(1, 256)
            sh["qeT%d" % L] = bf(4, 128, TOK)
            sh["keT%d" % L] = bf(4, 128, TOK)
            sh["kd%d" % L] = bf(TOK, 512)
            sh["vv%d" % L] = bf(TOK, 1024)
            sh["sg%d" % L] = bf(TOK, 1024)
            sh["ebl%d" % L] = f32(128, 64)
            sh["sloc%d" % L] = f32(128, 1028)
            sh["sall%d" % L] = f32(4, 128, 1028)
            sh["yoT%d" % L] = bf(8, 128, TOK)
    return sh


QKT_NAMES = ["%sallt%d_" % (n, tg) for n in ("q", "k") for tg in range(4)]


def phase_io(kind, L):
    s = lambda *names: ["%s%d" % (n, L) for n in names]
    ffn = s("wout", "ln1g", "ln1b", "ln2g", "ln2b", "wff1", "wff2")
    if kind == "E1":
        return (["x%d" % L, "pos", "invf"] + s("win", "wkr", "wuq", "wuqr", "wuk", "wuv", "gq", "gkv", "wsT", "bs", "alng", "alnb"),
                s("qs", "ks", "vs", "yaT"))
    if kind == "E1M":
        return (["x%d" % L, "pos", "invf"] + s("win", "wkr", "wuq", "wuqr", "wuk", "wuv", "gq", "gkv"),
                s("qst", "kst", "vs", *QKT_NAMES) if OVERLAP_QK else s("qs", "ks", "vs"))
    if kind == "E1G":
        if OVERLAP_QK:
            return (["x%d" % L] + s("win", "wsT", "bs", "alng", "alnb", "vs"), s("yaT", "vall"))
        return (["x%d" % L] + s("win", "wsT", "bs", "alng", "alnb", "qs", "ks", "vs"), s("yaT", "qall", "kall", "vall"))
    if kind == "XE1":
        return (s("qs", "ks", "vs"), s("qall", "kall", "vall") if DIRECT else s("qr", "kr", "vr"))
    if kind == "E2A":
        return ((s("vall", *QKT_NAMES) if OVERLAP_QK else s("qall", "kall", "vall")) if DIRECT else s("qr", "kr", "vr"),
                s("ybs") + (s("yball") if OVERLAP_XE2 else []))
    if kind == "XE2":
        return (s("ybs"), s("yball") if DIRECT else s("ybr"))
    if kind == "POSTE":
        return (["x%d" % L] + s("yaT", "yball" if DIRECT else "ybr") + ffn, ["x%d" % (L + 1)])
    if kind == "O1":
        return (["x%d" % L] + s("win", "wgate", "bgate"), s("qeT", "keT", "kd", "vv", "sg", "ebl", "sloc") + (s("sall") if OVERLAP_XO else []))
    if kind == "XO":
        return (s("sloc"), s("sall"))
    if kind == "O2":
        return (["oh"] + s("qeT", "keT", "kd", "vv", "sg", "ebl", "sall", "clng", "clnb"), s("yoT"))
    if kind == "POSTO":
        return (["x%d" % L] + s("yoT") + ffn, ["x%d" % (L + 1)])
    raise ValueError(kind)


def all_phases():
    ph = []
    for L in range(4):
        if L % 2 == 0:
            if OVERLAP_XE1:
                ph += [("E1M", L), ("E1G", L), ("E2A", L)] + ([] if OVERLAP_XE2 else [("XE2", L)]) + [("POSTE", L)]
            else:
                ph += [("E1", L), ("XE1", L), ("E2A", L), ("XE2", L), ("POSTE", L)]
        else:
            ph += [("O1", L)] + ([] if OVERLAP_XO else [("XO", L)]) + [("O2", L), ("POSTO", L)]
    return ph


def xe1_gathers(C, L):
    g = []
    for nm in (() if OVERLAP_QK else ("q", "k")):
        sn, an = "%ss%d" % (nm, L), "%sall%d" % (nm, L)
        for j in range(4):
            g.append((sn, C.d(sn)[2 * j:2 * j + 2].rearrange("h p t -> h (p t)"),
                      an, C.d(an)[j].rearrange("r h p t -> (r h) (p t)")))
    for j in range(4):
        g.append(("vs%d" % L, C.d("vs%d" % L)[j], "vall%d" % L, C.d("vall%d" % L)[j].rearrange("r t c -> (r t) c")))
    return g


def build_program(phases, later_reads):
    shapes = dram_shapes()
    written = set()
    ext_in, ext_out = set(), set()
    for (k, L) in phases:
        r, w = phase_io(k, L)
        for n in r:
            if n not in written:
                ext_in.add(n)
        written.update(w)
    for n in written:
        if n in later_reads or n == "x4":
            ext_out.add(n)
    import os as _os2
    if _os2.environ.get("DBGPOST"):
        ext_out.update(["dbgyt", "dbgr0", "dbgr4"])
    nc = bass.Bass("TRN2", target_bir_lowering=False)
    C = Ctx(nc, ext_in, ext_out, shapes)
    C.nx = 0
    for (k, L) in phases:
        if k == "E1":
            phase_E1(C, L)
        elif k == "E2A":
            phase_E2A(C, L)
        elif k == "POSTE":
            phase_POST(C, L, [("yaT%d" % L, 4), (("yball%d" if DIRECT else "ybr%d") % L, 4)])
        elif k == "POSTO":
            phase_POST(C, L, [("yoT%d" % L, 8)])
        elif k == "O1":
            phase_O1(C, L)
        elif k == "O2":
            phase_O2(C, L)
        elif k == "E1M":
            phase_E1(C, L, "mla")
        elif k == "E1G":
            phase_E1(C, L, "gmlp", xe1_gathers(C, L))
        elif k == "XE1":
            g = xe1_gathers(C, L)
            phase_X(C, g, [] if DIRECT else [("qall%d" % L, "qr%d" % L, 8 * 96 * TOK), ("kall%d" % L, "kr%d" % L, 8 * 96 * TOK),
                                             ("vall%d" % L, "vr%d" % L, 4 * TOK * 128)])
        elif k == "XE2":
            g = []
            for j in range(4):
                g.append(("ybs%d" % L, C.d("ybs%d" % L)[j].rearrange("h p t -> (h p) t"),
                          "yball%d" % L, C.d("yball%d" % L)[j].rearrange("r p t -> (r p) t")))
            phase_X(C, g, [] if DIRECT else [("yball%d" % L, "ybr%d" % L, 4 * 128 * TOK)])
        elif k == "XO":
            phase_X(C, [("sloc%d" % L, C.d("sloc%d" % L), "sall%d" % L, C.d("sall%d" % L).rearrange("r p n -> (r p) n"))])
    C.st.close()
    return nc, sorted(ext_in), sorted(ext_out)


def prep_weights(inp):
    W = {}
    f = lambda a: np.ascontiguousarray(a, dtype=np.float32)
    invf = np.zeros((128, 2), np.float32)
    fr = (10000.0 ** (-np.arange(16, dtype=np.float32) / 16.0)).astype(np.float32)
    invf[64:80, 0] = fr
    invf[80:96, 0] = fr
    invf[:, 1] = 1.0
    invf[64:80, 1] = -1.0
    W["invf"] = invf
    perm = np.concatenate([np.arange(16, 32), np.arange(0, 16)])
    for L in range(4):
        j = L // 2
        for nm in ("ln1_g", "ln1_b", "ln2_g", "ln2_b"):
            W["%s%d" % (nm.replace("_", ""), L)] = f(inp[nm][L][None, :])
        W["wff1%d" % L] = f(inp["w_ff1"][L])
        W["wff2%d" % L] = f(inp["w_ff2"][L])
        if L % 2 == 0:
            win = inp["w_in_even"][j]
            W["win%d" % L] = f(win)
            wkr = np.zeros((D, 192), np.float32)
            wkr[:, 64:96] = win[:, 1664:1696]
            wkr[:, 96 + 64:192] = win[:, 1664:1696][:, perm]
            W["wkr%d" % L] = wkr
            wuq = inp["b_w_uq"][j]
            W["wuq%d" % L] = f(wuq)
            wuqr = np.zeros((384, 768), np.float32)
            for h in range(8):
                wuqr[:, h * 96 + 64:(h + 1) * 96] = wuq[:, h * 96 + 64:(h + 1) * 96][:, perm]
            W["wuqr%d" % L] = wuqr
            wukv = inp["b_w_ukv"][j].reshape(256, 8, 128)
            W["wuk%d" % L] = f(wukv[:, :, :64].reshape(256, 512))
            W["wuv%d" % L] = f(wukv[:, :, 64:].reshape(256, 512))
            W["gq%d" % L] = f(inp["b_q_norm"][j].reshape(3, 128).T)
            W["gkv%d" % L] = f(inp["b_kv_norm"][j].reshape(2, 128).T)
            W["wsT%d" % L] = f(np.transpose(inp["a_w_s"][j], (2, 0, 1)))
            W["bs%d" % L] = f(inp["a_b_s"][j].reshape(1, 512))
            W["alng%d" % L] = f(inp["a_ln_g"][j].reshape(1, 512))
            W["alnb%d" % L] = f(inp["a_ln_b"][j].reshape(1, 512))
            W["wout%d" % L] = f(inp["w_out_even"][j])
        else:
            W["win%d" % L] = f(inp["w_in_odd"][j])
            W["wgate%d" % L] = f(inp["c_w_gate"][j])
            W["bgate%d" % L] = f(inp["c_b_gate"][j][None, :])
            W["clng%d" % L] = f(inp["c_ln_g"][j][None, :])
            W["clnb%d" % L] = f(inp["c_ln_b"][j][None, :])
            W["wout%d" % L] = f(inp["w_out_odd"][j])
    return W


def host_exchange(kind, L, state):
    for g in GROUPS:
        if kind == "XE1":
            for i, ci in enumerate(g):
                state[ci]["qr%d" % L] = np.stack([state[cj]["qs%d" % L][2 * i:2 * i + 2] for cj in g])
                state[ci]["kr%d" % L] = np.stack([state[cj]["ks%d" % L][2 * i:2 * i + 2] for cj in g])
                state[ci]["vr%d" % L] = np.stack([state[cj]["vs%d" % L][i] for cj in g])
        elif kind == "XE2":
            for j, cj in enumerate(g):
                state[cj]["ybr%d" % L] = np.stack([state[ci]["ybs%d" % L][j].reshape(128, TOK) for ci in g])
        elif kind == "XO":
            sall = np.stack([state[c]["sloc%d" % L] for c in g])
            for c in g:
                state[c]["sall%d" % L] = sall


_PROG_CACHE = {}


def kernel(**inp):
    inp = {k: np.asarray(v) for k, v in inp.items()}
    W = prep_weights(inp)
    x = inp["x"].astype(np.float32)
    pos = inp["positions"].astype(np.int32)
    state = []
    for c in range(NCORES):
        b, q = c // 4, c % 4
        oh = np.zeros((128, 4), np.float32)
        oh[:, q] = 1.0
        state.append({"x0": np.ascontiguousarray(x[b, q * TOK:(q + 1) * TOK]),
                      "pos": np.ascontiguousarray(pos[b, q * TOK:(q + 1) * TOK][None, :]), "oh": oh})
    phases = all_phases()
    if FUSED:
        launches = [phases]
    else:
        launches, cur = [], []
        for p in phases:
            if p[0].startswith("X"):
                launches.append(cur)
                launches.append([p])
                cur = []
            else:
                cur.append(p)
        launches.append(cur)
    for li, lp in enumerate(launches):
        if not FUSED and lp[0][0].startswith("X"):
            host_exchange(lp[0][0], lp[0][1], state)
            continue
        later = set()
        for lq in launches[li + 1:]:
            for (k, L) in lq:
                later.update(phase_io(k, L)[0])
        key = tuple(lp)
        if key not in _PROG_CACHE:
            _PROG_CACHE[key] = build_program(lp, later)
        nc, ext_in, ext_out = _PROG_CACHE[key]
        in_maps = []
        for c in range(NCORES):
            m = {}
            for n in ext_in:
                m[n] = state[c][n] if n in state[c] else W[n]
            in_maps.append(m)
        res = run_bass_kernel_spmd(nc, in_maps, core_ids=list(range(NCORES)))
        for c in range(NCORES):
            for n in ext_out:
                state[c][n] = np.asarray(res.results[c][n])
    out = np.zeros((2, 8192, D), np.float32)
    for c in range(NCORES):
        b, q = c // 4, c % 4
        out[b, q * TOK:(q + 1) * TOK] = state[c]["x4"]
    return out
```

```python
import contextlib
import math
import numpy as np
import ml_dtypes
import concourse.bass as bass
import concourse.mybir as mybir
from concourse.bass_utils import run_bass_kernel_spmd

F32 = mybir.dt.float32
BF16 = mybir.dt.bfloat16
I32 = mybir.dt.int32
AF = mybir.ActivationFunctionType
ALU = mybir.AluOpType

NCORES = 8
TOK = 2048
NTL = 16
D = 1024
DFF = 4096
ALPHA = 8.0 ** 0.25
EPS = 1e-5
PI = math.pi
ENGS = ("pe", "act", "dve", "pool", "sp")
NDS = 8
NO_BARRIER = False
FUSED = True
OVERLAP_QK = True
OVERLAP_XE2 = True
OVERLAP_XE1 = True
DIRECT = True
CC_INC = 16


class Buf:
    __slots__ = ("w", "r")

    def __init__(self):
        self.w = None
        self.r = []


class Op:
    __slots__ = ("eng", "fn", "deps", "dma", "needed", "done", "cc")

    def __init__(self, eng, fn, dma):
        self.eng = eng
        self.fn = fn
        self.deps = []
        self.dma = dma
        self.needed = False
        self.done = None
        self.cc = False


class Phase:
    def __init__(self, C, name):
        self.C = C
        self.nc = C.nc
        self.name = name
        self.ops = {e: [] for e in ENGS}
        self.st = contextlib.ExitStack()
        self.n = 0

    def sb(self, name, shape, dt):
        self.n += 1
        t = self.st.enter_context(self.nc.sbuf_tensor("%s_%s" % (self.name, name), shape, dt))
        return t, Buf()

    def add(self, eng, fn, reads=(), writes=(), dma=False, cc=False):
        op = Op(eng, fn, dma)
        op.cc = cc
        deps = {}
        for b in reads:
            if b.w is not None:
                deps[id(b.w)] = b.w
        for b in writes:
            if b.w is not None:
                deps[id(b.w)] = b.w
            for r in b.r:
                deps[id(r)] = r
        for d in deps.values():
            need = True
            if d.eng == eng and not d.dma and not dma and eng == "pe":
                need = False
            if need:
                d.needed = True
                op.deps.append(d)
        for b in reads:
            b.r.append(op)
        for b in writes:
            b.w = op
            b.r = []
        self.ops[eng].append(op)
        return op

    def finish(self):
        nc = self.nc
        with contextlib.ExitStack() as st:
            esem = self.C.esem
            dsem = self.C.dsem
            last_dma = {}
            for e in ENGS:
                c = self.C.ecount[e]
                nd = self.C.ndma[e]
                last = None
                for op in self.ops[e]:
                    if not op.dma:
                        last = op
                if last is not None:
                    last.needed = True
                for op in self.ops[e]:
                    if op.cc:
                        self.C.ccn += 1
                        op.done = (self.C.ccsem, self.C.ccn, ("cc",))
                        last_dma[(e, "cc")] = op
                    elif op.dma:
                        k = nd % NDS
                        op.done = (dsem[e][k], 16 * (nd // NDS + 1), ("d", e, k))
                        last_dma[(e, k)] = op
                        nd += 1
                    elif op.needed:
                        c += 1
                        op.done = (esem[e], c, ("e", e))
                self.C.ecount[e] = c
                self.C.ndma[e] = nd
            block = st.enter_context(nc.Block())

            def gen(e):
                def body(eng):
                    waited = {}

                    def wait(d):
                        sem, val, key = d.done
                        if waited.get(key, 0) >= val:
                            return
                        eng.wait_ge(sem, val)
                        waited[key] = val

                    lastc = None
                    for op in self.ops[e]:
                        for d in op.deps:
                            wait(d)
                        if op.cc:
                            sem, val, key = op.done
                            op.fn(eng).then_inc(sem, 1)
                        elif op.dma:
                            sem, val, key = op.done
                            if val > 16 and waited.get(key, 0) < val - 16:
                                eng.wait_ge(sem, val - 16)
                                waited[key] = val - 16
                            op.fn(eng).then_inc(sem, 16)
                        else:
                            ins = op.fn(eng)
                            if op.needed:
                                ins.then_inc(op.done[0], 1)
                            lastc = op
                    for (ee, k), d in last_dma.items():
                        if ee == e:
                            wait(d)
                    if lastc is not None:
                        wait(lastc)
                return body

            block.tensor(gen("pe"))
            block.scalar(gen("act"))
            block.vector(gen("dve"))
            block.gpsimd(gen("pool"))
            block.sync(gen("sp"))
        self.st.close()
        self.C.dbuf.clear()
        for (_, b) in self.C.ps:
            b.w = None
            b.r = []
        if not NO_BARRIER:
            nc.all_engine_barrier()


class Ctx:
    def __init__(self, nc, ext_in, ext_out, shapes):
        self.nc = nc
        self.ext_in = ext_in
        self.ext_out = ext_out
        self.shapes = shapes
        self.dram = {}
        self.dbuf = {}
        self.st = contextlib.ExitStack()
        self.ps = []
        for i in range(8):
            t = self.st.enter_context(nc.psum_tensor("ps%d" % i, [128, 512], F32))
            self.ps.append((t, Buf()))
        self.nx = 0
        self.ccsem = self.st.enter_context(nc.semaphore("ccsem"))
        self.ccn = 0
        self.esem = {e: self.st.enter_context(nc.semaphore("s_%s" % e)) for e in ENGS}
        self.dsem = {e: [self.st.enter_context(nc.semaphore("d_%s%d" % (e, i))) for i in range(NDS)] for e in ("sp", "pool")}
        self.ecount = {e: 0 for e in ENGS}
        self.ndma = {e: 0 for e in ENGS}
        ph = Phase(self, "Z")
        for (t, b) in self.ps:
            ph.add("dve", lambda e, t=t: e.memset(t[:], 0.0), writes=[b])
        ph.finish()

    def d(self, name):
        if name not in self.dram:
            shape, dt = self.shapes[name]
            kind = "Internal"
            if name in self.ext_in:
                kind = "ExternalInput"
            elif name in self.ext_out:
                kind = "ExternalOutput"
            self.dram[name] = self.nc.dram_tensor(name, list(shape), dt, kind=kind).ap()
        return self.dram[name]

    def db(self, name, key=0):
        k = (name, key)
        if k not in self.dbuf:
            self.dbuf[k] = Buf()
        return self.dbuf[k]


def my_rank(ph, e):
    C = ph.C
    if getattr(C, "_rank", None) is None:
        C._rank = e.snap(e.partition_id() % 4, min_val=0, max_val=3)
    return C._rank


def make_ident(ph, ident, identb):
    ph.add("pool", lambda e: e.memset(ident[:], 1.0), writes=[identb])
    ph.add("pool", lambda e: e.affine_select(out=ident[:], in_=ident[:], pattern=[[-1, 128]],
                                            compare_op=ALU.is_equal, fill=0.0, base=0, channel_multiplier=1),
           reads=[identb], writes=[identb])


def load_xT(ph, C, xname, tile0, ntiles, xT, xTb, xst, ident, identb, psA, psB, col0=0):
    x = C.d(xname)
    for i in range(ntiles):
        t = tile0 + i
        xs, xsb = xst[i % len(xst)]
        ph.add("sp", lambda e, xs=xs, t=t: e.dma_start(out=xs[:], in_=x[t * 128:(t + 1) * 128, :]),
               reads=[C.db(xname, t)], writes=[xsb], dma=True)
        for half in range(2):
            pp, pb = (psA, psB)[half]
            for j in range(4):
                k = half * 4 + j
                ph.add("pe", lambda e, pp=pp, j=j, k=k, xs=xs: e.transpose(
                    out=pp[:, j * 128:(j + 1) * 128], in_=xs[:, k * 128:(k + 1) * 128], identity=ident[:]),
                    reads=[xsb, identb], writes=[pb])
            c0 = col0 + i * 128
            ph.add("act" if half == 0 else "dve",
                   (lambda e, pp=pp, half=half, c0=c0: e.copy(
                       out=xT[:, half * 4:(half + 1) * 4, c0:c0 + 128],
                       in_=pp[:].rearrange("p (k t) -> p k t", k=4))) if half == 0 else
                   (lambda e, pp=pp, half=half, c0=c0: e.tensor_copy(
                       out=xT[:, half * 4:(half + 1) * 4, c0:c0 + 128],
                       in_=pp[:].rearrange("p (k t) -> p k t", k=4))),
                   reads=[pb], writes=[xTb])


def load_w_bf16(ph, C, name, dst, dstb, rows, c0, c1):
    w = C.d(name)
    if rows >= 128:
        src = w.rearrange("(k p) n -> p k n", p=128)[:, :, c0:c1]
    else:
        src = w[:, c0:c1]
    ph.add("pool", lambda e: e.dma_start(out=dst, in_=src), writes=[dstb], dma=True)


def phase_E1(C, L, part="all", gathers=None):
    MLA = part in ("all", "mla")
    GM = part in ("all", "gmlp")
    ph = Phase(C, "E1%s_%d" % (part[0], L))
    nc = C.nc
    sb = ph.sb
    xname = "x%d" % L
    PS = C.ps
    ident, identb = sb("ident", [128, 128], F32)
    make_ident(ph, ident, identb)
    Wu, Wub = sb("Wu", [128, 8, 512], BF16)
    Wv, Wvb = sb("Wv", [128, 8, 512], BF16)
    Wcq, Wcqb = sb("Wcq", [128, 8, 384], BF16)
    Wckv, Wckvb = sb("Wckv", [128, 8, 256], BF16)
    Wkr, Wkrb = sb("Wkr", [128, 8, 192], BF16)
    if MLA:
        stg, stgb = sb("stg", [128, 3, 768], F32)
        gq, gqb = sb("gq", [128, 3], F32)
        gkv, gkvb = sb("gkv", [128, 2], F32)
        Wuq, Wuqb = sb("Wuq", [128, 3, 768], BF16)
        Wuqr, Wuqrb = sb("Wuqr", [128, 3, 768], BF16)
        Wuk, Wukb = sb("Wuk", [128, 2, 512], BF16)
        Wuv, Wuvb = sb("Wuv", [128, 2, 512], BF16)
        ph.add("sp", lambda e: e.dma_start(out=gq[:], in_=C.d("gq%d" % L)), writes=[gqb], dma=True)
        ph.add("sp", lambda e: e.dma_start(out=gkv[:], in_=C.d("gkv%d" % L)), writes=[gkvb], dma=True)
        qscale = 96.0 ** -0.5
        for (nm, dst, dstb, nk, ncol, gg, ggb, sc) in (
                ("wuq%d" % L, Wuq, Wuqb, 3, 768, gq, gqb, qscale),
                ("wuqr%d" % L, Wuqr, Wuqrb, 3, 768, gq, gqb, qscale),
                ("wuk%d" % L, Wuk, Wukb, 2, 512, gkv, gkvb, 1.0),
                ("wuv%d" % L, Wuv, Wuvb, 2, 512, gkv, gkvb, 1.0)):
            ph.add("sp", lambda e, nm=nm, nk=nk, ncol=ncol: e.dma_start(
                out=stg[:, 0:nk, 0:ncol], in_=C.d(nm).rearrange("(k p) n -> p k n", p=128)),
                writes=[stgb], dma=True)
            for k in range(nk):
                ph.add("dve", lambda e, dst=dst, k=k, ncol=ncol, gg=gg, sc=sc: e.tensor_scalar(
                    out=dst[:, k, :], in0=stg[:, k, 0:ncol], scalar1=gg[:, k:k + 1], scalar2=sc,
                    op0=ALU.mult, op1=ALU.mult), reads=[stgb, ggb], writes=[dstb])
    if GM:
        wsf, wsfb = sb("wsf", [128, 4, 128], F32)
        wsT, wsTb = sb("wsT", [128, 4, 128], BF16)
        ph.add("sp", lambda e: e.dma_start(out=wsf[:], in_=C.d("wsT%d" % L)), writes=[wsfb], dma=True)
        for g in range(4):
            ph.add("pool", lambda e, g=g: e.affine_select(out=wsf[:, g, :], in_=wsf[:, g, :], pattern=[[1, 128]],
                                                         compare_op=ALU.is_ge, fill=0.0, base=0, channel_multiplier=-1),
                   reads=[wsfb], writes=[wsfb])
        ph.add("pool", lambda e: e.tensor_copy(out=wsT[:], in_=wsf[:]), reads=[wsfb], writes=[wsTb])
        bs, bsb = sb("bs", [1, 512], F32)
        ones1, ones1b = sb("ones1", [1, 128], F32)
        ph.add("sp", lambda e: e.dma_start(out=bs[:], in_=C.d("bs%d" % L)), writes=[bsb], dma=True)
        ph.add("pool", lambda e: e.memset(ones1[:], 1.0), writes=[ones1b])
        lng, lngb = sb("lng", [128, 512], F32)
        lnb, lnbb = sb("lnb", [128, 512], F32)
        ph.add("sp", lambda e: e.dma_start(out=lng[:], in_=C.d("alng%d" % L).partition_broadcast(128)), writes=[lngb], dma=True)
        ph.add("sp", lambda e: e.dma_start(out=lnb[:], in_=C.d("alnb%d" % L).partition_broadcast(128)), writes=[lnbb], dma=True)
    if MLA:
        onesq, onesqb = sb("onesq", [128, 128], F32)
        oneskv, oneskvb = sb("oneskv", [128, 128], F32)
        ph.add("pool", lambda e: e.memset(onesq[:], 1.0 / 384.0), writes=[onesqb])
        ph.add("pool", lambda e: e.memset(oneskv[:], 1.0 / 256.0), writes=[oneskvb])
    if GM:
        load_w_bf16(ph, C, "win%d" % L, Wu[:], Wub, 1024, 0, 512)
        load_w_bf16(ph, C, "win%d" % L, Wv[:], Wvb, 1024, 512, 1024)
    if MLA:
        load_w_bf16(ph, C, "win%d" % L, Wcq[:], Wcqb, 1024, 1024, 1408)
        load_w_bf16(ph, C, "win%d" % L, Wckv[:], Wckvb, 1024, 1408, 1664)
        load_w_bf16(ph, C, "wkr%d" % L, Wkr[:], Wkrb, 1024, 0, 192)
    if gathers:
        for (s_, src_, d_, dst_) in gathers:
            ph.add("pool", lambda e, src_=src_, dst_=dst_: e.collective_compute(
                "AllGather", ALU.bypass, replica_groups=GROUPS, ins=[src_], outs=[dst_]),
                reads=[C.db(s_)], writes=[C.db(d_)], dma=True, cc=True)
    if MLA:
        cosT, cosTb = sb("cosT", [128, TOK], F32)
        sinT, sinTb = sb("sinT", [128, TOK], F32)
        posi, posib = sb("posi", [128, TOK], I32)
        ang, angb = sb("ang", [128, TOK], F32)
        tA, tAb = sb("tA", [128, TOK], F32)
        tB, tBb = sb("tB", [128, TOK], F32)
        ki, kib = sb("ki", [128, TOK], I32)
        invf, invfb = sb("invf", [128, 2], F32)
        ph.add("sp", lambda e: e.dma_start(out=posi[:], in_=C.d("pos").partition_broadcast(128)), writes=[posib], dma=True)
        ph.add("sp", lambda e: e.dma_start(out=invf[:], in_=C.d("invf")), writes=[invfb], dma=True)
        ph.add("dve", lambda e: e.tensor_copy(out=ang[:], in_=posi[:]), reads=[posib], writes=[angb])
        ph.add("dve", lambda e: e.tensor_scalar(out=ang[:], in0=ang[:], scalar1=invf[:, 0:1], scalar2=None, op0=ALU.mult),
               reads=[angb, invfb], writes=[angb])
        C1 = 6.28125
        C2 = 2.0 * PI - C1
        for (dstT, dstTb, shift) in ((sinT, sinTb, 0.0), (cosT, cosTb, PI / 2)):
            ph.add("dve", lambda e, shift=shift: e.tensor_scalar(out=tA[:], in0=ang[:], scalar1=shift, scalar2=None, op0=ALU.add),
                   reads=[angb], writes=[tAb])
            ph.add("dve", lambda e: e.tensor_scalar(out=tB[:], in0=tA[:], scalar1=1.0 / (2 * PI), scalar2=None, op0=ALU.mult),
                   reads=[tAb], writes=[tBb])
            ph.add("dve", lambda e: e.tensor_copy(out=ki[:], in_=tB[:]), reads=[tBb], writes=[kib])
            ph.add("dve", lambda e: e.tensor_copy(out=tB[:], in_=ki[:]), reads=[kib], writes=[tBb])
            ph.add("dve", lambda e: e.scalar_tensor_tensor(out=tA[:], in0=tB[:], scalar=-C1, in1=tA[:], op0=ALU.mult, op1=ALU.add),
                   reads=[tAb, tBb], writes=[tAb])
            ph.add("dve", lambda e: e.scalar_tensor_tensor(out=tA[:], in0=tB[:], scalar=-C2, in1=tA[:], op0=ALU.mult, op1=ALU.add),
                   reads=[tAb, tBb], writes=[tAb])
            ph.add("dve", lambda e: e.tensor_single_scalar(out=tB[:], in_=tA[:], scalar=PI, op=ALU.is_gt), reads=[tAb], writes=[tBb])
            ph.add("dve", lambda e: e.scalar_tensor_tensor(out=tA[:], in0=tB[:], scalar=-2 * PI, in1=tA[:], op0=ALU.mult, op1=ALU.add),
                   reads=[tAb, tBb], writes=[tAb])
            ph.add("dve", lambda e: e.tensor_single_scalar(out=tB[:], in_=tA[:], scalar=-PI, op=ALU.is_lt), reads=[tAb], writes=[tBb])
            ph.add("dve", lambda e: e.scalar_tensor_tensor(out=tA[:], in0=tB[:], scalar=2 * PI, in1=tA[:], op0=ALU.mult, op1=ALU.add),
                   reads=[tAb, tBb], writes=[tAb])
            ph.add("dve", lambda e: e.tensor_scalar(out=tA[:], in0=tA[:], scalar1=-PI, scalar2=PI, op0=ALU.max, op1=ALU.min),
                   reads=[tAb], writes=[tAb])
            ph.add("act", lambda e, dstT=dstT: e.activation(out=dstT[:], in_=tA[:], func=AF.Sin), reads=[tAb], writes=[dstTb])
        ph.add("dve", lambda e: e.tensor_scalar(out=sinT[:], in0=sinT[:], scalar1=invf[:, 1:2], scalar2=None, op0=ALU.mult),
               reads=[sinTb, invfb], writes=[sinTb])

    xst = [sb("xst%d" % i, [128, D], F32) for i in range(2)]
    xT, xTb = sb("xT", [128, 8, 512], BF16)
    uT, uTb = sb("uT", [128, 4, 512], BF16)
    vgs = [sb("vg%d" % i, [128, 512], F32) for i in range(4)]
    vlns = [sb("vln%d" % i, [128, 512], BF16) for i in range(4)]
    vscr = []
    for i in range(4):
        a_, ab_ = sb("st4_%d" % i, [128, 4, 6], F32)
        b_, bb_ = sb("mv4_%d" % i, [128, 4, 2], F32)
        c_, cb_ = sb("rs4_%d" % i, [128, 4], F32)
        vscr.append((a_, ab_, b_, bb_, c_, cb_))
    yaT, yaTb = sb("yaT", [128, 4, 512], BF16)
    cqT, cqTb = sb("cqT", [128, 3, 512], BF16)
    sqq, sqqb = sb("sqq", [128, 3, 512], F32)
    ckvT, ckvTb = sb("ckvT", [128, 2, 512], BF16)
    sqkv, sqkvb = sb("sqkv", [128, 2, 512], F32)
    rq, rqb = sb("rq", [128, 512], F32)
    rkv, rkvb = sb("rkv", [128, 512], F32)
    rkvt, rkvtb = sb("rkvt", [128, 4], F32)
    Cq, Cqb = sb("Cq", [128, 512], F32)
    Sq, Sqb = sb("Sq", [128, 512], F32)
    t1, t1b = sb("t1", [128, 512], F32)
    t2, t2b = sb("t2", [128, 512], F32)
    krp, krpb = sb("krp", [128, 512], BF16)
    QT, QTb = sb("QT", [96, 8, 512], BF16)
    KT, KTb = sb("KT", [96, 8, 512], BF16)
    Vt, Vtb = sb("Vt", [128, 4, 512], BF16)

    QKT = OVERLAP_QK and part == "mla"
    if QKT:
        qst = C.d("qst%d" % L)
        kst = C.d("kst%d" % L)
    else:
        qs = C.d("qs%d" % L).rearrange("h p t -> p h t")
        ks = C.d("ks%d" % L).rearrange("h p t -> p h t")
    vs = C.d("vs%d" % L)
    yad = C.d("yaT%d" % L).rearrange("c p t -> p c t")

    for tg in range(4):
        load_xT(ph, C, xname, tg * 4, 4, xT, xTb, xst, ident, identb, PS[0], PS[1])
        if GM:
            for t in range(4):
                pp, pb = PS[4 + t % 2]
                vg, vgb = vgs[t]
                for k in range(8):
                    ph.add("pe", lambda e, pp=pp, t=t, k=k: e.matmul(out=pp[:, :], lhsT=xT[:, k, t * 128:(t + 1) * 128],
                                                                     rhs=Wv[:, k, :], start=(k == 0), stop=(k == 7)),
                           reads=[Wvb, xTb], writes=[pb])
                ph.add("act", lambda e, pp=pp, vg=vg: e.activation(out=vg[:], in_=pp[:, :], func=AF.Gelu_apprx_tanh),
                       reads=[pb], writes=[vgb])
            for c in range(4):
                pp, pb = PS[2 + c % 2]
                for k in range(8):
                    ph.add("pe", lambda e, pp=pp, c=c, k=k: e.matmul(out=pp[:, :], lhsT=Wu[:, k, c * 128:(c + 1) * 128],
                                                                     rhs=xT[:, k, :], start=(k == 0), stop=(k == 7)),
                           reads=[Wub, xTb], writes=[pb])
                ph.add("act", lambda e, pp=pp, c=c: e.activation(out=uT[:, c, :], in_=pp[:, :], func=AF.Gelu_apprx_tanh),
                       reads=[pb], writes=[uTb])
        if MLA:
            for c in range(3):
                pp, pb = PS[2 + c % 2]
                for k in range(8):
                    ph.add("pe", lambda e, pp=pp, c=c, k=k: e.matmul(out=pp[:, :], lhsT=Wcq[:, k, c * 128:(c + 1) * 128],
                                                                     rhs=xT[:, k, :], start=(k == 0), stop=(k == 7)),
                           reads=[Wcqb, xTb], writes=[pb])
                ph.add("act", lambda e, pp=pp, c=c: e.copy(out=cqT[:, c, :], in_=pp[:, :]), reads=[pb], writes=[cqTb])
                ph.add("act", lambda e, pp=pp, c=c: e.activation(out=sqq[:, c, :], in_=pp[:, :], func=AF.Square), reads=[pb], writes=[sqqb])
            for c in range(2):
                pp, pb = PS[4 + c % 2]
                for k in range(8):
                    ph.add("pe", lambda e, pp=pp, c=c, k=k: e.matmul(out=pp[:, :], lhsT=Wckv[:, k, c * 128:(c + 1) * 128],
                                                                     rhs=xT[:, k, :], start=(k == 0), stop=(k == 7)),
                           reads=[Wckvb, xTb], writes=[pb])
                ph.add("act", lambda e, pp=pp, c=c: e.copy(out=ckvT[:, c, :], in_=pp[:, :]), reads=[pb], writes=[ckvTb])
                ph.add("act", lambda e, pp=pp, c=c: e.activation(out=sqkv[:, c, :], in_=pp[:, :], func=AF.Square), reads=[pb], writes=[sqkvb])
        if GM:
            for t in range(4):
                vg, vgb = vgs[t]
                st4, st4b, mv4, mv4b, rs4, rs4b = vscr[t]
                for g in range(4):
                    ph.add("dve", lambda e, g=g, vg=vg, st4=st4: e.bn_stats(out=st4[:, g, :], in_=vg[:, g * 128:(g + 1) * 128]),
                           reads=[vgb], writes=[st4b])
                for g in range(4):
                    ph.add("dve", lambda e, g=g, st4=st4, mv4=mv4: e.bn_aggr(out=mv4[:, g, :], in_=st4[:, g, :]), reads=[st4b], writes=[mv4b])
                ph.add("act", lambda e, mv4=mv4, rs4=rs4: e.activation(out=rs4[:], in_=mv4[:, :, 1], func=AF.Sqrt, bias=EPS, scale=1.0),
                       reads=[mv4b], writes=[rs4b])
            for t in range(4):
                vg, vgb = vgs[t]
                vln, vlnb = vlns[t]
                st4, st4b, mv4, mv4b, rs4, rs4b = vscr[t]
                ph.add("dve", lambda e, rs4=rs4: e.reciprocal(out=rs4[:], in_=rs4[:]), reads=[rs4b], writes=[rs4b])
                for g in range(4):
                    ph.add("dve", lambda e, g=g, vg=vg, mv4=mv4, rs4=rs4: e.tensor_scalar(
                        out=vg[:, g * 128:(g + 1) * 128], in0=vg[:, g * 128:(g + 1) * 128],
                        scalar1=mv4[:, g, 0:1], scalar2=rs4[:, g:g + 1], op0=ALU.subtract, op1=ALU.mult),
                        reads=[vgb, mv4b, rs4b], writes=[vgb])
                ph.add("dve", lambda e, vg=vg: e.tensor_tensor(out=vg[:], in0=vg[:], in1=lng[:], op=ALU.mult), reads=[vgb, lngb], writes=[vgb])
                ph.add("dve", lambda e, vg=vg, vln=vln: e.tensor_tensor(out=vln[:], in0=vg[:], in1=lnb[:], op=ALU.add), reads=[vgb, lnbb], writes=[vlnb])
        if MLA:
            pp, pb = PS[6]
            for c in range(3):
                ph.add("pe", lambda e, pp=pp, c=c: e.matmul(out=pp[:, :], lhsT=onesq[:], rhs=sqq[:, c, :], start=(c == 0), stop=(c == 2)),
                       reads=[onesqb, sqqb], writes=[pb])
            ph.add("act", lambda e, pp=pp: e.activation(out=rq[:], in_=pp[:, :], func=AF.Sqrt, bias=EPS, scale=1.0), reads=[pb], writes=[rqb])
            ph.add("dve", lambda e: e.reciprocal(out=rq[:], in_=rq[:]), reads=[rqb], writes=[rqb])
            pp, pb = PS[7]
            for c in range(2):
                ph.add("pe", lambda e, pp=pp, c=c: e.matmul(out=pp[:, :], lhsT=oneskv[:], rhs=sqkv[:, c, :], start=(c == 0), stop=(c == 1)),
                       reads=[oneskvb, sqkvb], writes=[pb])
            ph.add("act", lambda e, pp=pp: e.activation(out=rkv[:], in_=pp[:, :], func=AF.Sqrt, bias=EPS, scale=1.0), reads=[pb], writes=[rkvb])
            ph.add("dve", lambda e: e.reciprocal(out=rkv[:], in_=rkv[:]), reads=[rkvb], writes=[rkvb])
            pp, pb = PS[6]
            for t in range(4):
                for c in range(2):
                    ph.add("pe", lambda e, pp=pp, c=c, t=t: e.matmul(out=pp[:, t:t + 1], lhsT=sqkv[:, c, t * 128:(t + 1) * 128],
                                                                     rhs=oneskv[:, 0:1], start=(c == 0), stop=(c == 1)),
                           reads=[oneskvb, sqkvb], writes=[pb])
            ph.add("act", lambda e, pp=pp: e.activation(out=rkvt[:], in_=pp[:, 0:4], func=AF.Sqrt, bias=EPS, scale=1.0), reads=[pb], writes=[rkvtb])
            ph.add("dve", lambda e: e.reciprocal(out=rkvt[:], in_=rkvt[:]), reads=[rkvtb], writes=[rkvtb])
            cs = slice(tg * 512, (tg + 1) * 512)
            ph.add("dve", lambda e, cs=cs: e.tensor_tensor(out=Cq[64:96, :], in0=rq[64:96, :], in1=cosT[64:96, cs], op=ALU.mult),
                   reads=[rqb, cosTb], writes=[Cqb])
            ph.add("dve", lambda e, cs=cs: e.tensor_tensor(out=Sq[64:96, :], in0=rq[64:96, :], in1=sinT[64:96, cs], op=ALU.mult),
                   reads=[rqb, sinTb], writes=[Sqb])
            for h in range(8):
                pa, pab = PS[2 + (h % 2) * 2]
                pr, prb = PS[3 + (h % 2) * 2]
                for k in range(3):
                    ph.add("pe", lambda e, pa=pa, h=h, k=k: e.matmul(out=pa[0:96, :], lhsT=Wuq[:, k, h * 96:(h + 1) * 96],
                                                                     rhs=cqT[:, k, :], start=(k == 0), stop=(k == 2)),
                           reads=[Wuqb, cqTb], writes=[pab])
                for k in range(3):
                    ph.add("pe", lambda e, pr=pr, h=h, k=k: e.matmul(out=pr[0:96, :], lhsT=Wuqr[:, k, h * 96:(h + 1) * 96],
                                                                     rhs=cqT[:, k, :], start=(k == 0), stop=(k == 2)),
                           reads=[Wuqrb, cqTb], writes=[prb])
                ph.add("dve", lambda e, pa=pa, h=h: e.tensor_tensor(out=QT[0:64, h, :], in0=pa[0:64, :], in1=rq[0:64, :], op=ALU.mult),
                       reads=[pab, rqb], writes=[QTb])
                ph.add("dve", lambda e, pa=pa: e.tensor_tensor(out=t1[64:96, :], in0=pa[64:96, :], in1=Cq[64:96, :], op=ALU.mult),
                       reads=[pab, Cqb], writes=[t1b])
                ph.add("dve", lambda e, pr=pr: e.tensor_tensor(out=t2[64:96, :], in0=pr[64:96, :], in1=Sq[64:96, :], op=ALU.mult),
                       reads=[prb, Sqb], writes=[t2b])
                ph.add("dve", lambda e, h=h: e.tensor_tensor(out=QT[64:96, h, :], in0=t1[64:96, :], in1=t2[64:96, :], op=ALU.add),
                       reads=[t1b, t2b], writes=[QTb])
            if QKT:
                ph.add("sp", lambda e, tg=tg: e.dma_start(out=qst[tg].rearrange("h p t -> p h t"), in_=QT[:]),
                       reads=[QTb], writes=[C.db("qst%d" % L, tg)], dma=True)
                ph.add("pool", lambda e, tg=tg: e.collective_compute(
                    "AllGather", ALU.bypass, replica_groups=GROUPS, ins=[qst[tg].rearrange("h p t -> h (p t)")],
                    outs=[C.d("qallt%d_%d" % (tg, L)).rearrange("r h p t -> (r h) (p t)")]),
                    reads=[C.db("qst%d" % L, tg)], writes=[C.db("qallt%d_%d" % (tg, L))], dma=True, cc=True)
            else:
                ph.add("sp", lambda e, cs=cs: e.dma_start(out=qs[:, :, cs], in_=QT[:]), reads=[QTb], writes=[C.db("qs%d" % L, tg)], dma=True)
            pa, pab = PS[2]
            pr, prb = PS[3]
            for k in range(8):
                ph.add("pe", lambda e, pa=pa, k=k: e.matmul(out=pa[0:96, :], lhsT=Wkr[:, k, 0:96], rhs=xT[:, k, :], start=(k == 0), stop=(k == 7)),
                       reads=[Wkrb, xTb], writes=[pab])
            for k in range(8):
                ph.add("pe", lambda e, pr=pr, k=k: e.matmul(out=pr[0:96, :], lhsT=Wkr[:, k, 96:192], rhs=xT[:, k, :], start=(k == 0), stop=(k == 7)),
                       reads=[Wkrb, xTb], writes=[prb])
            ph.add("dve", lambda e, pa=pa, cs=cs: e.tensor_tensor(out=t1[64:96, :], in0=pa[64:96, :], in1=cosT[64:96, cs], op=ALU.mult),
                   reads=[pab, cosTb], writes=[t1b])
            ph.add("dve", lambda e, pr=pr, cs=cs: e.tensor_tensor(out=t2[64:96, :], in0=pr[64:96, :], in1=sinT[64:96, cs], op=ALU.mult),
                   reads=[prb, sinTb], writes=[t2b])
            ph.add("dve", lambda e: e.tensor_tensor(out=krp[64:96, :], in0=t1[64:96, :], in1=t2[64:96, :], op=ALU.add),
                   reads=[t1b, t2b], writes=[krpb])
            for h in range(8):
                pp, pb = PS[4 + h % 2]
                for k in range(2):
                    ph.add("pe", lambda e, pp=pp, h=h, k=k: e.matmul(out=pp[0:64, :], lhsT=Wuk[:, k, h * 64:(h + 1) * 64],
                                                                     rhs=ckvT[:, k, :], start=(k == 0), stop=(k == 1)),
                           reads=[Wukb, ckvTb], writes=[pb])
                ph.add("dve", lambda e, pp=pp, h=h: e.tensor_tensor(out=KT[0:64, h, :], in0=pp[0:64, :], in1=rkv[0:64, :], op=ALU.mult),
                       reads=[pb, rkvb], writes=[KTb])
                ph.add("act", lambda e, h=h: e.copy(out=KT[64:96, h, :], in_=krp[64:96, :]), reads=[krpb], writes=[KTb])
            if QKT:
                ph.add("sp", lambda e, tg=tg: e.dma_start(out=kst[tg].rearrange("h p t -> p h t"), in_=KT[:]),
                       reads=[KTb], writes=[C.db("kst%d" % L, tg)], dma=True)
                ph.add("pool", lambda e, tg=tg: e.collective_compute(
                    "AllGather", ALU.bypass, replica_groups=GROUPS, ins=[kst[tg].rearrange("h p t -> h (p t)")],
                    outs=[C.d("kallt%d_%d" % (tg, L)).rearrange("r h p t -> (r h) (p t)")]),
                    reads=[C.db("kst%d" % L, tg)], writes=[C.db("kallt%d_%d" % (tg, L))], dma=True, cc=True)
            else:
                ph.add("sp", lambda e, cs=cs: e.dma_start(out=ks[:, :, cs], in_=KT[:]), reads=[KTb], writes=[C.db("ks%d" % L, tg)], dma=True)
            for t in range(4):
                pp, pb = PS[6 + t % 2]
                for k in range(2):
                    ph.add("pe", lambda e, pp=pp, t=t, k=k: e.matmul(out=pp[:, :], lhsT=ckvT[:, k, t * 128:(t + 1) * 128],
                                                                     rhs=Wuv[:, k, :], start=(k == 0), stop=(k == 1)),
                           reads=[Wuvb, ckvTb], writes=[pb])
                ph.add("dve", lambda e, pp=pp, t=t: e.tensor_scalar(out=Vt[:, t, :], in0=pp[:, :], scalar1=rkvt[:, t:t + 1], scalar2=None, op0=ALU.mult),
                       reads=[pb, rkvtb], writes=[Vtb])
            for dst in range(4):
                ph.add("sp", lambda e, dst=dst, tg=tg: e.dma_start(
                    out=vs[dst, tg * 512:(tg + 1) * 512, :].rearrange("(t p) c -> p t c", p=128),
                    in_=Vt[:, :, dst * 128:(dst + 1) * 128]),
                    reads=[Vtb], writes=[C.db("vs%d" % L, tg * 4 + dst)], dma=True)
        if GM:
            for t in range(4):
                vln, vlnb = vlns[t]
                mp, mpb = PS[6 + t % 2]
                for g in range(4):
                    ph.add("pe", lambda e, mp=mp, g=g, vln=vln: e.matmul(out=mp[:, g * 128:(g + 1) * 128], lhsT=vln[:, g * 128:(g + 1) * 128],
                                                                rhs=wsT[:, g, :], start=True, stop=False),
                           reads=[vlnb, wsTb], writes=[mpb])
                    ph.add("pe", lambda e, mp=mp, g=g: e.matmul(out=mp[:, g * 128:(g + 1) * 128], lhsT=ones1[0:1, :],
                                                                rhs=bs[0:1, g * 128:(g + 1) * 128], start=False, stop=True),
                           reads=[ones1b, bsb], writes=[mpb])
                ph.add("dve", lambda e, mp=mp, t=t: e.tensor_tensor(out=yaT[:, :, t * 128:(t + 1) * 128],
                                                                    in0=mp[:].rearrange("p (g t) -> p g t", g=4),
                                                                    in1=uT[:, :, t * 128:(t + 1) * 128], op=ALU.mult),
                       reads=[mpb, uTb], writes=[yaTb])
            ph.add("sp", lambda e, tg=tg: e.dma_start(out=yad[:, :, tg * 512:(tg + 1) * 512], in_=yaT[:]),
                   reads=[yaTb], writes=[C.db("yaT%d" % L, tg)], dma=True)
    ph.finish()


def phase_E2A(C, L):
    ph = Phase(C, "E2A_%d" % L)
    sb = ph.sb
    PS = C.ps
    S = 8192
    if DIRECT and OVERLAP_QK:
        qrn, krn, vrn = [("%s%d" % (n, L)) for n in ("vall", "vall", "vall")]
    elif DIRECT:
        qrn, krn, vrn = [("%s%d" % (n, L)) for n in ("qall", "kall", "vall")]
    else:
        qrn, krn, vrn = [("%s%d" % (n, L)) for n in ("qr", "kr", "vr")]
    qr = C.d(qrn)
    kr = C.d(krn)
    vr = C.d(vrn)
    ybs = C.d("ybs%d" % L)
    tri, trib = sb("tri", [128, 128], BF16)
    trf, trfb = sb("trf", [128, 128], F32)
    ph.add("pool", lambda e: e.memset(trf[:], 1.0), writes=[trfb])
    ph.add("pool", lambda e: e.affine_select(out=trf[:], in_=trf[:], pattern=[[1, 128]], compare_op=ALU.is_ge,
                                            fill=0.0, base=0, channel_multiplier=-1), reads=[trfb], writes=[trfb])
    ph.add("pool", lambda e: e.tensor_copy(out=tri[:], in_=trf[:]), reads=[trfb], writes=[trib])
    sel, selb = sb("sel", [65, 64], F32)
    ph.add("pool", lambda e: e.memset(sel[:], 0.0), writes=[selb])
    ph.add("pool", lambda e: e.memset(sel[64:65, :], 1.0), reads=[selb], writes=[selb])
    QTh, KTh, Vh, yb = [], [], [], []
    Vrb = []
    HQ = 96 * TOK
    for hh in range(2):
        q, qb = sb("Q%d" % hh, [96, 4, 2048], BF16)
        k, kb = sb("K%d" % hh, [96, 4, 2048], BF16)
        v, vb = sb("V%d" % hh, [128, 64, 65], BF16)
        y, yb_ = sb("Y%d" % hh, [64, S], BF16)
        QTh.append((q, qb)); KTh.append((k, kb)); Vh.append((v, vb)); yb.append((y, yb_))
        vrb = [Buf() for _ in range(4)]
        Vrb.append(vrb)
        ph.add("pool", lambda e, v=v: e.memset(v[:, :, 64:65], 1.0), writes=[vb])
        if DIRECT:
            def ldq(e, dst=q, hh=hh, src=qr):
                rank = my_rank(ph, e)
                return e.dma_start(out=dst[:], in_=bass.AP(src.tensor, rank * (8 * HQ) + hh * HQ, [[TOK, 96], [2 * HQ, 4], [1, TOK]]))

            def ldk(e, dst=k, hh=hh, src=kr):
                rank = my_rank(ph, e)
                return e.dma_start(out=dst[:], in_=bass.AP(src.tensor, rank * (8 * HQ) + hh * HQ, [[TOK, 96], [2 * HQ, 4], [1, TOK]]))

            def ldv(e, dst=v, hh=hh, src=vr):
                rank = my_rank(ph, e)
                return e.dma_start(out=dst[:, :, 0:64], in_=bass.AP(src.tensor, rank * (4 * TOK * 128) + hh * 64,
                                                                    [[128, 128], [128 * 128, 64], [1, 64]]))
            if OVERLAP_QK:
                QB = 96 * 512
                for (dst_, dstb_, pre_) in ((q, qb, "qallt"), (k, kb, "kallt")):
                    for tg in range(4):
                        nm_ = "%s%d_%d" % (pre_, tg, L)

                        def ldt(e, dst_=dst_, nm_=nm_, tg=tg, hh=hh):
                            rank = my_rank(ph, e)
                            return e.dma_start(out=dst_[:, :, tg * 512:(tg + 1) * 512],
                                               in_=bass.AP(C.d(nm_).tensor, rank * (2 * QB) + hh * QB,
                                                           [[512, 96], [8 * QB, 4], [1, 512]]))
                        ph.add("sp", ldt, reads=[C.db(nm_)], writes=[dstb_], dma=True)
            else:
                ph.add("sp", ldq, reads=[C.db(qrn)], writes=[qb], dma=True)
                ph.add("sp", ldk, reads=[C.db(krn)], writes=[kb], dma=True)
            ph.add("sp", ldv, reads=[C.db(vrn), vb], writes=vrb, dma=True)
        else:
            ph.add("sp", lambda e, q=q, hh=hh: e.dma_start(out=q[:], in_=qr[:, hh, :, :].rearrange("r p t -> p r t")),
                   reads=[C.db(qrn)], writes=[qb], dma=True)
            ph.add("sp", lambda e, k=k, hh=hh: e.dma_start(out=k[:], in_=kr[:, hh, :, :].rearrange("r p t -> p r t")),
                   reads=[C.db(krn)], writes=[kb], dma=True)
            for r in range(4):
                ph.add("sp", lambda e, v=v, hh=hh, r=r: e.dma_start(
                    out=v[:, r * 16:(r + 1) * 16, 0:64],
                    in_=vr[r, :, hh * 64:(hh + 1) * 64].rearrange("(t p) c -> p t c", p=128)),
                    reads=[C.db(vrn), vb], writes=[vrb[r]], dma=True)
    NSB = 4
    LA = NSB - 1
    PT = [sb("PT%d" % i, [128, 512], BF16) for i in range(NSB)]
    acs = [sb("acs%d" % i, [65, 512], F32) for i in range(2)]
    rden = [sb("rden%d" % i, [64, 512], F32) for i in range(2)]
    steps = []
    for hh in range(2):
        for G in range(16):
            for kt in range(4 * G + 4):
                steps.append((hh, G, kt))
    n = len(steps)

    def geo(i):
        hh, G, kt = steps[i]
        j = kt - 4 * G
        c0 = 0 if j < 0 else j * 128
        return hh, G, kt, j, c0, 512 - c0

    for i in range(n + LA):
        if i < n:
            hh, G, kt, j, c0, ncol = geo(i)
            q, qb = QTh[hh]
            k, kb = KTh[hh]
            qf = q[:].rearrange("p r t -> p (r t)")
            kf = k[:].rearrange("p r t -> p (r t)")
            sp_, spb = PS[i % NSB]
            pt, ptb = PT[i % NSB]
            ph.add("pe", lambda e, sp_=sp_, kt=kt, G=G, c0=c0, ncol=ncol, kf=kf, qf=qf: e.matmul(
                out=sp_[:, 0:ncol], lhsT=kf[0:96, kt * 128:(kt + 1) * 128],
                rhs=qf[0:96, G * 512 + c0:(G + 1) * 512], start=True, stop=True),
                reads=[kb, qb], writes=[spb])
            ph.add("act", lambda e, sp_=sp_, pt=pt, ncol=ncol: e.activation(out=pt[:, 0:ncol], in_=sp_[:, 0:ncol], func=AF.Exp),
                   reads=[spb], writes=[ptb])
            if j >= 0:
                ph.add("dve" if OVERLAP_XE2 else "pool", lambda e, pt=pt: e.tensor_tensor(out=pt[:, 0:128], in0=pt[:, 0:128], in1=tri[:], op=ALU.mult),
                       reads=[ptb, trib], writes=[ptb])
        m = i - LA
        if m >= 0:
            hh, G, kt, j, c0, ncol = geo(m)
            ng = hh * 16 + G
            nkt = 4 * G + 4
            v, vb = Vh[hh]
            y, ybf = yb[hh]
            acc, accb = PS[4 + ng % 2]
            pt, ptb = PT[m % NSB]
            ph.add("pe", lambda e, acc=acc, pt=pt, kt=kt, c0=c0, ncol=ncol, nkt=nkt, v=v: e.matmul(
                out=acc[0:65, c0:512], lhsT=v[:, kt, :], rhs=pt[:, 0:ncol], start=(kt == 0), stop=(kt == nkt - 1)),
                reads=[vb, ptb, Vrb[hh][kt // 16]], writes=[accb])
            if kt == nkt - 1:
                a, ab = acs[ng % 2]
                rd, rdb = rden[ng % 2]
                dn, dnb = PS[6 + ng % 2]
                ph.add("dve", lambda e, a=a, acc=acc: e.tensor_copy(out=a[:], in_=acc[0:65, :]), reads=[accb], writes=[ab])
                ph.add("pe", lambda e, dn=dn, a=a: e.matmul(out=dn[0:64, :], lhsT=sel[:], rhs=a[:], start=True, stop=True),
                       reads=[selb, ab], writes=[dnb])
                ph.add("dve", lambda e, rd=rd, dn=dn: e.reciprocal(out=rd[:], in_=dn[0:64, :]), reads=[dnb], writes=[rdb])
                ph.add("dve", lambda e, rd=rd, a=a, G=G, y=y: e.tensor_tensor(out=y[:, G * 512:(G + 1) * 512], in0=a[0:64, :], in1=rd[:], op=ALU.mult),
                       reads=[ab, rdb], writes=[ybf])
                if G % 4 == 3:
                    r = G // 4
                    ph.add("sp", lambda e, y=y, hh=hh, r=r: e.dma_start(out=ybs[r, hh, :, :], in_=y[:, r * 2048:(r + 1) * 2048]),
                           reads=[ybf], writes=[C.db("ybs%d" % L, hh * 4 + r)], dma=True)
                    if OVERLAP_XE2 and hh == 1:
                        src_ = ybs[r].rearrange("h p t -> (h p) t")
                        dst_ = C.d("yball%d" % L)[r].rearrange("r p t -> (r p) t")
                        ph.add("pool", lambda e, src_=src_, dst_=dst_: e.collective_compute(
                            "AllGather", ALU.bypass, replica_groups=GROUPS, ins=[src_], outs=[dst_]),
                            reads=[C.db("ybs%d" % L, r), C.db("ybs%d" % L, 4 + r)], writes=[C.db("yball%d" % L, r)], dma=True, cc=True)
    ph.finish()


def ln_stats(ph, src, srcb, dst, dstb, scr):
    st2, st2b, mv, mvb, rs, rsb = scr
    for c in range(2):
        ph.add("dve", lambda e, c=c: e.bn_stats(out=st2[:, c, :], in_=src[:, c * 512:(c + 1) * 512]), reads=[srcb], writes=[st2b])
    ph.add("dve", lambda e: e.bn_aggr(out=mv[:], in_=st2[:]), reads=[st2b], writes=[mvb])
    ph.add("act", lambda e: e.activation(out=rs[:, 0:1], in_=mv[:, 1:2], func=AF.Sqrt, bias=EPS, scale=1.0), reads=[mvb], writes=[rsb])
    ph.add("dve", lambda e: e.reciprocal(out=rs[:, 0:1], in_=rs[:, 0:1]), reads=[rsb], writes=[rsb])
    ph.add("dve", lambda e: e.scalar_tensor_tensor(out=rs[:, 1:2], in0=mv[:, 0:1], scalar=-1.0, in1=rs[:, 0:1], op0=ALU.mult, op1=ALU.mult),
           reads=[mvb, rsb], writes=[rsb])
    ph.add("act", lambda e: e.activation(out=dst, in_=src[:], func=AF.Identity, bias=rs[:, 1:2], scale=rs[:, 0:1]),
           reads=[srcb, rsb], writes=[dstb])


def ln_affine(ph, dst, dstb, gt, gtb, bt, btb):
    ph.add("dve", lambda e: e.tensor_tensor(out=dst, in0=dst, in1=gt[:], op=ALU.mult), reads=[dstb, gtb], writes=[dstb])
    ph.add("dve", lambda e: e.tensor_tensor(out=dst, in0=dst, in1=bt[:], op=ALU.add), reads=[dstb, btb], writes=[dstb])


def phase_POST(C, L, ysrc):
    ph = Phase(C, "PO_%d" % L)
    sb = ph.sb
    PS = C.ps
    xin = C.d("x%d" % L)
    xout = C.d("x%d" % (L + 1))
    ident, identb = sb("ident", [128, 128], F32)
    make_ident(ph, ident, identb)
    Wo, Wob = sb("Wo", [128, 8, 1024], BF16)
    Wob2 = [Buf(), Buf()]
    for n in range(2):
        load_w_bf16(ph, C, "wout%d" % L, Wo[:, :, n * 512:(n + 1) * 512], Wob2[n], 1024, n * 512, (n + 1) * 512)
    lnp = {}
    for nm in ("ln1g", "ln1b", "ln2g", "ln2b"):
        t, b = sb(nm, [128, D], F32)
        lnp[nm] = (t, b)

    def load_lnp(names):
        for nm in names:
            t, b = lnp[nm]
            ph.add("sp", lambda e, t=t, nm=nm: e.dma_start(out=t[:], in_=C.d("%s%d" % (nm, L)).partition_broadcast(128)), writes=[b], dma=True)
    NSG = 2
    TSG = TOK // NSG
    NTS = TSG // 128
    FG = 1024
    NFG = DFF // FG
    R = [sb("R%d" % i, [128, D], F32) for i in range(NTS)]
    xT1, xT1b = sb("xT1", [128, 8, TSG], BF16)
    import os as _os
    if _os.environ.get("PADYT"):
        sb("padyt", [128, 8, 512], BF16)
    yT = [sb("yT%d" % i, [128, 8, 512], BF16) for i in range(2)]
    W1 = [sb("W1_%d" % i, [128, 8, FG], BF16) for i in range(2)]
    W2 = [sb("W2_%d" % i, [128, FG // 128, D], BF16) for i in range(2)]
    hT = [sb("hT%d" % i, [128, FG // 128, 512], BF16) for i in range(2)]
    hr = [sb("hr%d" % i, [128, 512], F32) for i in range(2)]
    xs = [sb("xs%d" % i, [128, D], F32) for i in range(2)]
    scr = []
    for i in range(4):
        a_, ab_ = sb("st2_%d" % i, [128, 2, 6], F32)
        b_, bb_ = sb("mv_%d" % i, [128, 2], F32)
        c_, cb_ = sb("rs_%d" % i, [128, 2], F32)
        scr.append((a_, ab_, b_, bb_, c_, cb_))
    xo = [sb("xo%d" % i, [128, D], F32) for i in range(2)]
    w1d = C.d("wff1%d" % L).rearrange("(k p) n -> p k n", p=128)
    w2d = C.d("wff2%d" % L).rearrange("(k p) n -> p k n", p=128)
    nw = 0
    nh = 0

    def emit_A_load(sg, ti):
        tg, t = ti // 4, ti % 4
        gt = sg * NTS + ti
        if t == 0:
            yt, ytb = yT[tg % 2]
            c0 = 0
            tok0 = sg * TSG + tg * 512
            for (nm, nch) in ysrc:
                if nm.startswith("yball"):
                    def ldy(e, yt=yt, c0=c0, nch=nch, tok0=tok0, src=C.d(nm)):
                        rank = my_rank(ph, e)
                        return e.dma_start(out=yt[:, c0:c0 + nch, :],
                                           in_=bass.AP(src.tensor, rank * (4 * 128 * TOK) + tok0,
                                                       [[TOK, 128], [128 * TOK, 4], [1, 512]]))
                    ph.add("sp", ldy, reads=[C.db(nm)], writes=[ytb], dma=True)
                else:
                    src = C.d(nm).rearrange("c p t -> p c t")[:, :, tok0:tok0 + 512]
                    ph.add("sp", lambda e, yt=yt, c0=c0, nch=nch, src=src: e.dma_start(out=yt[:, c0:c0 + nch, :], in_=src),
                           reads=[C.db(nm)], writes=[ytb], dma=True)
                c0 += nch
        xx, xxb = xs[ti % 2]
        ph.add("sp", lambda e, xx=xx, gt=gt: e.dma_start(out=xx[:], in_=xin[gt * 128:(gt + 1) * 128, :]),
               reads=[C.db("x%d" % L, gt)], writes=[xxb], dma=True)

    def emit_A_mm(sg, ti):
        tg, t = ti // 4, ti % 4
        yt, ytb = yT[tg % 2]
        xx, xxb = xs[ti % 2]
        for n in range(2):
            pp, pb = PS[n + 2 * (t % 2)]
            for k in range(8):
                ph.add("pe", lambda e, pp=pp, yt=yt, k=k, n=n, t=t: e.matmul(
                    out=pp[:, :], lhsT=yt[:, k, t * 128:(t + 1) * 128], rhs=Wo[:, k, n * 512:(n + 1) * 512],
                    start=(k == 0), stop=(k == 7)), reads=[ytb, Wob2[n]], writes=[pb])
            ph.add("dve", lambda e, pp=pp, xx=xx, n=n: e.scalar_tensor_tensor(
                out=xx[:, n * 512:(n + 1) * 512], in0=xx[:, n * 512:(n + 1) * 512], scalar=ALPHA, in1=pp[:, :],
                op0=ALU.mult, op1=ALU.add), reads=[pb, xxb], writes=[xxb])

    def emit_A_stats(sg, ti):
        r, rb = R[ti]
        xx, xxb = xs[ti % 2]
        ln_stats(ph, xx, xxb, r[:], rb, scr[ti % 2])

    def emit_A_aff(sg, ti):
        r, rb = R[ti]
        ln_affine(ph, r[:], rb, lnp["ln1g"][0], lnp["ln1g"][1], lnp["ln1b"][0], lnp["ln1b"][1])
        for half in range(2):
            pp, pb = PS[4 + half]
            for j in range(4):
                k = half * 4 + j
                ph.add("pe", lambda e, pp=pp, r=r, j=j, k=k: e.transpose(
                    out=pp[:, j * 128:(j + 1) * 128], in_=r[:, k * 128:(k + 1) * 128], identity=ident[:]),
                    reads=[rb, identb], writes=[pb])
            ph.add("act", lambda e, pp=pp, half=half, ti=ti: e.copy(
                out=xT1[:, half * 4:(half + 1) * 4, ti * 128:(ti + 1) * 128],
                in_=pp[:].rearrange("p (k t) -> p k t", k=4)), reads=[pb], writes=[xT1b])

    def emit_C_stats(sg, ti):
        r, rb = R[ti]
        o, ob = xo[ti % 2]
        ln_stats(ph, r, rb, o[:], ob, scr[2 + ti % 2])

    def emit_C_aff(sg, ti):
        gt = sg * NTS + ti
        o, ob = xo[ti % 2]
        ln_affine(ph, o[:], ob, lnp["ln2g"][0], lnp["ln2g"][1], lnp["ln2b"][0], lnp["ln2b"][1])
        ph.add("sp", lambda e, o=o, gt=gt: e.dma_start(out=xout[gt * 128:(gt + 1) * 128, :], in_=o[:]),
               reads=[ob], writes=[C.db("x%d" % (L + 1), gt)], dma=True)

    for sg in range(NSG):
        if sg > 0:
            emit_C_stats(sg - 1, 0)
        emit_A_load(sg, 0)
        if sg == 0:
            load_lnp(("ln1g", "ln1b", "ln2g", "ln2b"))
        emit_A_mm(sg, 0)
        emit_A_stats(sg, 0)
        for ti in range(NTS):
            if sg > 0:
                emit_C_aff(sg - 1, ti)
                if ti + 1 < NTS:
                    emit_C_stats(sg - 1, ti + 1)
            if ti + 1 < NTS:
                emit_A_load(sg, ti + 1)
                emit_A_mm(sg, ti + 1)
                emit_A_stats(sg, ti + 1)
            emit_A_aff(sg, ti)
        msteps = [(fg, tg) for fg in range(NFG) for tg in range(TSG // 512)]
        wcur = {}

        def emit_mm1(si):
            nonlocal nw
            fg, tg = msteps[si]
            if tg == 0:
                w1, w1b = W1[nw % 2]
                w2, w2b = W2[nw % 2]
                nw += 1
                wcur[fg] = (w1, w1b, w2, w2b)
                ph.add("pool", lambda e, w1=w1, fg=fg: e.dma_start(out=w1[:], in_=w1d[:, :, fg * FG:(fg + 1) * FG]), writes=[w1b], dma=True)
                ph.add("pool", lambda e, w2=w2, fg=fg: e.dma_start(out=w2[:], in_=w2d[:, fg * (FG // 128):(fg + 1) * (FG // 128), :]), writes=[w2b], dma=True)
            w1, w1b, w2, w2b = wcur[fg]
            h, hb = hT[si % 2]
            for f in range(FG // 128):
                pp, pb = PS[f % 2]
                hrr, hrb = hr[f % 2]
                for k in range(8):
                    ph.add("pe", lambda e, pp=pp, w1=w1, f=f, k=k, tg=tg: e.matmul(
                        out=pp[:, :], lhsT=w1[:, k, f * 128:(f + 1) * 128], rhs=xT1[:, k, tg * 512:(tg + 1) * 512],
                        start=(k == 0), stop=(k == 7)), reads=[w1b, xT1b], writes=[pb])
                ph.add("act", lambda e, pp=pp, hrr=hrr: e.activation(out=hrr[:], in_=pp[:, :], func=AF.Relu), reads=[pb], writes=[hrb])
                ph.add("act", lambda e, hrr=hrr, h=h, f=f: e.activation(out=h[:, f, :], in_=hrr[:], func=AF.Square),
                       reads=[hrb], writes=[hb])

        def emit_mm2(si):
            fg, tg = msteps[si]
            w1, w1b, w2, w2b = wcur[fg]
            h, hb = hT[si % 2]
            for t in range(4):
                ti = tg * 4 + t
                r, rb = R[ti]
                for n in range(2):
                    pp, pb = PS[2 + n + 2 * (t % 2)]
                    for f in range(FG // 128):
                        ph.add("pe", lambda e, pp=pp, h=h, w2=w2, f=f, n=n, t=t: e.matmul(
                            out=pp[:, :], lhsT=h[:, f, t * 128:(t + 1) * 128], rhs=w2[:, f, n * 512:(n + 1) * 512],
                            start=(f == 0), stop=(f == FG // 128 - 1)), reads=[hb, w2b], writes=[pb])
                    if fg == 0:
                        ph.add("dve", lambda e, pp=pp, r=r, n=n: e.scalar_tensor_tensor(
                            out=r[:, n * 512:(n + 1) * 512], in0=r[:, n * 512:(n + 1) * 512], scalar=ALPHA, in1=pp[:, :],
                            op0=ALU.mult, op1=ALU.add), reads=[pb, rb], writes=[rb])
                    else:
                        ph.add("dve", lambda e, pp=pp, r=r, n=n: e.tensor_tensor(
                            out=r[:, n * 512:(n + 1) * 512], in0=r[:, n * 512:(n + 1) * 512], in1=pp[:, :], op=ALU.add),
                            reads=[pb, rb], writes=[rb])

        for si in range(len(msteps) + 1):
            if si < len(msteps):
                emit_mm1(si)
            if si >= 1:
                emit_mm2(si - 1)
    emit_C_stats(NSG - 1, 0)
    for ti in range(NTS):
        if ti + 1 < NTS:
            emit_C_stats(NSG - 1, ti + 1)
        emit_C_aff(NSG - 1, ti)
    ph.finish()


def phase_O1(C, L):
    ph = Phase(C, "O1_%d" % L)
    sb = ph.sb
    PS = C.ps
    xname = "x%d" % L
    ident, identb = sb("ident", [128, 128], F32)
    make_ident(ph, ident, identb)
    wn = "win%d" % L
    Wq, Wqb = sb("Wq", [128, 8, 512], BF16)
    Wk, Wkb = sb("Wk", [128, 8, 512], BF16)
    Wv, Wvb = sb("Wv", [128, 8, 1024], BF16)
    Wg, Wgb = sb("Wg", [128, 8, 1024], BF16)
    Wz, Wzb = sb("Wz", [128, 8, 16], BF16)
    wga, wgab = sb("wga", [16, 512], BF16)
    bga, bgab = sb("bga", [1, 512], F32)
    ones1, ones1b = sb("ones1", [1, 128], F32)
    ph.add("sp", lambda e: e.dma_start(out=bga[:], in_=C.d("bgate%d" % L)), writes=[bgab], dma=True)
    ph.add("pool", lambda e: e.memset(ones1[:], 1.0), writes=[ones1b])
    triU, triUb = sb("triU", [128, 128], F32)
    triL, triLb = sb("triL", [128, 128], F32)
    ph.add("pool", lambda e: e.memset(triU[:], -1.0 / 16.0), writes=[triUb])
    ph.add("pool", lambda e: e.affine_select(out=triU[:], in_=triU[:], pattern=[[1, 128]], compare_op=ALU.is_ge,
                                            fill=0.0, base=0, channel_multiplier=-1), reads=[triUb], writes=[triUb])
    ph.add("pool", lambda e: e.memset(triL[:], -1.0 / 16.0), writes=[triLb])
    ph.add("pool", lambda e: e.affine_select(out=triL[:], in_=triL[:], pattern=[[-1, 128]], compare_op=ALU.is_gt,
                                            fill=0.0, base=0, channel_multiplier=1), reads=[triLb], writes=[triLb])
    load_w_bf16(ph, C, wn, Wq[:], Wqb, 1024, 0, 512)
    load_w_bf16(ph, C, wn, Wk[:], Wkb, 1024, 512, 1024)
    load_w_bf16(ph, C, wn, Wv[:], Wvb, 1024, 1024, 2048)
    load_w_bf16(ph, C, wn, Wg[:], Wgb, 1024, 2048, 3072)
    load_w_bf16(ph, C, wn, Wz[:], Wzb, 1024, 3072, 3088)
    load_w_bf16(ph, C, "wgate%d" % L, wga[:], wgab, 16, 0, 512)
    xst = [sb("xst%d" % i, [128, D], F32) for i in range(2)]
    xT, xTb = sb("xT", [128, 8, 512], BF16)
    zgT, zgTb = sb("zgT", [16, 512], BF16)
    ex, exb = sb("ex", [128, 512], F32)
    Gp, Gpb = sb("Gp", [128, 512], F32)
    eb4, eb4b = sb("eb4", [128, 4, 4, 128], F32)
    enb4, enb4b = sb("enb4", [128, 4, 4, 128], F32)
    erev, erevb = sb("erev", [128, 512], F32)
    ebl, eblb = sb("ebl", [128, 4, 16], F32)
    dt, dtb = sb("dt", [128, 4], F32)
    qeT, qeTb = sb("qeT", [128, 4, 512], BF16)
    keT, keTb = sb("keT", [128, 4, 512], BF16)
    kdg, kdgb = sb("kdg", [128, 4, 512], BF16)
    vvg, vvgb = sb("vvg", [128, 4, 1024], BF16)
    sgg, sggb = sb("sgg", [128, 4, 1024], BF16)
    Sl, Slb = sb("Sl", [128, 1028], F32)
    ph.add("dve", lambda e: e.memset(Sl[:], 0.0), writes=[Slb])
    ph.add("dve", lambda e: e.memset(dt[:], 1.0), writes=[dtb])
    qed = C.d("qeT%d" % L).rearrange("h p t -> p h t")
    ked = C.d("keT%d" % L).rearrange("h p t -> p h t")
    kdd = C.d("kd%d" % L).rearrange("(t p) n -> p t n", p=128)
    vvd = C.d("vv%d" % L).rearrange("(t p) n -> p t n", p=128)
    sgd = C.d("sg%d" % L).rearrange("(t p) n -> p t n", p=128)
    qsc = 128.0 ** -0.5
    for tg in range(4):
        load_xT(ph, C, xname, tg * 4, 4, xT, xTb, xst, ident, identb, PS[0], PS[1])
        cs = slice(tg * 512, (tg + 1) * 512)
        pp, pb = PS[2]
        for k in range(8):
            ph.add("pe", lambda e, pp=pp, k=k: e.matmul(out=pp[0:16, :], lhsT=Wz[:, k, :], rhs=xT[:, k, :], start=(k == 0), stop=(k == 7)),
                   reads=[Wzb, xTb], writes=[pb])
        ph.add("act", lambda e, pp=pp: e.copy(out=zgT[:], in_=pp[0:16, :]), reads=[pb], writes=[zgTb])
        for t in range(4):
            ch = tg * 4 + t
            pp, pb = PS[3]
            ph.add("pe", lambda e, pp=pp, t=t: e.matmul(out=pp[:, :], lhsT=zgT[0:16, t * 128:(t + 1) * 128], rhs=wga[0:16, :], start=True, stop=False),
                   reads=[zgTb, wgab], writes=[pb])
            ph.add("pe", lambda e, pp=pp: e.matmul(out=pp[:, :], lhsT=ones1[0:1, :], rhs=bga[0:1, :], start=False, stop=True),
                   reads=[ones1b, bgab], writes=[pb])
            ph.add("act", lambda e, pp=pp: e.activation(out=ex[:], in_=pp[:, :], func=AF.Exp, scale=-1.0), reads=[pb], writes=[exb])
            ph.add("act", lambda e: e.activation(out=Gp[:], in_=ex[:], func=AF.Ln, bias=1.0, scale=1.0), reads=[exb], writes=[Gpb])
            pc, pcb = PS[4]
            for h in range(4):
                ph.add("pe", lambda e, pc=pc, h=h: e.matmul(out=pc[:, h * 128:(h + 1) * 128], lhsT=Gp[:, h * 128:(h + 1) * 128], rhs=triU[:],
                                                            start=True, stop=True), reads=[Gpb, triUb], writes=[pcb])
            ph.add("act", lambda e, pc=pc, t=t: e.activation(out=eb4[:, t, :, :], in_=pc[:].rearrange("p (h t) -> p h t", h=4), func=AF.Exp),
                   reads=[pcb], writes=[eb4b])
            ph.add("act", lambda e, pc=pc, t=t: e.activation(out=enb4[:, t, :, :], in_=pc[:].rearrange("p (h t) -> p h t", h=4), func=AF.Exp, scale=-1.0),
                   reads=[pcb], writes=[enb4b])
            ph.add("dve", lambda e, t=t, ch=ch: e.tensor_copy(out=ebl[:, :, ch], in_=eb4[:, t, :, 127]), reads=[eb4b], writes=[eblb])
            ph.add("dve", lambda e, ch=ch: e.tensor_tensor(out=dt[:], in0=dt[:], in1=ebl[:, :, ch], op=ALU.mult), reads=[eblb, dtb], writes=[dtb])
            pr, prb = PS[5]
            ph.add("pe", lambda e, pr=pr: e.matmul(out=pr[:, :], lhsT=triL[:], rhs=Gp[:], start=True, stop=True), reads=[Gpb, triLb], writes=[prb])
            ph.add("act", lambda e, pr=pr: e.activation(out=erev[:], in_=pr[:, :], func=AF.Exp), reads=[prb], writes=[erevb])
            pk, pkb = PS[6]
            for k in range(8):
                ph.add("pe", lambda e, pk=pk, k=k, t=t: e.matmul(out=pk[:, :], lhsT=xT[:, k, t * 128:(t + 1) * 128], rhs=Wk[:, k, :],
                                                                 start=(k == 0), stop=(k == 7)), reads=[xTb, Wkb], writes=[pkb])
            ph.add("dve", lambda e, pk=pk, t=t: e.tensor_tensor(out=kdg[:, t, :], in0=pk[:, :], in1=erev[:], op=ALU.mult),
                   reads=[pkb, erevb], writes=[kdgb])
            for n in range(2):
                pv, pvb = PS[2 + n * 5]
                for k in range(8):
                    ph.add("pe", lambda e, pv=pv, k=k, t=t, n=n: e.matmul(out=pv[:, :], lhsT=xT[:, k, t * 128:(t + 1) * 128],
                                                                          rhs=Wv[:, k, n * 512:(n + 1) * 512], start=(k == 0), stop=(k == 7)),
                           reads=[xTb, Wvb], writes=[pvb])
                ph.add("act" if n == 0 else "dve",
                       (lambda e, pv=pv, t=t, n=n: e.copy(out=vvg[:, t, n * 512:(n + 1) * 512], in_=pv[:, :])) if n == 0 else
                       (lambda e, pv=pv, t=t, n=n: e.tensor_copy(out=vvg[:, t, n * 512:(n + 1) * 512], in_=pv[:, :])),
                       reads=[pvb], writes=[vvgb])
            for n in range(2):
                pv, pvb = PS[2 + n * 5]
                for k in range(8):
                    ph.add("pe", lambda e, pv=pv, k=k, t=t, n=n: e.matmul(out=pv[:, :], lhsT=xT[:, k, t * 128:(t + 1) * 128],
                                                                          rhs=Wg[:, k, n * 512:(n + 1) * 512], start=(k == 0), stop=(k == 7)),
                           reads=[xTb, Wgb], writes=[pvb])
                ph.add("act", lambda e, pv=pv, t=t, n=n: e.activation(out=sgg[:, t, n * 512:(n + 1) * 512], in_=pv[:, :], func=AF.Silu),
                       reads=[pvb], writes=[sggb])
            for h in range(4):
                pss, pssb = PS[3 + (h % 2)]
                ph.add("pe", lambda e, pss=pss, h=h, t=t: e.matmul(out=pss[:, 0:256], lhsT=kdg[:, t, h * 128:(h + 1) * 128],
                                                                   rhs=vvg[:, t, h * 256:(h + 1) * 256], start=True, stop=True),
                       reads=[kdgb, vvgb], writes=[pssb])
                ph.add("dve", lambda e, pss=pss, h=h, ch=ch: e.scalar_tensor_tensor(
                    out=Sl[:, h * 256:(h + 1) * 256], in0=Sl[:, h * 256:(h + 1) * 256], scalar=ebl[:, h, ch:ch + 1],
                    in1=pss[:, 0:256], op0=ALU.mult, op1=ALU.add), reads=[pssb, eblb, Slb], writes=[Slb])
        for h in range(4):
            pq, pqb = PS[5 + (h % 2)]
            for k in range(8):
                ph.add("pe", lambda e, pq=pq, h=h, k=k: e.matmul(out=pq[:, :], lhsT=Wq[:, k, h * 128:(h + 1) * 128], rhs=xT[:, k, :],
                                                                 start=(k == 0), stop=(k == 7)), reads=[Wqb, xTb], writes=[pqb])
            ph.add("dve", lambda e, pq=pq, h=h: e.scalar_tensor_tensor(
                out=qeT[:, h, :].rearrange("p (c t) -> p c t", c=4), in0=pq[:].rearrange("p (c t) -> p c t", c=4), scalar=qsc,
                in1=eb4[:, :, h, :], op0=ALU.mult, op1=ALU.mult), reads=[pqb, eb4b], writes=[qeTb])
            pq, pqb = PS[2 + 5 * (h % 2)]
            for k in range(8):
                ph.add("pe", lambda e, pq=pq, h=h, k=k: e.matmul(out=pq[:, :], lhsT=Wk[:, k, h * 128:(h + 1) * 128], rhs=xT[:, k, :],
                                                                 start=(k == 0), stop=(k == 7)), reads=[Wkb, xTb], writes=[pqb])
            ph.add("dve", lambda e, pq=pq, h=h: e.tensor_tensor(
                out=keT[:, h, :].rearrange("p (c t) -> p c t", c=4), in0=pq[:].rearrange("p (c t) -> p c t", c=4),
                in1=enb4[:, :, h, :], op=ALU.mult), reads=[pqb, enb4b], writes=[keTb])
        ph.add("sp", lambda e, cs=cs: e.dma_start(out=qed[:, :, cs], in_=qeT[:]), reads=[qeTb], writes=[C.db("qeT%d" % L, tg)], dma=True)
        ph.add("sp", lambda e, cs=cs: e.dma_start(out=ked[:, :, cs], in_=keT[:]), reads=[keTb], writes=[C.db("keT%d" % L, tg)], dma=True)
        ph.add("sp", lambda e, tg=tg: e.dma_start(out=kdd[:, tg * 4:(tg + 1) * 4, :], in_=kdg[:]), reads=[kdgb], writes=[C.db("kd%d" % L, tg)], dma=True)
        ph.add("sp", lambda e, tg=tg: e.dma_start(out=vvd[:, tg * 4:(tg + 1) * 4, :], in_=vvg[:]), reads=[vvgb], writes=[C.db("vv%d" % L, tg)], dma=True)
        ph.add("sp", lambda e, tg=tg: e.dma_start(out=sgd[:, tg * 4:(tg + 1) * 4, :], in_=sgg[:]), reads=[sggb], writes=[C.db("sg%d" % L, tg)], dma=True)
    ph.add("dve", lambda e: e.tensor_copy(out=Sl[:, 1024:1028], in_=dt[:]), reads=[dtb, Slb], writes=[Slb])
    ph.add("sp", lambda e: e.dma_start(out=C.d("sloc%d" % L), in_=Sl[:]), reads=[Slb], writes=[C.db("sloc%d" % L)], dma=True)
    ph.add("sp", lambda e: e.dma_start(out=C.d("ebl%d" % L), in_=ebl[:].rearrange("p h c -> p (h c)")), reads=[eblb], writes=[C.db("ebl%d" % L)], dma=True)
    ph.finish()


def phase_O2(C, L):
    ph = Phase(C, "O2_%d" % L)
    sb = ph.sb
    PS = C.ps
    ident, identb = sb("ident", [128, 128], F32)
    make_ident(ph, ident, identb)
    tri4, tri4b = sb("tri4", [128, 4, 128], F32)
    ph.add("pool", lambda e: e.memset(tri4[:], 1.0), writes=[tri4b])
    for h in range(4):
        ph.add("pool", lambda e, h=h: e.affine_select(out=tri4[:, h, :], in_=tri4[:, h, :], pattern=[[1, 128]], compare_op=ALU.is_ge,
                                                     fill=0.0, base=0, channel_multiplier=-1), reads=[tri4b], writes=[tri4b])
    SA, SAb = sb("SA", [128, 4, 1028], F32)
    oh, ohb = sb("oh", [128, 4], F32)
    ebl, eblb = sb("ebl", [128, 4, 16], F32)
    cg, cgb = sb("cg", [128, 4, 256], F32)
    cb, cbb = sb("cb", [128, 4, 256], F32)
    ph.add("sp", lambda e: e.dma_start(out=SA[:], in_=C.d("sall%d" % L).rearrange("r p n -> p r n")), reads=[C.db("sall%d" % L)], writes=[SAb], dma=True)
    ph.add("sp", lambda e: e.dma_start(out=oh[:], in_=C.d("oh")), writes=[ohb], dma=True)
    ph.add("sp", lambda e: e.dma_start(out=ebl[:].rearrange("p h c -> p (h c)"), in_=C.d("ebl%d" % L)), reads=[C.db("ebl%d" % L)], writes=[eblb], dma=True)
    for h in range(4):
        ph.add("sp", lambda e, h=h: e.dma_start(out=cg[:, h, :], in_=C.d("clng%d" % L).partition_broadcast(128)), writes=[cgb], dma=True)
        ph.add("sp", lambda e, h=h: e.dma_start(out=cb[:, h, :], in_=C.d("clnb%d" % L).partition_broadcast(128)), writes=[cbb], dma=True)
    S, Sb_ = sb("S", [128, 1024], F32)
    Sbf, Sbfb = sb("Sbf", [128, 1024], BF16)
    cur, curb = sb("cur", [128, 1024], F32)
    ph.add("dve", lambda e: e.tensor_copy(out=cur[:], in_=SA[:, 0, 0:1024]), reads=[SAb], writes=[curb])
    ph.add("dve", lambda e: e.tensor_scalar(out=S[:], in0=cur[:], scalar1=oh[:, 1:2], scalar2=None, op0=ALU.mult), reads=[curb, ohb], writes=[Sb_])
    for r in (1, 2):
        for h in range(4):
            ph.add("dve", lambda e, r=r, h=h: e.scalar_tensor_tensor(
                out=cur[:, h * 256:(h + 1) * 256], in0=cur[:, h * 256:(h + 1) * 256], scalar=SA[:, r, 1024 + h:1025 + h],
                in1=SA[:, r, h * 256:(h + 1) * 256], op0=ALU.mult, op1=ALU.add), reads=[curb, SAb], writes=[curb])
        ph.add("dve", lambda e, r=r: e.scalar_tensor_tensor(out=S[:], in0=cur[:], scalar=oh[:, r + 1:r + 2], in1=S[:], op0=ALU.mult, op1=ALU.add),
               reads=[curb, ohb, Sb_], writes=[Sb_])
    ph.add("act", lambda e: e.copy(out=Sbf[:], in_=S[:]), reads=[Sb_], writes=[Sbfb])
    NB = 4
    qe = [sb("qe%d" % i, [128, 4, 128], BF16) for i in range(NB)]
    ke = [sb("ke%d" % i, [128, 4, 128], BF16) for i in range(NB)]
    kd = [sb("kd%d" % i, [128, 512], BF16) for i in range(NB)]
    vv = [sb("vv%d" % i, [128, 1024], BF16) for i in range(NB)]
    sg = [sb("sg%d" % i, [128, 1024], BF16) for i in range(NB)]
    am = [sb("am%d" % i, [128, 4, 128], BF16) for i in range(NB)]
    ob = [sb("ob%d" % i, [128, 1024], F32) for i in range(NB)]
    sgf = [sb("sgf%d" % i, [128, 1024], F32) for i in range(NB)]
    yo = [sb("yo%d" % i, [128, 8, 128], BF16) for i in range(NB)]
    st4, st4b = sb("st4", [128, 4, 6], F32)
    mv4, mv4b = sb("mv4", [128, 4, 2], F32)
    rs4, rs4b = sb("rs4", [128, 4], F32)
    qed = C.d("qeT%d" % L).rearrange("h p t -> p h t")
    ked = C.d("keT%d" % L).rearrange("h p t -> p h t")
    kdd = C.d("kd%d" % L)
    vvd = C.d("vv%d" % L)
    sgd = C.d("sg%d" % L)
    yod = C.d("yoT%d" % L).rearrange("c p t -> p c t")
    def o2_scan(ch):
            i = ch % NB
            cs = slice(ch * 128, (ch + 1) * 128)
            q_, qb = qe[i]; k_, kb = ke[i]; d_, db_ = kd[i]; v_, vb = vv[i]; s_, sb_ = sg[i]
            a_, ab = am[i]; o_, obb = ob[i]; sf, sfb = sgf[i]; y_, yb_ = yo[i]
            tgk = ch // 4
            ph.add("sp", lambda e, q_=q_, cs=cs: e.dma_start(out=q_[:], in_=qed[:, :, cs]), reads=[C.db("qeT%d" % L, tgk)], writes=[qb], dma=True)
            ph.add("sp", lambda e, k_=k_, cs=cs: e.dma_start(out=k_[:], in_=ked[:, :, cs]), reads=[C.db("keT%d" % L, tgk)], writes=[kb], dma=True)
            ph.add("sp", lambda e, d_=d_, cs=cs: e.dma_start(out=d_[:], in_=kdd[cs, :]), reads=[C.db("kd%d" % L, tgk)], writes=[db_], dma=True)
            ph.add("sp", lambda e, v_=v_, cs=cs: e.dma_start(out=v_[:], in_=vvd[cs, :]), reads=[C.db("vv%d" % L, tgk)], writes=[vb], dma=True)
            ph.add("sp", lambda e, s_=s_, cs=cs: e.dma_start(out=s_[:], in_=sgd[cs, :]), reads=[C.db("sg%d" % L, tgk)], writes=[sb_], dma=True)
            pa, pab = PS[0 + ch % 2]
            for h in range(4):
                ph.add("pe", lambda e, pa=pa, h=h, k_=k_, q_=q_: e.matmul(out=pa[:, h * 128:(h + 1) * 128], lhsT=k_[:, h, :], rhs=q_[:, h, :], start=True, stop=True),
                       reads=[kb, qb], writes=[pab])
            ph.add("dve", lambda e, pa=pa, a_=a_: e.tensor_tensor(out=a_[:], in0=pa[:].rearrange("p (h t) -> p h t", h=4), in1=tri4[:], op=ALU.mult),
                   reads=[pab, tri4b], writes=[ab])
            po = [PS[2 + 2 * (ch % 2)], PS[3 + 2 * (ch % 2)]]
            for h in range(4):
                pp, pb = po[h // 2]
                oc = (h % 2) * 256
                ph.add("pe", lambda e, pp=pp, h=h, oc=oc, q_=q_: e.matmul(out=pp[:, oc:oc + 256], lhsT=q_[:, h, :], rhs=Sbf[:, h * 256:(h + 1) * 256], start=True, stop=False),
                       reads=[qb, Sbfb], writes=[pb])
                ph.add("pe", lambda e, pp=pp, h=h, oc=oc, a_=a_, v_=v_: e.matmul(out=pp[:, oc:oc + 256], lhsT=a_[:, h, :], rhs=v_[:, h * 256:(h + 1) * 256], start=False, stop=True),
                       reads=[ab, vb], writes=[pb])
            for h in range(4):
                pss, pssb = PS[6 + (h % 2)]
                ph.add("pe", lambda e, pss=pss, h=h, d_=d_, v_=v_: e.matmul(out=pss[:, 0:256], lhsT=d_[:, h * 128:(h + 1) * 128], rhs=v_[:, h * 256:(h + 1) * 256], start=True, stop=True),
                       reads=[db_, vb], writes=[pssb])
                ph.add("dve", lambda e, pss=pss, h=h, ch=ch: e.scalar_tensor_tensor(
                    out=S[:, h * 256:(h + 1) * 256], in0=S[:, h * 256:(h + 1) * 256], scalar=ebl[:, h, ch:ch + 1], in1=pss[:, 0:256],
                    op0=ALU.mult, op1=ALU.add), reads=[pssb, eblb, Sb_], writes=[Sb_])
            ph.add("act", lambda e: e.copy(out=Sbf[:], in_=S[:]), reads=[Sb_], writes=[Sbfb])

    def o2_epi_a(ch):
            i = ch % NB
            cs = slice(ch * 128, (ch + 1) * 128)
            q_, qb = qe[i]; k_, kb = ke[i]; d_, db_ = kd[i]; v_, vb = vv[i]; s_, sb_ = sg[i]
            a_, ab = am[i]; o_, obb = ob[i]; sf, sfb = sgf[i]; y_, yb_ = yo[i]
            tgk = ch // 4
            po = [PS[2 + 2 * (ch % 2)], PS[3 + 2 * (ch % 2)]]
            for n in range(2):
                pp, pb = po[n]
                ph.add("act", lambda e, pp=pp, n=n, o_=o_: e.copy(out=o_[:, n * 512:(n + 1) * 512], in_=pp[:, :]), reads=[pb], writes=[obb])
            for h in range(4):
                ph.add("dve", lambda e, h=h, o_=o_: e.bn_stats(out=st4[:, h, :], in_=o_[:, h * 256:(h + 1) * 256]), reads=[obb], writes=[st4b])
            for h in range(4):
                ph.add("dve", lambda e, h=h: e.bn_aggr(out=mv4[:, h, :], in_=st4[:, h, :]), reads=[st4b], writes=[mv4b])
            ph.add("act", lambda e: e.activation(out=rs4[:], in_=mv4[:, :, 1], func=AF.Sqrt, bias=EPS, scale=1.0), reads=[mv4b], writes=[rs4b])
            ph.add("dve", lambda e: e.reciprocal(out=rs4[:], in_=rs4[:]), reads=[rs4b], writes=[rs4b])
            for h in range(4):
                ph.add("dve", lambda e, h=h, o_=o_: e.tensor_scalar(out=o_[:, h * 256:(h + 1) * 256], in0=o_[:, h * 256:(h + 1) * 256],
                                                                 scalar1=mv4[:, h, 0:1], scalar2=rs4[:, h:h + 1], op0=ALU.subtract, op1=ALU.mult),
                       reads=[obb, mv4b, rs4b], writes=[obb])
            ph.add("dve", lambda e, o_=o_: e.tensor_tensor(out=o_[:], in0=o_[:], in1=cg[:].rearrange("p h d -> p (h d)"), op=ALU.mult), reads=[obb, cgb], writes=[obb])
            ph.add("dve", lambda e, o_=o_: e.tensor_tensor(out=o_[:], in0=o_[:], in1=cb[:].rearrange("p h d -> p (h d)"), op=ALU.add), reads=[obb, cbb], writes=[obb])
            ph.add("act", lambda e, sf=sf, s_=s_: e.copy(out=sf[:], in_=s_[:]), reads=[sb_], writes=[sfb])
            ph.add("dve", lambda e, o_=o_, sf=sf: e.tensor_tensor(out=o_[:], in0=o_[:], in1=sf[:], op=ALU.mult), reads=[obb, sfb], writes=[obb])

    def o2_epi_b(ch):
            i = ch % NB
            cs = slice(ch * 128, (ch + 1) * 128)
            q_, qb = qe[i]; k_, kb = ke[i]; d_, db_ = kd[i]; v_, vb = vv[i]; s_, sb_ = sg[i]
            a_, ab = am[i]; o_, obb = ob[i]; sf, sfb = sgf[i]; y_, yb_ = yo[i]
            tgk = ch // 4
            po = [PS[2 + 2 * (ch % 2)], PS[3 + 2 * (ch % 2)]]
            for half in range(2):
                pp, pb = PS[6 + half]
                for j in range(4):
                    k = half * 4 + j
                    ph.add("pe", lambda e, pp=pp, j=j, k=k, o_=o_: e.transpose(out=pp[:, j * 128:(j + 1) * 128], in_=o_[:, k * 128:(k + 1) * 128], identity=ident[:]),
                           reads=[obb, identb], writes=[pb])
                ph.add("act", lambda e, pp=pp, half=half, y_=y_: e.copy(out=y_[:, half * 4:(half + 1) * 4, :], in_=pp[:].rearrange("p (k t) -> p k t", k=4)),
                       reads=[pb], writes=[yb_])
            ph.add("sp", lambda e, y_=y_, cs=cs: e.dma_start(out=yod[:, :, cs], in_=y_[:]), reads=[yb_], writes=[C.db("yoT%d" % L, ch)], dma=True)

    for ch in range(18):
        if ch < 16:
            o2_scan(ch)
        if 1 <= ch <= 16:
            o2_epi_a(ch - 1)
        if ch >= 2:
            o2_epi_b(ch - 2)
    ph.finish()


GROUPS = [[0, 1, 2, 3], [4, 5, 6, 7]]


def phase_X(C, gathers, selects=()):
    ph = Phase(C, "X%d" % C.nx)
    C.nx += 1
    for (s, src, d, dst) in gathers:
        ph.add("pool", lambda e, src=src, dst=dst: e.collective_compute(
            "AllGather", ALU.bypass, replica_groups=GROUPS, ins=[src], outs=[dst]),
            reads=[C.db(s)], writes=[C.db(d)], dma=True, cc=True)
    for (gn, dn, n) in selects:
        def sel(e, gn=gn, dn=dn, n=n):
            rank = my_rank(ph, e)
            q = n // 16
            return e.dma_start(out=bass.AP(C.d(dn).tensor, 0, [[q, 16], [1, q]]),
                               in_=bass.AP(C.d(gn).tensor, rank * n, [[q, 16], [1, q]]))
        ph.add("sp", sel, reads=[C.db(g[2]) for g in gathers], writes=[C.db(dn)], dma=True)
    ph.finish()


def flat2(ap):
    nd = len(ap.shape)
    if nd == 2:
        return ap
    names = " ".join("a%d" % i for i in range(nd))
    rest = " ".join("a%d" % i for i in range(1, nd))
    return ap.rearrange("%s -> a0 (%s)" % (names, rest))


def bf(*s):
    return (tuple(s), BF16)


def f32(*s):
    return (tuple(s), F32)


def dram_shapes():
    sh = {"dbgr0": f32(128, 1024), "dbgr4": f32(128, 1024), "dbgyt": bf(128, 8, 512), "pos": ((1, TOK), I32), "invf": f32(128, 2), "oh": f32(128, 4)}
    for L in range(5):
        sh["x%d" % L] = f32(TOK, D)
    for L in range(4):
        for nm in ("ln1g", "ln1b", "ln2g", "ln2b"):
            sh["%s%d" % (nm, L)] = f32(1, D)
        sh["wff1%d" % L] = f32(D, DFF)
        sh["wff2%d" % L] = f32(DFF, D)
        sh["wout%d" % L] = f32(D, D)
        if L % 2 == 0:
            sh["win%d" % L] = f32(D, 1696)
            sh["wkr%d" % L] = f32(D, 192)
            sh["wuq%d" % L] = f32(384, 768)
            sh["wuqr%d" % L] = f32(384, 768)
            sh["wuk%d" % L] = f32(256, 512)
            sh["wuv%d" % L] = f32(256, 512)
            sh["gq%d" % L] = f32(128, 3)
            sh["gkv%d" % L] = f32(128, 2)
            sh["wsT%d" % L] = f32(128, 4, 128)
            sh["bs%d" % L] = f32(1, 512)
            sh["alng%d" % L] = f32(1, 512)
            sh["alnb%d" % L] = f32(1, 512)
            sh["qs%d" % L] = bf(8, 96, TOK)
            sh["ks%d" % L] = bf(8, 96, TOK)
            sh["vs%d" % L] = bf(4, TOK, 128)
            sh["qr%d" % L] = bf(4, 2, 96, TOK)
            sh["kr%d" % L] = bf(4, 2, 96, TOK)
            sh["vr%d" % L] = bf(4, TOK, 128)
            sh["yaT%d" % L] = bf(4, 128, TOK)
            sh["ybs%d" % L] = bf(4, 2, 64, TOK)
            sh["ybr%d" % L] = bf(4, 128, TOK)
            sh["qall%d" % L] = bf(4, 4, 2, 96, TOK)
            sh["kall%d" % L] = bf(4, 4, 2, 96, TOK)
            sh["vall%d" % L] = bf(4, 4, TOK, 128)
            sh["yball%d" % L] = bf(4, 4, 128, TOK)
            sh["qst%d" % L] = bf(4, 8, 96, 512)
            sh["kst%d" % L] = bf(4, 8, 96, 512)
            for tg in range(4):
                sh["qallt%d_%d" % (tg, L)] = bf(4, 8, 96, 512)
                sh["kallt%d_%d" % (tg, L)] = bf(4, 8, 96, 512)
        else:
            sh["win%d" % L] = f32(D, 3088)
            sh["wgate%d" % L] = f32(16, 512)
            sh["bgate%d" % L] = f32(1, 512)
            sh["clng%d" % L] = f32(1, 256)
            sh["clnb%d" % L] = f32(1, 256)
            sh["qeT%d" % L] = bf(4, 128, TOK)
            sh["keT%d" % L] = bf(4, 128, TOK)
            sh["kd%d" % L] = bf(TOK, 512)
            sh["vv%d" % L] = bf(TOK, 1024)
            sh["sg%d" % L] = bf(TOK, 1024)
            sh["ebl%d" % L] = f32(128, 64)
            sh["sloc%d" % L] = f32(128, 1028)
            sh["sall%d" % L] = f32(4, 128, 1028)
            sh["yoT%d" % L] = bf(8, 128, TOK)
    return sh


QKT_NAMES = ["%sallt%d_" % (n, tg) for n in ("q", "k") for tg in range(4)]


def phase_io(kind, L):
    s = lambda *names: ["%s%d" % (n, L) for n in names]
    ffn = s("wout", "ln1g", "ln1b", "ln2g", "ln2b", "wff1", "wff2")
    if kind == "E1":
        return (["x%d" % L, "pos", "invf"] + s("win", "wkr", "wuq", "wuqr", "wuk", "wuv", "gq", "gkv", "wsT", "bs", "alng", "alnb"),
                s("qs", "ks", "vs", "yaT"))
    if kind == "E1M":
        return (["x%d" % L, "pos", "invf"] + s("win", "wkr", "wuq", "wuqr", "wuk", "wuv", "gq", "gkv"),
                s("qst", "kst", "vs", *QKT_NAMES) if OVERLAP_QK else s("qs", "ks", "vs"))
    if kind == "E1G":
        if OVERLAP_QK:
            return (["x%d" % L] + s("win", "wsT", "bs", "alng", "alnb", "vs"), s("yaT", "vall"))
        return (["x%d" % L] + s("win", "wsT", "bs", "alng", "alnb", "qs", "ks", "vs"), s("yaT", "qall", "kall", "vall"))
    if kind == "XE1":
        return (s("qs", "ks", "vs"), s("qall", "kall", "vall") if DIRECT else s("qr", "kr", "vr"))
    if kind == "E2A":
        return ((s("vall", *QKT_NAMES) if OVERLAP_QK else s("qall", "kall", "vall")) if DIRECT else s("qr", "kr", "vr"),
                s("ybs") + (s("yball") if OVERLAP_XE2 else []))
    if kind == "XE2":
        return (s("ybs"), s("yball") if DIRECT else s("ybr"))
    if kind == "POSTE":
        return (["x%d" % L] + s("yaT", "yball" if DIRECT else "ybr") + ffn, ["x%d" % (L + 1)])
    if kind == "O1":
        return (["x%d" % L] + s("win", "wgate", "bgate"), s("qeT", "keT", "kd", "vv", "sg", "ebl", "sloc"))
    if kind == "XO":
        return (s("sloc"), s("sall"))
    if kind == "O2":
        return (["oh"] + s("qeT", "keT", "kd", "vv", "sg", "ebl", "sall", "clng", "clnb"), s("yoT"))
    if kind == "POSTO":
        return (["x%d" % L] + s("yoT") + ffn, ["x%d" % (L + 1)])
    raise ValueError(kind)


def all_phases():
    ph = []
    for L in range(4):
        if L % 2 == 0:
            if OVERLAP_XE1:
                ph += [("E1M", L), ("E1G", L), ("E2A", L)] + ([] if OVERLAP_XE2 else [("XE2", L)]) + [("POSTE", L)]
            else:
                ph += [("E1", L), ("XE1", L), ("E2A", L), ("XE2", L), ("POSTE", L)]
        else:
            ph += [("O1", L), ("XO", L), ("O2", L), ("POSTO", L)]
    return ph


def xe1_gathers(C, L):
    g = []
    for nm in (() if OVERLAP_QK else ("q", "k")):
        sn, an = "%ss%d" % (nm, L), "%sall%d" % (nm, L)
        for j in range(4):
            g.append((sn, C.d(sn)[2 * j:2 * j + 2].rearrange("h p t -> h (p t)"),
                      an, C.d(an)[j].rearrange("r h p t -> (r h) (p t)")))
    for j in range(4):
        g.append(("vs%d" % L, C.d("vs%d" % L)[j], "vall%d" % L, C.d("vall%d" % L)[j].rearrange("r t c -> (r t) c")))
    return g


def build_program(phases, later_reads):
    shapes = dram_shapes()
    written = set()
    ext_in, ext_out = set(), set()
    for (k, L) in phases:
        r, w = phase_io(k, L)
        for n in r:
            if n not in written:
                ext_in.add(n)
        written.update(w)
    for n in written:
        if n in later_reads or n == "x4":
            ext_out.add(n)
    import os as _os2
    if _os2.environ.get("DBGPOST"):
        ext_out.update(["dbgyt", "dbgr0", "dbgr4"])
    nc = bass.Bass("TRN2", target_bir_lowering=False)
    C = Ctx(nc, ext_in, ext_out, shapes)
    C.nx = 0
    for (k, L) in phases:
        if k == "E1":
            phase_E1(C, L)
        elif k == "E2A":
            phase_E2A(C, L)
        elif k == "POSTE":
            phase_POST(C, L, [("yaT%d" % L, 4), (("yball%d" if DIRECT else "ybr%d") % L, 4)])
        elif k == "POSTO":
            phase_POST(C, L, [("yoT%d" % L, 8)])
        elif k == "O1":
            phase_O1(C, L)
        elif k == "O2":
            phase_O2(C, L)
        elif k == "E1M":
            phase_E1(C, L, "mla")
        elif k == "E1G":
            phase_E1(C, L, "gmlp", xe1_gathers(C, L))
        elif k == "XE1":
            g = xe1_gathers(C, L)
            phase_X(C, g, [] if DIRECT else [("qall%d" % L, "qr%d" % L, 8 * 96 * TOK), ("kall%d" % L, "kr%d" % L, 8 * 96 * TOK),
                                             ("vall%d" % L, "vr%d" % L, 4 * TOK * 128)])
        elif k == "XE2":
            g = []
            for j in range(4):
                g.append(("ybs%d" % L, C.d("ybs%d" % L)[j].rearrange("h p t -> (h p) t"),
                          "yball%d" % L, C.d("yball%d" % L)[j].rearrange("r p t -> (r p) t")))
            phase_X(C, g, [] if DIRECT else [("yball%d" % L, "ybr%d" % L, 4 * 128 * TOK)])
        elif k == "XO":
            phase_X(C, [("sloc%d" % L, C.d("sloc%d" % L), "sall%d" % L, C.d("sall%d" % L).rearrange("r p n -> (r p) n"))])
    C.st.close()
    return nc, sorted(ext_in), sorted(ext_out)


def prep_weights(inp):
    W = {}
    f = lambda a: np.ascontiguousarray(a, dtype=np.float32)
    invf = np.zeros((128, 2), np.float32)
    fr = (10000.0 ** (-np.arange(16, dtype=np.float32) / 16.0)).astype(np.float32)
    invf[64:80, 0] = fr
    invf[80:96, 0] = fr
    invf[:, 1] = 1.0
    invf[64:80, 1] = -1.0
    W["invf"] = invf
    perm = np.concatenate([np.arange(16, 32), np.arange(0, 16)])
    for L in range(4):
        j = L // 2
        for nm in ("ln1_g", "ln1_b", "ln2_g", "ln2_b"):
            W["%s%d" % (nm.replace("_", ""), L)] = f(inp[nm][L][None, :])
        W["wff1%d" % L] = f(inp["w_ff1"][L])
        W["wff2%d" % L] = f(inp["w_ff2"][L])
        if L % 2 == 0:
            win = inp["w_in_even"][j]
            W["win%d" % L] = f(win)
            wkr = np.zeros((D, 192), np.float32)
            wkr[:, 64:96] = win[:, 1664:1696]
            wkr[:, 96 + 64:192] = win[:, 1664:1696][:, perm]
            W["wkr%d" % L] = wkr
            wuq = inp["b_w_uq"][j]
            W["wuq%d" % L] = f(wuq)
            wuqr = np.zeros((384, 768), np.float32)
            for h in range(8):
                wuqr[:, h * 96 + 64:(h + 1) * 96] = wuq[:, h * 96 + 64:(h + 1) * 96][:, perm]
            W["wuqr%d" % L] = wuqr
            wukv = inp["b_w_ukv"][j].reshape(256, 8, 128)
            W["wuk%d" % L] = f(wukv[:, :, :64].reshape(256, 512))
            W["wuv%d" % L] = f(wukv[:, :, 64:].reshape(256, 512))
            W["gq%d" % L] = f(inp["b_q_norm"][j].reshape(3, 128).T)
            W["gkv%d" % L] = f(inp["b_kv_norm"][j].reshape(2, 128).T)
            W["wsT%d" % L] = f(np.transpose(inp["a_w_s"][j], (2, 0, 1)))
            W["bs%d" % L] = f(inp["a_b_s"][j].reshape(1, 512))
            W["alng%d" % L] = f(inp["a_ln_g"][j].reshape(1, 512))
            W["alnb%d" % L] = f(inp["a_ln_b"][j].reshape(1, 512))
            W["wout%d" % L] = f(inp["w_out_even"][j])
        else:
            W["win%d" % L] = f(inp["w_in_odd"][j])
            W["wgate%d" % L] = f(inp["c_w_gate"][j])
            W["bgate%d" % L] = f(inp["c_b_gate"][j][None, :])
            W["clng%d" % L] = f(inp["c_ln_g"][j][None, :])
            W["clnb%d" % L] = f(inp["c_ln_b"][j][None, :])
            W["wout%d" % L] = f(inp["w_out_odd"][j])
    return W


def host_exchange(kind, L, state):
    for g in GROUPS:
        if kind == "XE1":
            for i, ci in enumerate(g):
                state[ci]["qr%d" % L] = np.stack([state[cj]["qs%d" % L][2 * i:2 * i + 2] for cj in g])
                state[ci]["kr%d" % L] = np.stack([state[cj]["ks%d" % L][2 * i:2 * i + 2] for cj in g])
                state[ci]["vr%d" % L] = np.stack([state[cj]["vs%d" % L][i] for cj in g])
        elif kind == "XE2":
            for j, cj in enumerate(g):
                state[cj]["ybr%d" % L] = np.stack([state[ci]["ybs%d" % L][j].reshape(128, TOK) for ci in g])
        elif kind == "XO":
            sall = np.stack([state[c]["sloc%d" % L] for c in g])
            for c in g:
                state[c]["sall%d" % L] = sall


_PROG_CACHE = {}


def kernel(**inp):
    inp = {k: np.asarray(v) for k, v in inp.items()}
    W = prep_weights(inp)
    x = inp["x"].astype(np.float32)
    pos = inp["positions"].astype(np.int32)
    state = []
    for c in range(NCORES):
        b, q = c // 4, c % 4
        oh = np.zeros((128, 4), np.float32)
        oh[:, q] = 1.0
        state.append({"x0": np.ascontiguousarray(x[b, q * TOK:(q + 1) * TOK]),
                      "pos": np.ascontiguousarray(pos[b, q * TOK:(q + 1) * TOK][None, :]), "oh": oh})
    phases = all_phases()
    if FUSED:
        launches = [phases]
    else:
        launches, cur = [], []
        for p in phases:
            if p[0].startswith("X"):
                launches.append(cur)
                launches.append([p])
                cur = []
            else:
                cur.append(p)
        launches.append(cur)
    for li, lp in enumerate(launches):
        if not FUSED and lp[0][0].startswith("X"):
            host_exchange(lp[0][0], lp[0][1], state)
            continue
        later = set()
        for lq in launches[li + 1:]:
            for (k, L) in lq:
                later.update(phase_io(k, L)[0])
        key = tuple(lp)
        if key not in _PROG_CACHE:
            _PROG_CACHE[key] = build_program(lp, later)
        nc, ext_in, ext_out = _PROG_CACHE[key]
        in_maps = []
        for c in range(NCORES):
            m = {}
            for n in ext_in:
                m[n] = state[c][n] if n in state[c] else W[n]
            in_maps.append(m)
        res = run_bass_kernel_spmd(nc, in_maps, core_ids=list(range(NCORES)))
        for c in range(NCORES):
            for n in ext_out:
                state[c][n] = np.asarray(res.results[c][n])
    out = np.zeros((2, 8192, D), np.float32)
    for c in range(NCORES):
        b, q = c // 4, c % 4
        out[b, q * TOK:(q + 1) * TOK] = state[c]["x4"]
    return out
```
